# Optimizing a Trainium2 kernel written in Bass

```python
import jax, jax.numpy as jnp
from jax import lax
import numpy as np

D_MODEL = 2048
BATCH = 8
SEQ = 4096
DEPTH = 4
DEC_BATCH = 4
DEC_SEQ = 2048
PAST_LEN = 128

GRID_W = 64
ATT_WIDTH = D_MODEL // 2
RWKV_WIDTH = D_MODEL - ATT_WIDTH
ATT_HEAD_DIM = 128
ATT_HEADS = ATT_WIDTH // ATT_HEAD_DIM
RWKV_HEAD_DIM = 64
RWKV_HEADS = RWKV_WIDTH // RWKV_HEAD_DIM
NA_ROWS = 8
NA_COLS = 16
DECAY_LORA = 64
AAA_LORA = 64
MV_LORA = 32
GATE_LORA = 64
D_FF = 4 * D_MODEL
IN_WIDTH = 3 * ATT_WIDTH + 3 * RWKV_WIDTH
NORM_EPS = 1e-6
GN_EPS = 64e-5

kernel_name = "hymba_natten_rwkv7_bidir_encoder"


def rmsnorm(x, g):
    xf = x.astype(jnp.float32)
    y = xf * lax.rsqrt(jnp.mean(xf * xf, axis=-1, keepdims=True) + NORM_EPS)
    return (y * g.astype(jnp.float32)).astype(x.dtype)


def token_shift(u, mu):
    prev = jnp.pad(u[:, :-1], ((0, 0), (1, 0), (0, 0)))
    nxt = jnp.pad(u[:, 1:], ((0, 0), (0, 1), (0, 0)))
    return u + mu[0] * (prev - u) + mu[1] * (nxt - u)


def neighbourhood_attention(q, k, v, rpb):
    B, T, H, Dh = q.shape
    rows = T // GRID_W
    wr = min(NA_ROWS, rows)
    qg = q.reshape(B, rows, GRID_W, H, Dh)
    kg = k.reshape(B, rows, GRID_W, H, Dh)
    vg = v.reshape(B, rows, GRID_W, H, Dh)
    cols = jnp.arange(GRID_W)
    col_start = jnp.clip(cols - NA_COLS // 2, 0, GRID_W - NA_COLS)
    col_idx = col_start[:, None] + jnp.arange(NA_COLS)[None, :]
    dc = col_idx - cols[:, None]
    scale = ATT_HEAD_DIM ** -0.5

    def row_block(r):
        rs = jnp.clip(r - wr // 2, 0, rows - wr)
        q_r = lax.dynamic_index_in_dim(qg, r, axis=1, keepdims=False)
        k_w = lax.dynamic_slice_in_dim(kg, rs, wr, axis=1)
        v_w = lax.dynamic_slice_in_dim(vg, rs, wr, axis=1)
        k_sel = k_w[:, :, col_idx]
        v_sel = v_w[:, :, col_idx]
        dr = rs + jnp.arange(wr) - r
        bias = rpb[:, (dr + NA_ROWS - 1)[None, :, None], (dc + NA_COLS - 1)[:, None, :]]
        s = jnp.einsum('bqhd,bwqchd->bhqwc', q_r, k_sel).astype(jnp.float32) * scale
        s = s + bias.astype(jnp.float32)[None]
        p = jax.nn.softmax(s.reshape(B, H, GRID_W, wr * NA_COLS), axis=-1)
        p = p.reshape(B, H, GRID_W, wr, NA_COLS).astype(v.dtype)
        return jnp.einsum('bhqwc,bwqchd->bqhd', p, v_sel)

    out = lax.map(row_block, jnp.arange(rows))
    return jnp.moveaxis(out, 0, 1).reshape(B, T, H, Dh)


def wkv_scan(r, w, k, v, a, b, reverse):
    f32 = jnp.float32
    B, T, H, N = r.shape
    xs = tuple(jnp.moveaxis(u.astype(f32), 1, 0) for u in (r, w, k, v, a, b))

    def step(S, inp):
        r_t, w_t, k_t, v_t, a_t, b_t = inp
        sa = jnp.einsum('bhij,bhj->bhi', S, a_t)
        S = S * w_t[:, :, None, :] + sa[..., None] * b_t[:, :, None, :] + v_t[..., None] * k_t[:, :, None, :]
        return S, jnp.einsum('bhij,bhj->bhi', S, r_t)

    S0 = jnp.zeros((B, H, N, N), f32)
    _, y = lax.scan(step, S0, xs, reverse=reverse)
    return jnp.moveaxis(y, 0, 1)


def heads(u):
    return u.reshape(u.shape[:-1] + (RWKV_HEADS, RWKV_HEAD_DIM))


def rwkv_time_mix(h, r, k, v, v_first, mu_rkv, mu_x, w0, w1, w2, a0, a1, a2, g1, g2,
                  k_k, k_a, r_k, ln_w, ln_b, vres):
    f32 = jnp.float32
    B, T, _ = h.shape
    r = token_shift(r, mu_rkv[0])
    k = token_shift(k, mu_rkv[1])
    v = token_shift(v, mu_rkv[2])
    xw = token_shift(h, mu_x[0])
    xa = token_shift(h, mu_x[1])
    xg = token_shift(h, mu_x[2])
    if vres is None:
        v_first = v
    else:
        mu_v, v0, v1, v2 = vres
        xv = token_shift(h, mu_v)
        v = v + (v_first - v) * jax.nn.sigmoid(v0 + (xv @ v1) @ v2)
    w_log = w0[:, None, None, :] + jnp.einsum('ebtr,erc->ebtc', jnp.tanh(jnp.einsum('btd,edr->ebtr', xw, w1)), w2)
    decay = jnp.exp(-jnp.exp(-jax.nn.softplus(-w_log.astype(f32)) - 0.5))
    a = jax.nn.sigmoid(a0[:, None, None, :] + jnp.einsum('ebtr,erc->ebtc', jnp.einsum('btd,edr->ebtr', xa, a1), a2))
    g = jax.nn.sigmoid(xg @ g1) @ g2
    kk = heads((k * k_k).astype(f32))
    kk = kk * lax.rsqrt(jnp.maximum(jnp.sum(kk * kk, axis=-1, keepdims=True), 1e-24))
    k_dir = k[None] * (1.0 + (a - 1.0) * k_a)
    rh = heads(r)
    vh = heads(v)
    y = jnp.zeros((B, T, RWKV_HEADS, RWKV_HEAD_DIM), f32)
    for d, reverse in ((0, False), (1, True)):
        kd = heads(k_dir[d])
        ad = heads(a[d]).astype(f32)
        y = y + wkv_scan(rh, heads(decay[d]), kd, vh, -kk, kk * ad, reverse)
        y = y + (jnp.sum(rh * kd * r_k, axis=-1, keepdims=True) * vh).astype(f32)
    mean = jnp.mean(y, axis=-1, keepdims=True)
    var = jnp.mean(jnp.square(y - mean), axis=-1, keepdims=True)
    y = ((y - mean) * lax.rsqrt(var + GN_EPS)).reshape(B, T, RWKV_WIDTH)
    y = (y * ln_w.astype(f32) + ln_b.astype(f32)).astype(h.dtype)
    return y * g, v_first


def trunk(x, c, w_ada, b_ada, g_pre_mix, g_post_mix, g_pre_ffn, g_post_ffn, w_in, rpb,
          g_att_out, mu_rkv, mu_x, w0, w1, w2, a0, a1, a2, g1, g2, mu_v, v0, v1, v2,
          k_k, k_a, r_k, ln_x_w, ln_x_b, w_out, w_ffn1, w_ffn2):
    B, T, _ = x.shape
    A, R = ATT_WIDTH, RWKV_WIDTH
    cs = jax.nn.silu(c)
    v_first = None
    for l in range(DEPTH):
        mod = (cs @ w_ada[l] + b_ada[l])[:, None, :]
        sh1, sc1, gt1, sh2, sc2, gt2 = jnp.split(mod, 6, axis=-1)
        h = rmsnorm(x, g_pre_mix[l]) * (1.0 + sc1) + sh1
        proj = h @ w_in[l]
        qa, ka, va, rr, kr, vr = jnp.split(proj, [A, 2 * A, 3 * A, 3 * A + R, 3 * A + 2 * R], axis=-1)
        att = neighbourhood_attention(qa.reshape(B, T, ATT_HEADS, ATT_HEAD_DIM),
                                      ka.reshape(B, T, ATT_HEADS, ATT_HEAD_DIM),
                                      va.reshape(B, T, ATT_HEADS, ATT_HEAD_DIM), rpb[l])
        att = rmsnorm(att.reshape(B, T, A), g_att_out[l])
        vres = None if l == 0 else (mu_v[l - 1], v0[l - 1], v1[l - 1], v2[l - 1])
        rw, v_first = rwkv_time_mix(h, rr, kr, vr, v_first, mu_rkv[l], mu_x[l], w0[l], w1[l], w2[l],
                                    a0[l], a1[l], a2[l], g1[l], g2[l], k_k[l], k_a[l], r_k[l],
                                    ln_x_w[l], ln_x_b[l], vres)
        mix = jnp.concatenate([att, rw], axis=-1) @ w_out[l]
        x = x + gt1 * rmsnorm(mix, g_post_mix[l])
        h2 = rmsnorm(x, g_pre_ffn[l]) * (1.0 + sc2) + sh2
        f = jnp.square(jax.nn.relu(h2 @ w_ffn1[l])) @ w_ffn2[l]
        x = x + gt2 * rmsnorm(f, g_post_ffn[l])
    return x


def setup_inputs(seed: int = 0) -> dict:
    key = jax.random.key(seed)
    ks = jax.random.split(key, 40)
    f32 = jnp.float32
    L, D, A, R = DEPTH, D_MODEL, ATT_WIDTH, RWKV_WIDTH

    def nrm(k, shape, scale):
        return jax.random.normal(k, shape, f32) * scale

    def gain(k, shape):
        return 1.0 + 0.02 * jax.random.normal(k, shape, f32)

    def uni(k, shape, lo, hi):
        return jax.random.uniform(k, shape, f32, minval=lo, maxval=hi)

    return {
        "x_prompt": nrm(ks[0], (BATCH, SEQ, D), 1.0),
        "x_sample": nrm(ks[1], (DEC_BATCH, DEC_SEQ, D), 1.0),
        "c_prompt": nrm(ks[2], (BATCH, D), 1.0),
        "c_sample": nrm(ks[3], (DEC_BATCH, D), 1.0),
        "w_ada": nrm(ks[4], (L, D, 6 * D), 0.5 * D ** -0.5),
        "b_ada": nrm(ks[5], (L, 6 * D), 0.02),
        "g_pre_mix": gain(ks[6], (L, D)),
        "g_post_mix": gain(ks[7], (L, D)),
        "g_pre_ffn": gain(ks[8], (L, D)),
        "g_post_ffn": gain(ks[9], (L, D)),
        "w_in": nrm(ks[10], (L, D, IN_WIDTH), D ** -0.5),
        "rpb": nrm(ks[11], (L, ATT_HEADS, 2 * NA_ROWS - 1, 2 * NA_COLS - 1), 0.5),
        "g_att_out": gain(ks[12], (L, A)),
        "mu_rkv": uni(ks[13], (L, 3, 2, R), 0.0, 0.5),
        "mu_x": uni(ks[14], (L, 3, 2, D), 0.0, 0.5),
        "w0": uni(ks[15], (L, 2, R), -5.0, -1.0),
        "w1": nrm(ks[16], (L, 2, D, DECAY_LORA), D ** -0.5),
        "w2": nrm(ks[17], (L, 2, DECAY_LORA, R), 0.3 * DECAY_LORA ** -0.5),
        "a0": nrm(ks[18], (L, 2, R), 0.5),
        "a1": nrm(ks[19], (L, 2, D, AAA_LORA), D ** -0.5),
        "a2": nrm(ks[20], (L, 2, AAA_LORA, R), 0.3 * AAA_LORA ** -0.5),
        "g1": nrm(ks[21], (L, D, GATE_LORA), D ** -0.5),
        "g2": nrm(ks[22], (L, GATE_LORA, R), GATE_LORA ** -0.5),
        "mu_v": uni(ks[23], (L - 1, 2, D), 0.0, 0.5),
        "v0": nrm(ks[24], (L - 1, R), 0.5),
        "v1": nrm(ks[25], (L - 1, D, MV_LORA), D ** -0.5),
        "v2": nrm(ks[26], (L - 1, MV_LORA, R), 0.3 * MV_LORA ** -0.5),
        "k_k": 0.85 + 0.1 * jax.random.normal(ks[27], (L, R), f32),
        "k_a": 1.0 + 0.05 * jax.random.normal(ks[28], (L, R), f32),
        "r_k": nrm(ks[29], (L, RWKV_HEADS, RWKV_HEAD_DIM), 0.1),
        "ln_x_w": gain(ks[30], (L, R)),
        "ln_x_b": nrm(ks[31], (L, R), 0.02),
        "w_out": nrm(ks[32], (L, A + R, D), (A + R) ** -0.5),
        "w_ffn1": nrm(ks[33], (L, D, D_FF), D ** -0.5),
        "w_ffn2": nrm(ks[34], (L, D_FF, D), D_FF ** -0.5),
    }


def reference(x_prompt, x_sample, c_prompt, c_sample, w_ada, b_ada, g_pre_mix, g_post_mix,
              g_pre_ffn, g_post_ffn, w_in, rpb, g_att_out, mu_rkv, mu_x, w0, w1, w2, a0, a1, a2,
              g1, g2, mu_v, v0, v1, v2, k_k, k_a, r_k, ln_x_w, ln_x_b, w_out, w_ffn1, w_ffn2):
    y_prompt = trunk(x_prompt, c_prompt, w_ada, b_ada, g_pre_mix, g_post_mix, g_pre_ffn, g_post_ffn,
                     w_in, rpb, g_att_out, mu_rkv, mu_x, w0, w1, w2, a0, a1, a2, g1, g2, mu_v, v0, v1, v2,
                     k_k, k_a, r_k, ln_x_w, ln_x_b, w_out, w_ffn1, w_ffn2)
    y_sample = trunk(x_sample, c_sample, w_ada, b_ada, g_pre_mix, g_post_mix, g_pre_ffn, g_post_ffn,
                     w_in, rpb, g_att_out, mu_rkv, mu_x, w0, w1, w2, a0, a1, a2, g1, g2, mu_v, v0, v1, v2,
                     k_k, k_a, r_k, ln_x_w, ln_x_b, w_out, w_ffn1, w_ffn2)
    return (y_prompt, y_sample)
```

```python
import contextlib
import numpy as np
import ml_dtypes
import concourse.bass as bass
import concourse.mybir as mybir
from concourse.bass_utils import run_bass_kernel_spmd

F32 = mybir.dt.float32
BF16 = mybir.dt.bfloat16
F32R = mybir.dt.float32r
ALU = mybir.AluOpType
AF = mybir.ActivationFunctionType
AX = mybir.AxisListType

D = 2048
KC = 16
NT = 512
GW = 64
NEG = -30000.0
EPS = 1e-6
GN_EPS = 64e-5
NPV = 256
NLORA = 352


class Sem:
    def __init__(self, nc, name):
        self.h = nc.alloc_semaphore(name=name)
        self.count = 0


class Buf:
    __slots__ = ("w", "r")

    def __init__(self):
        self.w = {}
        self.r = {}


def flat(x):
    if isinstance(x, Buf):
        yield x
    elif x is None:
        return
    else:
        for y in x:
            yield from flat(y)


class Prog:
    def __init__(self, nc):
        self.nc = nc
        self.eng = {"pe": nc.tensor, "dve": nc.vector, "act": nc.scalar, "pool": nc.gpsimd, "sp": nc.sync}
        self.esem = {k: Sem(nc, "e_" + k) for k in self.eng}
        self.obs = {k: {} for k in self.eng}
        self.dsems = {"sp": [Sem(nc, f"dsp{i}") for i in range(16)],
                      "pool": [Sem(nc, f"dpl{i}") for i in range(16)],
                      "act": [Sem(nc, f"dac{i}") for i in range(8)]}
        self.dnext = {"sp": 0, "pool": 0, "act": 0}
        self.uid = 0
        self.ps = []
        self.psb = []
        self.psn = 0
        self.rr = 0
        self.rec = None
        self.psp = 0
        self.nmain = 8

    def name(self, s):
        self.uid += 1
        return f"{s}_{self.uid}"

    def _wait(self, e, sem, val):
        if val <= 0:
            return
        o = self.obs[e]
        if o.get(sem, 0) >= val:
            return
        self.eng[e].wait_ge(sem.h, val)
        o[sem] = val

    def _deps(self, e, reads, writes):
        need = {}
        for b in flat(reads):
            for s, v in b.w.items():
                if need.get(s, 0) < v:
                    need[s] = v
        for b in flat(writes):
            for s, v in b.w.items():
                if need.get(s, 0) < v:
                    need[s] = v
            for s, v in b.r.items():
                if need.get(s, 0) < v:
                    need[s] = v
        for s, v in need.items():
            self._wait(e, s, v)

    def _mark(self, s, v, reads, writes):
        for b in flat(reads):
            if b.r.get(s, 0) < v:
                b.r[s] = v
        for b in flat(writes):
            b.w = {s: v}
            b.r = {}

    def replay(self, it):
        if it[0] == "op":
            self.op(it[1], it[2], it[3], it[4])
        else:
            self.dma(it[1], it[2], it[3], it[4], it[5], **it[6])

    def op(self, e, fn, reads=(), writes=()):
        if self.rec is not None:
            self.rec.append(("op", e, fn, reads, writes))
            return
        self._deps(e, reads, writes)
        ins = fn(self.eng[e])
        sem = self.esem[e]
        sem.count += 1
        ins.then_inc(sem.h, 1)
        self._mark(sem, sem.count, reads, writes)

    def dma(self, e, out, in_, reads=(), writes=(), **kw):
        if self.rec is not None:
            self.rec.append(("dma", e, out, in_, reads, writes, kw))
            return
        self._deps(e, reads, writes)
        lst = self.dsems[e]
        ds = lst[self.dnext[e]]
        self.dnext[e] = (self.dnext[e] + 1) % len(lst)
        self._wait(e, ds, 16 * ds.count)
        ds.count += 1
        self.eng[e].dma_start(out=out, in_=in_, **kw).then_inc(ds.h, 16)
        self._mark(ds, 16 * ds.count, reads, writes)

    def barrier(self):
        sems = list(self.esem.values())
        for lst in self.dsems.values():
            sems += lst
        for e in self.eng:
            for s in sems:
                if s in self.esem.values():
                    self._wait(e, s, s.count)
                else:
                    self._wait(e, s, 16 * s.count)

    def psum(self, pool=None):
        if pool == "prep":
            self.psp = (self.psp + 1) % 2
            i = 6 + self.psp
            return self.ps[i], self.psb[i]
        i = self.psn % self.nmain
        self.psn = (self.psn + 1) % self.nmain
        return self.ps[i], self.psb[i]

    def ev_eng(self):
        self.rr += 1
        return "act" if self.rr % 2 else "dve"


def sb(P, es, nm, shape, dt):
    return es.enter_context(P.nc.sbuf_tensor(P.name(nm), list(shape), dt))


def build(cfg):
    L = cfg["depth"]
    TS = cfg["seqs"]
    NS = len(TS)
    dbg = cfg.get("debug", False)
    nc = bass.Bass("TRN2", target_bir_lowering=False)
    P = Prog(nc)
    SK = "ExternalOutput" if dbg else "Internal"

    def din(name, shape, dt=F32):
        return nc.dram_tensor(name, list(shape), dt, kind="ExternalInput").ap()

    def dscr(name, shape, dt, kind=None):
        return nc.dram_tensor(name, list(shape), dt, kind=kind or SK).ap()

    x_in = [din(f"x{s}", [TS[s], D]) for s in range(NS)]
    y_out = [nc.dram_tensor(f"y{s}", [TS[s], D], F32, kind="ExternalOutput").ap() for s in range(NS)]
    cvec = din("cvec", [128, KC, NS])
    win32 = din("win", [L, 12, 128, KC, 512])
    wout32 = din("wout", [L, 4, 128, KC, 512])
    wf132 = din("wf1", [L, 16, 128, KC, 512])
    wf232 = din("wf2", [L, 2, 16, 128, 32, 128])
    wada32 = din("wada", [L, 24, 128, KC, 512])
    pvec = din("pvec", [L, 128, NPV])
    lora_dn = din("lora_dn", [L, 128, KC, NLORA])
    lora_mp = din("lora_mp", [L, 128, KC, NLORA])
    lora_mn = din("lora_mn", [L, 128, KC, NLORA])
    w2aug = din("w2aug", [L, 2, 65, 1024])
    a2aug = din("a2aug", [L, 2, 65, 1024])
    g2in = din("g2", [L, 64, 1024])
    v2aug = din("v2aug", [L, 33, 1024])
    rpb = din("rpb", [L, 8, 15, 31])
    cident = din("cident", [128, 128])
    cmask = din("cmask", [2, 128, 512])
    cmaskn = din("cmaskn", [2, 128, 128])
    cblk = din("cblk", [128, 128])
    ckeep = din("ckeep", [128, 512])

    win = dscr("win_b", [L, 12, 128, KC, 512], BF16, "Internal")
    wout = dscr("wout_b", [L, 4, 128, KC, 512], BF16, "Internal")
    wf1 = dscr("wf1_b", [L, 16, 128, KC, 512], BF16, "Internal")
    wf2 = dscr("wf2_b", [L, 2, 16, 128, 32, 128], BF16, "Internal")
    wada = dscr("wada_b", [L, 24, 128, KC, 512], BF16, "Internal")

    xres = [dscr(f"xres{s}", [D, TS[s]], F32) for s in range(NS)]
    hfm = [dscr(f"hfm{s}", [D, TS[s] + 2], BF16) for s in range(NS)]
    qfm = [dscr(f"qfm{s}", [1024, TS[s]], BF16) for s in range(NS)]
    kfm = [dscr(f"kfm{s}", [1024, TS[s]], BF16) for s in range(NS)]
    vtm = [dscr(f"vtm{s}", [TS[s] + 64, 1024], BF16) for s in range(NS)]
    rkv = [dscr(f"rkv{s}", [3072, TS[s] + 2], F32) for s in range(NS)]
    lw_d = [dscr(f"lw{s}", [128, TS[s]], BF16) for s in range(NS)]
    la_d = [dscr(f"la{s}", [128, TS[s]], BF16) for s in range(NS)]
    lg_d = [dscr(f"lg{s}", [64, TS[s]], BF16) for s in range(NS)]
    lv_d = [dscr(f"lv{s}", [32, TS[s]], BF16) for s in range(NS)]
    mixin = [dscr(f"mixin{s}", [D, TS[s]], BF16) for s in range(NS)]
    ysc = [dscr(f"ysc{s}", [2, 1024, TS[s]], F32) for s in range(NS)]
    vfirst = [dscr(f"vfirst{s}", [1024, TS[s]], F32) for s in range(NS)]

    with contextlib.ExitStack() as g:
        for i in range(8):
            P.ps.append(g.enter_context(nc.psum_tensor(f"ps{i}", [128, 512], F32)))
            P.psb.append(Buf())
        ident = sb(P, g, "ident", [128, 128], F32)
        identb = sb(P, g, "identb", [128, 128], BF16)
        onesb = sb(P, g, "onesb", [128, 128], BF16)
        blkb = sb(P, g, "blkb", [128, 128], BF16)
        blkf = sb(P, g, "blkf", [128, 128], F32)
        maskb = sb(P, g, "maskb", [128, 2, 512], BF16)
        masknb = sb(P, g, "masknb", [128, 2, 128], BF16)
        keep = sb(P, g, "keep", [128, 512], F32)
        maskPf = sb(P, g, "maskPf", [128, 2, 128], F32)
        maskNf = sb(P, g, "maskNf", [128, 2, 128], F32)
        onesf = sb(P, g, "onesf", [128, 512], F32)
        cs = sb(P, g, "cs", [128, KC, NS], BF16)
        pv = sb(P, g, "pv", [128, NPV], F32)
        modv = sb(P, g, "modv", [128, NS, 6, KC], F32)
        cB = Buf()
        pvB = Buf()
        modB = Buf()

        with contextlib.ExitStack() as es:
            t1 = sb(P, es, "t1", [128, 2, 512], F32)
            t2 = sb(P, es, "t2", [128, 2, 128], F32)
            t3 = sb(P, es, "t3", [128, 128], F32)
            cv = sb(P, es, "cv", [128, KC, NS], F32)
            tb = Buf()
            P.dma("sp", ident[:], cident[:, :], writes=cB)
            P.dma("sp", t1[:], cmask.rearrange("d p n -> p d n"), writes=tb)
            P.dma("sp", t2[:], cmaskn.rearrange("d p n -> p d n"), writes=tb)
            P.dma("sp", t3[:], cblk[:, :], writes=tb)
            P.dma("sp", blkf[:], cblk[:, :], writes=cB)
            P.dma("sp", keep[:], ckeep[:, :], writes=cB)
            P.dma("sp", cv[:], cvec[:, :, :], writes=tb)
            P.op("dve", lambda e: e.tensor_copy(out=identb[:], in_=ident[:]), reads=cB, writes=cB)
            P.op("dve", lambda e: e.tensor_copy(out=maskb[:], in_=t1[:]), reads=tb, writes=cB)
            P.op("dve", lambda e: e.tensor_copy(out=masknb[:], in_=t2[:]), reads=tb, writes=cB)
            P.op("dve", lambda e: e.tensor_copy(out=maskPf[:], in_=t1[:, :, 0:128]), reads=tb, writes=cB)
            P.op("dve", lambda e: e.tensor_copy(out=maskNf[:], in_=t2[:]), reads=tb, writes=cB)
            P.op("dve", lambda e: e.tensor_copy(out=blkb[:], in_=t3[:]), reads=tb, writes=cB)
            P.op("dve", lambda e: e.memset(onesb[:], 1.0), writes=cB)
            P.op("dve", lambda e: e.memset(onesf[:], 1.0), writes=cB)
            P.op("act", lambda e: e.activation(out=cs[:], in_=cv[:], func=AF.Silu), reads=tb, writes=cB)
            P.barrier()

        def conv(dst, src):
            n = 1
            for d_ in src.shape:
                n *= d_
            rows = n // 2048
            s2 = src.tensor.reshape([rows, 2048]) if False else None
            return n

        def conv2d(dst2, src2):
            rows = src2.shape[0]
            step = 2048
            for r0 in range(0, rows, step):
                r1 = min(rows, r0 + step)
                P.dma("pool", dst2[r0:r1, :], src2[r0:r1, :])

        for l in range(L):
            conv2d(win[l].rearrange("g p k (a n) -> (g p k a) n", n=512), win32[l].rearrange("g p k (a n) -> (g p k a) n", n=512))
            conv2d(wout[l].rearrange("g p k (a n) -> (g p k a) n", n=512), wout32[l].rearrange("g p k (a n) -> (g p k a) n", n=512))
            conv2d(wf1[l].rearrange("g p k (a n) -> (g p k a) n", n=512), wf132[l].rearrange("g p k (a n) -> (g p k a) n", n=512))
            conv2d(wf2[l].rearrange("h g p (k a) n -> (h g p k) (a n)", a=4), wf232[l].rearrange("h g p (k a) n -> (h g p k) (a n)", a=4))
            conv2d(wada[l].rearrange("g p k (a n) -> (g p k a) n", n=512), wada32[l].rearrange("g p k (a n) -> (g p k a) n", n=512))

        for s in range(NS):
            T = TS[s]
            xr = xres[s].rearrange("(c p) t -> p c t", p=128)
            with contextlib.ExitStack() as es:
                xt = [sb(P, es, "xt", [128, 4, D], F32) for _ in range(2)]
                xtB = [Buf(), Buf()]
                xf = [sb(P, es, "xf", [128, KC, NT], F32) for _ in range(2)]
                xfB = [Buf(), Buf()]
                for ti in range(T // NT):
                    k = ti % 2
                    P.dma("sp", xt[k][:], x_in[s][ti * NT:(ti + 1) * NT, :].rearrange("(j p) d -> p j d", p=128), writes=xtB[k])
                    for c in range(KC):
                        ps, pb = P.psum()

                        def f(e, ps=ps, c=c, k=k):
                            for j in range(4):
                                ins = e.transpose(out=ps[:, j * 128:(j + 1) * 128], in_=xt[k][:, j, c * 128:(c + 1) * 128], identity=ident[:])
                            return ins
                        P.op("pe", f, reads=[xtB[k], cB], writes=pb)
                        ee = P.ev_eng()
                        if ee == "act":
                            P.op("act", lambda e, ps=ps, c=c, k=k: e.copy(out=xf[k][:, c, :], in_=ps[:]), reads=pb, writes=xfB[k])
                        else:
                            P.op("dve", lambda e, ps=ps, c=c, k=k: e.tensor_copy(out=xf[k][:, c, :], in_=ps[:]), reads=pb, writes=xfB[k])
                    P.dma("pool", xr[:, :, ti * NT:(ti + 1) * NT], xf[k][:], reads=xfB[k])
                P.barrier()

        with contextlib.ExitStack() as es:
            zf = sb(P, es, "zf", [128, 24], F32)
            zb = sb(P, es, "zb", [128, 16], BF16)
            zB = Buf()
            P.op("dve", lambda e: e.memset(zf[:], 0.0), writes=zB)
            P.op("dve", lambda e: e.memset(zb[:], 0.0), writes=zB)
            for s in range(NS):
                T = TS[s]
                for col in (0, T + 1):
                    with nc.allow_non_contiguous_dma(reason="halo zero"):
                        P.dma("sp", hfm[s].rearrange("(c p) t -> p c t", p=128)[:, :, col:col + 1], zb[:].unsqueeze(2), reads=zB)
                        P.dma("sp", rkv[s].rearrange("(c p) t -> p c t", p=128)[:, :, col:col + 1], zf[:].unsqueeze(2), reads=zB)
            P.barrier()

        for l in range(L):
            layer(P, g, cfg, l, locals())

        for s in range(NS):
            T = TS[s]
            xr = xres[s].rearrange("(c p) t -> p c t", p=128)
            with contextlib.ExitStack() as es:
                xf = [sb(P, es, "xf", [128, KC, NT], F32) for _ in range(2)]
                xfB = [Buf(), Buf()]
                xt = [sb(P, es, "xt", [128, 4, D], F32) for _ in range(2)]
                xtB = [Buf(), Buf()]
                for ti in range(T // NT):
                    k = ti % 2
                    P.dma("sp", xf[k][:], xr[:, :, ti * NT:(ti + 1) * NT], writes=xfB[k])
                    for j in range(4):
                        for c4 in range(4):
                            ps, pb = P.psum()

                            def f(e, ps=ps, c4=c4, j=j, k=k):
                                for cc in range(4):
                                    c = c4 * 4 + cc
                                    ins = e.transpose(out=ps[:, cc * 128:(cc + 1) * 128], in_=xf[k][:, c, j * 128:(j + 1) * 128], identity=ident[:])
                                return ins
                            P.op("pe", f, reads=[xfB[k], cB], writes=pb)
                            ee = P.ev_eng()
                            if ee == "act":
                                P.op("act", lambda e, ps=ps, c4=c4, j=j, k=k: e.copy(out=xt[k][:, j, c4 * 512:(c4 + 1) * 512], in_=ps[:]), reads=pb, writes=xtB[k])
                            else:
                                P.op("dve", lambda e, ps=ps, c4=c4, j=j, k=k: e.tensor_copy(out=xt[k][:, j, c4 * 512:(c4 + 1) * 512], in_=ps[:]), reads=pb, writes=xtB[k])
                    P.dma("pool", y_out[s][ti * NT:(ti + 1) * NT, :].rearrange("(j p) d -> p j d", p=128), xt[k][:], reads=xtB[k])
                P.barrier()
        P.barrier()
    return nc


def rms_rstd(P, G, src, srcB, kcn, sq, sqB, sd, rstd, rsB, dim, eps):
    onesb = G["onesb"]
    cB = G["cB"]
    P.op("act", lambda e: e.activation(out=sq[:, 0:kcn, :], in_=src, func=AF.Square), reads=srcB, writes=sqB)
    ps, pb = P.psum()

    def f(e):
        for c in range(kcn):
            ins = e.matmul(ps[:], lhsT=onesb[:], rhs=sq[:, c, :], start=(c == 0), stop=(c == kcn - 1))
        return ins
    P.op("pe", f, reads=[sqB, cB], writes=pb)
    P.op("act", lambda e: e.activation(out=sd[:], in_=ps[:], func=AF.Sqrt, scale=1.0 / dim, bias=float(eps)), reads=[pb], writes=rsB)
    P.op("dve", lambda e: e.reciprocal(out=rstd[:], in_=sd[:]), reads=rsB, writes=rsB)


def rms_rstd(P, G, src, srcB, kcn, sq, sqB, sd, rstd, rsB, dim, eps):
    onesb = G["onesb"]
    cB = G["cB"]
    P.op("act", lambda e: e.activation(out=sq[:, 0:kcn, :], in_=src, func=AF.Square), reads=srcB, writes=sqB)
    ps, pb = P.psum()

    def f(e):
        for c in range(kcn):
            ins = e.matmul(ps[:], lhsT=onesb[:], rhs=sq[:, c, :], start=(c == 0), stop=(c == kcn - 1))
        return ins
    P.op("pe", f, reads=[sqB, cB], writes=pb)
    P.op("act", lambda e: e.activation(out=sd[:], in_=ps[:], func=AF.Sqrt, scale=1.0 / dim, bias=float(eps)), reads=[pb], writes=rsB)
    P.op("dve", lambda e: e.reciprocal(out=rstd[:], in_=sd[:]), reads=rsB, writes=rsB)


def rms_rstd(P, G, src, srcB, kcn, sq, sqB, sd, rstd, rsB, dim, eps):
    onesb = G["onesb"]
    cB = G["cB"]
    P.op("act", lambda e: e.activation(out=sq[:, 0:kcn, :], in_=src, func=AF.Square), reads=srcB, writes=sqB)
    ps, pb = P.psum()

    def f(e):
        for c in range(kcn):
            ins = e.matmul(ps[:], lhsT=onesb[:], rhs=sq[:, c, :], start=(c == 0), stop=(c == kcn - 1))
        return ins
    P.op("pe", f, reads=[sqB, cB], writes=pb)
    P.op("act", lambda e: e.activation(out=sd[:], in_=ps[:], func=AF.Sqrt, scale=1.0 / dim, bias=float(eps)), reads=[pb], writes=rsB)
    P.op("dve", lambda e: e.reciprocal(out=rstd[:], in_=sd[:]), reads=rsB, writes=rsB)


def rms_rstd(P, G, src, srcB, kcn, sq, sqB, sd, rstd, rsB, dim, eps):
    onesb = G["onesb"]
    cB = G["cB"]
    P.op("act", lambda e: e.activation(out=sq[:, 0:kcn, :], in_=src, func=AF.Square), reads=srcB, writes=sqB)
    ps, pb = P.psum()

    def f(e):
        for c in range(kcn):
            ins = e.matmul(ps[:], lhsT=onesb[:], rhs=sq[:, c, :], start=(c == 0), stop=(c == kcn - 1))
        return ins
    P.op("pe", f, reads=[sqB, cB], writes=pb)
    P.op("act", lambda e: e.activation(out=sd[:], in_=ps[:], func=AF.Sqrt, scale=1.0 / dim, bias=float(eps)), reads=[pb], writes=rsB)
    P.op("dve", lambda e: e.reciprocal(out=rstd[:], in_=sd[:]), reads=rsB, writes=rsB)


def rms_rstd(P, G, src, srcB, kcn, sq, sqB, sd, rstd, rsB, dim, eps):
    onesb = G["onesb"]
    cB = G["cB"]
    P.op("act", lambda e: e.activation(out=sq[:, 0:kcn, :], in_=src, func=AF.Square), reads=srcB, writes=sqB)
    ps, pb = P.psum()

    def f(e):
        for c in range(kcn):
            ins = e.matmul(ps[:], lhsT=onesb[:], rhs=sq[:, c, :], start=(c == 0), stop=(c == kcn - 1))
        return ins
    P.op("pe", f, reads=[sqB, cB], writes=pb)
    P.op("act", lambda e: e.activation(out=sd[:], in_=ps[:], func=AF.Sqrt, scale=1.0 / dim, bias=float(eps)), reads=[pb], writes=rsB)
    P.op("dve", lambda e: e.reciprocal(out=rstd[:], in_=sd[:]), reads=rsB, writes=rsB)


def rms_rstd(P, G, src, srcB, kcn, sq, sqB, sd, rstd, rsB, dim, eps):
    onesb = G["onesb"]
    cB = G["cB"]
    P.op("act", lambda e: e.activation(out=sq[:, 0:kcn, :], in_=src, func=AF.Square), reads=srcB, writes=sqB)
    ps, pb = P.psum()

    def f(e):
        for c in range(kcn):
            ins = e.matmul(ps[:], lhsT=onesb[:], rhs=sq[:, c, :], start=(c == 0), stop=(c == kcn - 1))
        return ins
    P.op("pe", f, reads=[sqB, cB], writes=pb)
    P.op("act", lambda e: e.activation(out=sd[:], in_=ps[:], func=AF.Sqrt, scale=1.0 / dim, bias=float(eps)), reads=[pb], writes=rsB)
    P.op("dve", lambda e: e.reciprocal(out=rstd[:], in_=sd[:]), reads=rsB, writes=rsB)


def rms_rstd(P, G, src, srcB, kcn, sq, sqB, sd, rstd, rsB, dim, eps):
    onesb = G["onesb"]
    cB = G["cB"]
    P.op("act", lambda e: e.activation(out=sq[:, 0:kcn, :], in_=src, func=AF.Square), reads=srcB, writes=sqB)
    ps, pb = P.psum()

    def f(e):
        for c in range(kcn):
            ins = e.matmul(ps[:], lhsT=onesb[:], rhs=sq[:, c, :], start=(c == 0), stop=(c == kcn - 1))
        return ins
    P.op("pe", f, reads=[sqB, cB], writes=pb)
    P.op("act", lambda e: e.activation(out=sd[:], in_=ps[:], func=AF.Sqrt, scale=1.0 / dim, bias=float(eps)), reads=[pb], writes=rsB)
    P.op("dve", lambda e: e.reciprocal(out=rstd[:], in_=sd[:]), reads=rsB, writes=rsB)


def rms_rstd(P, G, src, srcB, kcn, sq, sqB, sd, rstd, rsB, dim, eps):
    onesb = G["onesb"]
    cB = G["cB"]
    P.op("act", lambda e: e.activation(out=sq[:, 0:kcn, :], in_=src, func=AF.Square), reads=srcB, writes=sqB)
    ps, pb = P.psum()

    def f(e):
        for c in range(kcn):
            ins = e.matmul(ps[:], lhsT=onesb[:], rhs=sq[:, c, :], start=(c == 0), stop=(c == kcn - 1))
        return ins
    P.op("pe", f, reads=[sqB, cB], writes=pb)
    P.op("act", lambda e: e.activation(out=sd[:], in_=ps[:], func=AF.Sqrt, scale=1.0 / dim, bias=float(eps)), reads=[pb], writes=rsB)
    P.op("dve", lambda e: e.reciprocal(out=rstd[:], in_=sd[:]), reads=rsB, writes=rsB)


def rms_rstd(P, G, src, srcB, kcn, sq, sqB, sd, rstd, rsB, dim, eps):
    onesb = G["onesb"]
    cB = G["cB"]
    P.op("act", lambda e: e.activation(out=sq[:, 0:kcn, :], in_=src, func=AF.Square), reads=srcB, writes=sqB)
    ps, pb = P.psum()

    def f(e):
        for c in range(kcn):
            ins = e.matmul(ps[:], lhsT=onesb[:], rhs=sq[:, c, :], start=(c == 0), stop=(c == kcn - 1))
        return ins
    P.op("pe", f, reads=[sqB, cB], writes=pb)
    P.op("act", lambda e: e.activation(out=sd[:], in_=ps[:], func=AF.Sqrt, scale=1.0 / dim, bias=float(eps)), reads=[pb], writes=rsB)
    P.op("dve", lambda e: e.reciprocal(out=rstd[:], in_=sd[:]), reads=rsB, writes=rsB)


def rms_rstd(P, G, src, srcB, kcn, sq, sqB, sd, rstd, rsB, dim, eps):
    onesb = G["onesb"]
    cB = G["cB"]
    P.op("act", lambda e: e.activation(out=sq[:, 0:kcn, :], in_=src, func=AF.Square), reads=srcB, writes=sqB)
    ps, pb = P.psum()

    def f(e):
        for c in range(kcn):
            ins = e.matmul(ps[:], lhsT=onesb[:], rhs=sq[:, c, :], start=(c == 0), stop=(c == kcn - 1))
        return ins
    P.op("pe", f, reads=[sqB, cB], writes=pb)
    P.op("act", lambda e: e.activation(out=sd[:], in_=ps[:], func=AF.Sqrt, scale=1.0 / dim, bias=float(eps)), reads=[pb], writes=rsB)
    P.op("dve", lambda e: e.reciprocal(out=rstd[:], in_=sd[:]), reads=rsB, writes=rsB)


def rms_rstd(P, G, src, srcB, kcn, sq, sqB, sd, rstd, rsB, dim, eps):
    onesb = G["onesb"]
    cB = G["cB"]
    P.op("act", lambda e: e.activation(out=sq[:, 0:kcn, :], in_=src, func=AF.Square), reads=srcB, writes=sqB)
    ps, pb = P.psum()

    def f(e):
        for c in range(kcn):
            ins = e.matmul(ps[:], lhsT=onesb[:], rhs=sq[:, c, :], start=(c == 0), stop=(c == kcn - 1))
        return ins
    P.op("pe", f, reads=[sqB, cB], writes=pb)
    P.op("act", lambda e: e.activation(out=sd[:], in_=ps[:], func=AF.Sqrt, scale=1.0 / dim, bias=float(eps)), reads=[pb], writes=rsB)
    P.op("dve", lambda e: e.reciprocal(out=rstd[:], in_=sd[:]), reads=rsB, writes=rsB)


def rms_rstd(P, G, src, srcB, kcn, sq, sqB, sd, rstd, rsB, dim, eps):
    onesb = G["onesb"]
    cB = G["cB"]
    P.op("act", lambda e: e.activation(out=sq[:, 0:kcn, :], in_=src, func=AF.Square), reads=srcB, writes=sqB)
    ps, pb = P.psum()

    def f(e):
        for c in range(kcn):
            ins = e.matmul(ps[:], lhsT=onesb[:], rhs=sq[:, c, :], start=(c == 0), stop=(c == kcn - 1))
        return ins
    P.op("pe", f, reads=[sqB, cB], writes=pb)
    P.op("act", lambda e: e.activation(out=sd[:], in_=ps[:], func=AF.Sqrt, scale=1.0 / dim, bias=float(eps)), reads=[pb], writes=rsB)
    P.op("dve", lambda e: e.reciprocal(out=rstd[:], in_=sd[:]), reads=rsB, writes=rsB)


def rms_rstd(P, G, src, srcB, kcn, sq, sqB, sd, rstd, rsB, dim, eps):
    onesb = G["onesb"]
    cB = G["cB"]
    P.op("act", lambda e: e.activation(out=sq[:, 0:kcn, :], in_=src, func=AF.Square), reads=srcB, writes=sqB)
    ps, pb = P.psum()

    def f(e):
        for c in range(kcn):
            ins = e.matmul(ps[:], lhsT=onesb[:], rhs=sq[:, c, :], start=(c == 0), stop=(c == kcn - 1))
        return ins
    P.op("pe", f, reads=[sqB, cB], writes=pb)
    P.op("act", lambda e: e.activation(out=sd[:], in_=ps[:], func=AF.Sqrt, scale=1.0 / dim, bias=float(eps)), reads=[pb], writes=rsB)
    P.op("dve", lambda e: e.reciprocal(out=rstd[:], in_=sd[:]), reads=rsB, writes=rsB)


def rms_rstd(P, G, src, srcB, kcn, sq, sqB, sd, rstd, rsB, dim, eps):
    onesb = G["onesb"]
    cB = G["cB"]
    P.op("act", lambda e: e.activation(out=sq[:, 0:kcn, :], in_=src, func=AF.Square), reads=srcB, writes=sqB)
    ps, pb = P.psum()

    def f(e):
        for c in range(kcn):
            ins = e.matmul(ps[:], lhsT=onesb[:], rhs=sq[:, c, :], start=(c == 0), stop=(c == kcn - 1))
        return ins
    P.op("pe", f, reads=[sqB, cB], writes=pb)
    P.op("act", lambda e: e.activation(out=sd[:], in_=ps[:], func=AF.Sqrt, scale=1.0 / dim, bias=float(eps)), reads=[pb], writes=rsB)
    P.op("dve", lambda e: e.reciprocal(out=rstd[:], in_=sd[:]), reads=rsB, writes=rsB)


def rms_rstd(P, G, src, srcB, kcn, sq, sqB, sd, rstd, rsB, dim, eps):
    onesb = G["onesb"]
    cB = G["cB"]
    P.op("act", lambda e: e.activation(out=sq[:, 0:kcn, :], in_=src, func=AF.Square), reads=srcB, writes=sqB)
    ps, pb = P.psum()

    def f(e):
        for c in range(kcn):
            ins = e.matmul(ps[:], lhsT=onesb[:], rhs=sq[:, c, :], start=(c == 0), stop=(c == kcn - 1))
        return ins
    P.op("pe", f, reads=[sqB, cB], writes=pb)
    P.op("act", lambda e: e.activation(out=sd[:], in_=ps[:], func=AF.Sqrt, scale=1.0 / dim, bias=float(eps)), reads=[pb], writes=rsB)
    P.op("dve", lambda e: e.reciprocal(out=rstd[:], in_=sd[:]), reads=rsB, writes=rsB)


def rms_rstd(P, G, src, srcB, kcn, sq, sqB, sd, rstd, rsB, dim, eps):
    onesb = G["onesb"]
    cB = G["cB"]
    P.op("act", lambda e: e.activation(out=sq[:, 0:kcn, :], in_=src, func=AF.Square), reads=srcB, writes=sqB)
    ps, pb = P.psum()

    def f(e):
        for c in range(kcn):
            ins = e.matmul(ps[:], lhsT=onesb[:], rhs=sq[:, c, :], start=(c == 0), stop=(c == kcn - 1))
        return ins
    P.op("pe", f, reads=[sqB, cB], writes=pb)
    P.op("act", lambda e: e.activation(out=sd[:], in_=ps[:], func=AF.Sqrt, scale=1.0 / dim, bias=float(eps)), reads=[pb], writes=rsB)
    P.op("dve", lambda e: e.reciprocal(out=rstd[:], in_=sd[:]), reads=rsB, writes=rsB)


def rms_rstd(P, G, src, srcB, kcn, sq, sqB, sd, rstd, rsB, dim, eps):
    onesb = G["onesb"]
    cB = G["cB"]
    P.op("act", lambda e: e.activation(out=sq[:, 0:kcn, :], in_=src, func=AF.Square), reads=srcB, writes=sqB)
    ps, pb = P.psum()

    def f(e):
        for c in range(kcn):
            ins = e.matmul(ps[:], lhsT=onesb[:], rhs=sq[:, c, :], start=(c == 0), stop=(c == kcn - 1))
        return ins
    P.op("pe", f, reads=[sqB, cB], writes=pb)
    P.op("act", lambda e: e.activation(out=sd[:], in_=ps[:], func=AF.Sqrt, scale=1.0 / dim, bias=float(eps)), reads=[pb], writes=rsB)
    P.op("dve", lambda e: e.reciprocal(out=rstd[:], in_=sd[:]), reads=rsB, writes=rsB)


def rms_rstd(P, G, src, srcB, kcn, sq, sqB, sd, rstd, rsB, dim, eps):
    onesb = G["onesb"]
    cB = G["cB"]
    P.op("act", lambda e: e.activation(out=sq[:, 0:kcn, :], in_=src, func=AF.Square), reads=srcB, writes=sqB)
    ps, pb = P.psum()

    def f(e):
        for c in range(kcn):
            ins = e.matmul(ps[:], lhsT=onesb[:], rhs=sq[:, c, :], start=(c == 0), stop=(c == kcn - 1))
        return ins
    P.op("pe", f, reads=[sqB, cB], writes=pb)
    P.op("act", lambda e: e.activation(out=sd[:], in_=ps[:], func=AF.Sqrt, scale=1.0 / dim, bias=float(eps)), reads=[pb], writes=rsB)
    P.op("dve", lambda e: e.reciprocal(out=rstd[:], in_=sd[:]), reads=rsB, writes=rsB)


def rms_rstd(P, G, src, srcB, kcn, sq, sqB, sd, rstd, rsB, dim, eps):
    onesb = G["onesb"]
    cB = G["cB"]
    P.op("act", lambda e: e.activation(out=sq[:, 0:kcn, :], in_=src, func=AF.Square), reads=srcB, writes=sqB)
    ps, pb = P.psum()

    def f(e):
        for c in range(kcn):
            ins = e.matmul(ps[:], lhsT=onesb[:], rhs=sq[:, c, :], start=(c == 0), stop=(c == kcn - 1))
        return ins
    P.op("pe", f, reads=[sqB, cB], writes=pb)
    P.op("act", lambda e: e.activation(out=sd[:], in_=ps[:], func=AF.Sqrt, scale=1.0 / dim, bias=float(eps)), reads=[pb], writes=rsB)
    P.op("dve", lambda e: e.reciprocal(out=rstd[:], in_=sd[:]), reads=rsB, writes=rsB)


def rms_rstd(P, G, src, srcB, kcn, sq, sqB, sd, rstd, rsB, dim, eps):
    onesb = G["onesb"]
    cB = G["cB"]
    P.op("act", lambda e: e.activation(out=sq[:, 0:kcn, :], in_=src, func=AF.Square), reads=srcB, writes=sqB)
    ps, pb = P.psum()

    def f(e):
        for c in range(kcn):
            ins = e.matmul(ps[:], lhsT=onesb[:], rhs=sq[:, c, :], start=(c == 0), stop=(c == kcn - 1))
        return ins
    P.op("pe", f, reads=[sqB, cB], writes=pb)
    P.op("act", lambda e: e.activation(out=sd[:], in_=ps[:], func=AF.Sqrt, scale=1.0 / dim, bias=float(eps)), reads=[pb], writes=rsB)
    P.op("dve", lambda e: e.reciprocal(out=rstd[:], in_=sd[:]), reads=rsB, writes=rsB)


def rms_rstd(P, G, src, srcB, kcn, sq, sqB, sd, rstd, rsB, dim, eps):
    onesb = G["onesb"]
    cB = G["cB"]
    P.op("act", lambda e: e.activation(out=sq[:, 0:kcn, :], in_=src, func=AF.Square), reads=srcB, writes=sqB)
    ps, pb = P.psum()

    def f(e):
        for c in range(kcn):
            ins = e.matmul(ps[:], lhsT=onesb[:], rhs=sq[:, c, :], start=(c == 0), stop=(c == kcn - 1))
        return ins
    P.op("pe", f, reads=[sqB, cB], writes=pb)
    P.op("act", lambda e: e.activation(out=sd[:], in_=ps[:], func=AF.Sqrt, scale=1.0 / dim, bias=float(eps)), reads=[pb], writes=rsB)
    P.op("dve", lambda e: e.reciprocal(out=rstd[:], in_=sd[:]), reads=rsB, writes=rsB)


def rms_rstd(P, G, src, srcB, kcn, sq, sqB, sd, rstd, rsB, dim, eps):
    onesb = G["onesb"]
    cB = G["cB"]
    P.op("act", lambda e: e.activation(out=sq[:, 0:kcn, :], in_=src, func=AF.Square), reads=srcB, writes=sqB)
    ps, pb = P.psum()

    def f(e):
        for c in range(kcn):
            ins = e.matmul(ps[:], lhsT=onesb[:], rhs=sq[:, c, :], start=(c == 0), stop=(c == kcn - 1))
        return ins
    P.op("pe", f, reads=[sqB, cB], writes=pb)
    P.op("act", lambda e: e.activation(out=sd[:], in_=ps[:], func=AF.Sqrt, scale=1.0 / dim, bias=float(eps)), reads=[pb], writes=rsB)
    P.op("dve", lambda e: e.reciprocal(out=rstd[:], in_=sd[:]), reads=rsB, writes=rsB)


def layer(P, g, cfg, l, G):
    nc = P.nc
    TS = cfg["seqs"]
    NS = len(TS)
    pv, modv, cs, cB = G["pv"], G["modv"], G["cs"], G["cB"]
    pvB, modB = G["pvB"], G["modB"]
    ES = contextlib.ExitStack
    stop_after = cfg.get("stop_after", "")

    P.dma("sp", pv[:], G["pvec"][l], writes=pvB)
    with ES() as es:
        wsl = [sb(P, es, "wsl", [128, KC, 512], BF16) for _ in range(2)]
        wB = [Buf(), Buf()]
        modraw = sb(P, es, "modraw", [128, 96, NS], F32)
        mrB = Buf()
        ps, pb = P.psum()
        for gi in range(24):
            k = gi % 2
            P.dma("sp", wsl[k][:], G["wada"][l, gi], writes=wB[k])

            def f(e, gi=gi, k=k):
                for m in range(4):
                    j = gi * 4 + m
                    for kc in range(KC):
                        ins = e.matmul(ps[:, j * NS:(j + 1) * NS], lhsT=wsl[k][:, kc, m * 128:(m + 1) * 128], rhs=cs[:, kc, :], start=(kc == 0), stop=(kc == KC - 1))
                return ins
            P.op("pe", f, reads=[wB[k], cB], writes=pb)
        P.op("dve", lambda e: e.tensor_tensor(out=modraw[:], in0=ps[:, 0:96 * NS].rearrange("p (j s) -> p j s", s=NS),
                                              in1=pv[:, 64:160].unsqueeze(2).to_broadcast([128, 96, NS]), op=ALU.add), reads=[pb, pvB], writes=mrB)
        for s in range(NS):
            P.op("dve", lambda e, s=s: e.tensor_copy(out=modv[:, s, 0, :], in_=modraw[:, 0:16, s]), reads=mrB, writes=modB)
            P.op("dve", lambda e, s=s: e.scalar_tensor_tensor(out=modv[:, s, 1, :], in0=modraw[:, 16:32, s], scalar=1.0, in1=pv[:, 0:16], op0=ALU.add, op1=ALU.mult), reads=[mrB, pvB], writes=modB)
            P.op("dve", lambda e, s=s: e.tensor_tensor(out=modv[:, s, 2, :], in0=modraw[:, 32:48, s], in1=pv[:, 16:32], op=ALU.mult), reads=[mrB, pvB], writes=modB)
            P.op("dve", lambda e, s=s: e.tensor_copy(out=modv[:, s, 3, :], in_=modraw[:, 48:64, s]), reads=mrB, writes=modB)
            P.op("dve", lambda e, s=s: e.scalar_tensor_tensor(out=modv[:, s, 4, :], in0=modraw[:, 64:80, s], scalar=1.0, in1=pv[:, 32:48], op0=ALU.add, op1=ALU.mult), reads=[mrB, pvB], writes=modB)
            P.op("dve", lambda e, s=s: e.tensor_tensor(out=modv[:, s, 5, :], in0=modraw[:, 80:96, s], in1=pv[:, 48:64], op=ALU.mult), reads=[mrB, pvB], writes=modB)
        P.barrier()

    for s in range(NS):
        T = TS[s]
        ntile = T // NT
        xr = G["xres"][s].rearrange("(c p) t -> p c t", p=128)
        hf = G["hfm"][s].rearrange("(c p) t -> p c t", p=128)
        with ES() as es:
            xt = [sb(P, es, "xt", [128, KC, NT], F32) for _ in range(2)]
            xB = [Buf(), Buf()]
            sq = sb(P, es, "sq", [128, KC, NT], BF16)
            sqB = Buf()
            sd = sb(P, es, "sd", [128, NT], F32)
            rstd = sb(P, es, "rstd", [128, NT], F32)
            rsB = Buf()
            tmp = [sb(P, es, "tmp", [128, NT], F32) for _ in range(2)]
            tB = [Buf(), Buf()]
            h = [sb(P, es, "h", [128, KC, NT], BF16) for _ in range(2)]
            hB = [Buf(), Buf()]
            for ti in range(ntile):
                k = ti % 2
                P.dma("sp", xt[k][:], xr[:, :, ti * NT:(ti + 1) * NT], writes=xB[k])
                rms_rstd(P, G, xt[k][:], xB[k], KC, sq, sqB, sd, rstd, rsB, D, EPS)
                for c in range(KC):
                    j = c % 2
                    P.op("dve", lambda e, c=c, j=j, k=k: e.scalar_tensor_tensor(out=tmp[j][:], in0=xt[k][:, c, :], scalar=modv[:, s, 1, c:c + 1], in1=rstd[:], op0=ALU.mult, op1=ALU.mult),
                         reads=[xB[k], rsB, modB], writes=tB[j])
                    P.op("act", lambda e, c=c, j=j, k=k: e.activation(out=h[k][:, c, :], in_=tmp[j][:], func=AF.Identity, bias=modv[:, s, 0, c:c + 1], scale=1.0),
                         reads=[tB[j], modB], writes=hB[k])
                P.dma("pool", hf[:, :, 1 + ti * NT:1 + (ti + 1) * NT], h[k][:], reads=hB[k])
            P.barrier()
        if stop_after == "A1":
            continue

        with ES() as es:
            w32 = sb(P, es, "w32", [128, KC, NLORA], F32)
            p32 = sb(P, es, "p32", [128, KC, NLORA], F32)
            n32 = sb(P, es, "n32", [128, KC, NLORA], F32)
            Wc = sb(P, es, "Wc", [128, KC, NLORA], BF16)
            Wp = sb(P, es, "Wp", [128, KC, NLORA], BF16)
            Wn = sb(P, es, "Wn", [128, KC, NLORA], BF16)
            lB = Buf()
            P.dma("sp", w32[:], G["lora_dn"][l], writes=lB)
            P.dma("sp", p32[:], G["lora_mp"][l], writes=lB)
            P.dma("sp", n32[:], G["lora_mn"][l], writes=lB)
            P.op("dve", lambda e: e.tensor_tensor(out=p32[:], in0=p32[:], in1=w32[:], op=ALU.mult), reads=lB, writes=lB)
            P.op("dve", lambda e: e.tensor_tensor(out=n32[:], in0=n32[:], in1=w32[:], op=ALU.mult), reads=lB, writes=lB)
            P.op("dve", lambda e: e.tensor_tensor(out=w32[:], in0=w32[:], in1=p32[:], op=ALU.subtract), reads=lB, writes=lB)
            P.op("dve", lambda e: e.tensor_tensor(out=w32[:], in0=w32[:], in1=n32[:], op=ALU.subtract), reads=lB, writes=lB)
            P.op("dve", lambda e: e.tensor_copy(out=Wc[:], in_=w32[:]), reads=lB, writes=lB)
            P.op("dve", lambda e: e.tensor_copy(out=Wp[:], in_=p32[:]), reads=lB, writes=lB)
            P.op("dve", lambda e: e.tensor_copy(out=Wn[:], in_=n32[:]), reads=lB, writes=lB)
            hext = [sb(P, es, "hext", [128, KC, NT + 2], BF16) for _ in range(2)]
            hxB = [Buf(), Buf()]
            wsl = [sb(P, es, "wsl", [128, KC, 512], BF16) for _ in range(2)]
            wB = [Buf(), Buf()]
            ot = [sb(P, es, "ot", [128, 4, NT], BF16) for _ in range(2)]
            otB = [Buf(), Buf()]
            of = [sb(P, es, "of", [128, 4, NT], F32) for _ in range(2)]
            ofB = [Buf(), Buf()]
            vt = [sb(P, es, "vt", [128, 512], BF16) for _ in range(2)]
            vtB = [Buf(), Buf()]
            lo = [sb(P, es, "lo", [128, NT], BF16) for _ in range(3)]
            loB = [Buf(), Buf(), Buf()]
            wcount = 0
            for ti in range(ntile):
                k = ti % 2
                tsl = slice(ti * NT, (ti + 1) * NT)
                P.dma("sp", hext[k][:], hf[:, :, ti * NT:ti * NT + NT + 2], writes=hxB[k])
                for og, (c0, c1) in enumerate(((0, 128), (128, 256), (256, 352))):
                    M = c1 - c0
                    ps, pb = P.psum()

                    def f(e, ps=ps, c0=c0, c1=c1, M=M, k=k):
                        n = 0
                        for W, off in ((Wc, 1), (Wp, 0), (Wn, 2)):
                            for kc in range(KC):
                                ins = e.matmul(ps[0:M, :], lhsT=W[:, kc, c0:c1], rhs=hext[k][:, kc, off:off + NT], start=(n == 0), stop=(n == 3 * KC - 1))
                                n += 1
                        return ins
                    P.op("pe", f, reads=[lB, hxB[k]], writes=pb)
                    if og == 0:
                        P.op("act", lambda e, ps=ps: e.activation(out=lo[0][:], in_=ps[:], func=AF.Tanh), reads=pb, writes=loB[0])
                        P.dma("pool", G["lw_d"][s][:, tsl], lo[0][:], reads=loB[0])
                    elif og == 1:
                        P.op("dve", lambda e, ps=ps: e.tensor_copy(out=lo[1][:], in_=ps[:]), reads=pb, writes=loB[1])
                        P.dma("pool", G["la_d"][s][:, tsl], lo[1][:], reads=loB[1])
                    else:
                        P.op("act", lambda e, ps=ps: e.activation(out=lo[2][0:64, :], in_=ps[0:64, :], func=AF.Sigmoid), reads=pb, writes=loB[2])
                        P.op("dve", lambda e, ps=ps: e.tensor_copy(out=lo[2][64:96, :], in_=ps[64:96, :]), reads=pb, writes=loB[2])
                        P.dma("pool", G["lg_d"][s][:, tsl], lo[2][0:64, :], reads=loB[2])
                        P.dma("pool", G["lv_d"][s][:, tsl], lo[2][64:96, :], reads=loB[2])
                for gi in range(12):
                    wk = wcount % 2
                    wcount += 1
                    P.dma("sp", wsl[wk][:], G["win"][l, gi], writes=wB[wk])
                    if gi in (4, 5):
                        for tsub in range(4):
                            ps, pb = P.psum()

                            def f(e, ps=ps, tsub=tsub, wk=wk, k=k):
                                for kc in range(KC):
                                    ins = e.matmul(ps[:], lhsT=hext[k][:, kc, 1 + tsub * 128:1 + (tsub + 1) * 128], rhs=wsl[wk][:, kc, :], start=(kc == 0), stop=(kc == KC - 1))
                                return ins
                            P.op("pe", f, reads=[wB[wk], hxB[k]], writes=pb)
                            j = tsub % 2
                            P.op("act", lambda e, ps=ps, j=j: e.copy(out=vt[j][:], in_=ps[:]), reads=pb, writes=vtB[j])
                            r0 = ti * NT + tsub * 128
                            P.dma("pool", G["vtm"][s][r0:r0 + 128, (gi - 4) * 512:(gi - 3) * 512], vt[j][:], reads=vtB[j])
                    else:
                        j = gi % 2
                        for m in range(4):
                            ps, pb = P.psum()

                            def f(e, ps=ps, m=m, wk=wk, k=k):
                                for kc in range(KC):
                                    ins = e.matmul(ps[:], lhsT=wsl[wk][:, kc, m * 128:(m + 1) * 128], rhs=hext[k][:, kc, 1:NT + 1], start=(kc == 0), stop=(kc == KC - 1))
                                return ins
                            P.op("pe", f, reads=[wB[wk], hxB[k]], writes=pb)
                            dst, dB = (ot[j], otB[j]) if gi < 4 else (of[j], ofB[j])
                            if P.ev_eng() == "act":
                                P.op("act", lambda e, ps=ps, m=m, dst=dst: e.copy(out=dst[:, m, :], in_=ps[:]), reads=pb, writes=dB)
                            else:
                                P.op("dve", lambda e, ps=ps, m=m, dst=dst: e.tensor_copy(out=dst[:, m, :], in_=ps[:]), reads=pb, writes=dB)
                        if gi < 4:
                            dd = (G["qfm"] if gi < 2 else G["kfm"])[s].rearrange("(c p) t -> p c t", p=128)
                            P.dma("pool", dd[:, (gi % 2) * 4:(gi % 2) * 4 + 4, tsl], ot[j][:], reads=otB[j])
                        else:
                            dd = G["rkv"][s].rearrange("(c p) t -> p c t", p=128)
                            P.dma("pool", dd[:, (gi - 6) * 4:(gi - 6) * 4 + 4, 1 + ti * NT:1 + (ti + 1) * NT], of[j][:], reads=ofB[j])
            P.barrier()
        if stop_after == "A2":
            continue
        attention(P, cfg, l, s, G)
        if stop_after == "B1":
            continue
        for d in range(2):
            scan_pass(P, cfg, l, s, d, G)
        if stop_after == "B2":
            continue
        scan_post(P, cfg, l, s, G)
        if stop_after == "B3":
            continue
        ffn_phase(P, cfg, l, s, G)


def ffn_phase(P, cfg, l, s, G):
    ES = contextlib.ExitStack
    TS = cfg["seqs"]
    T = TS[s]
    ntile = T // NT
    pv, modv, cB, pvB, modB = G["pv"], G["modv"], G["cB"], G["pvB"], G["modB"]
    xr = G["xres"][s].rearrange("(c p) t -> p c t", p=128)
    mx = G["mixin"][s].rearrange("(c p) t -> p c t", p=128)
    with ES() as es:
        xt = sb(P, es, "xt", [128, KC, NT], F32)
        xB = Buf()
        mf = sb(P, es, "mf", [128, KC, NT], F32)
        mfB = Buf()
        hb = sb(P, es, "hb", [128, KC, NT], BF16)
        hbB = Buf()
        hid = sb(P, es, "hid", [128, 32, NT], BF16)
        hidB = Buf()
        sq = sb(P, es, "sq", [128, KC, NT], BF16)
        sqB = Buf()
        sd = sb(P, es, "sd", [128, NT], F32)
        rstd = sb(P, es, "rstd", [128, NT], F32)
        rsB = Buf()
        tmp = [sb(P, es, "tmp", [128, NT], F32) for _ in range(2)]
        tB = [Buf(), Buf()]
        wsl = [sb(P, es, "wsl", [128, KC * 512], BF16) for _ in range(2)]
        wB = [Buf(), Buf()]
        gs = sb(P, es, "gs", [128, 8], F32)
        wc = 0
        for ti in range(ntile):
            tsl = slice(ti * NT, (ti + 1) * NT)
            P.dma("sp", xt[:], xr[:, :, tsl], writes=xB)
            P.dma("sp", hb[:], mx[:, :, tsl], writes=hbB)
            rms_rstd(P, G, hb[:, 0:8, :], hbB, 8, sq, sqB, sd, rstd, rsB, 1024, EPS)
            for c in range(8):
                P.op("dve", lambda e, c=c: e.scalar_tensor_tensor(out=hb[:, c, :], in0=hb[:, c, :], scalar=pv[:, 160 + c:161 + c], in1=rstd[:], op0=ALU.mult, op1=ALU.mult),
                     reads=[rsB, pvB], writes=hbB)
            for gi in range(4):
                wk = wc % 2
                wc += 1
                wv = wsl[wk][:].rearrange("p (k n) -> p k n", n=512)
                P.dma("sp", wv, G["wout"][l, gi], writes=wB[wk])
                for m in range(4):
                    ps, pb = P.psum()

                    def f(e, ps=ps, m=m, wv=wv):
                        for kc in range(KC):
                            ins = e.matmul(ps[:], lhsT=wv[:, kc, m * 128:(m + 1) * 128], rhs=hb[:, kc, :], start=(kc == 0), stop=(kc == KC - 1))
                        return ins
                    P.op("pe", f, reads=[wB[wk], hbB], writes=pb)
                    c = gi * 4 + m
                    if P.ev_eng() == "act":
                        P.op("act", lambda e, ps=ps, c=c: e.copy(out=mf[:, c, :], in_=ps[:]), reads=pb, writes=mfB)
                    else:
                        P.op("dve", lambda e, ps=ps, c=c: e.tensor_copy(out=mf[:, c, :], in_=ps[:]), reads=pb, writes=mfB)
            rms_rstd(P, G, mf[:], mfB, KC, sq, sqB, sd, rstd, rsB, D, EPS)
            for c in range(KC):
                j = c % 2
                P.op("dve", lambda e, c=c, j=j: e.scalar_tensor_tensor(out=tmp[j][:], in0=mf[:, c, :], scalar=modv[:, s, 2, c:c + 1], in1=rstd[:], op0=ALU.mult, op1=ALU.mult),
                     reads=[mfB, rsB, modB], writes=tB[j])
                P.op("dve", lambda e, c=c, j=j: e.tensor_tensor(out=xt[:, c, :], in0=xt[:, c, :], in1=tmp[j][:], op=ALU.add), reads=[tB[j]], writes=xB)
            rms_rstd(P, G, xt[:], xB, KC, sq, sqB, sd, rstd, rsB, D, EPS)
            for c in range(KC):
                j = c % 2
                P.op("dve", lambda e, c=c, j=j: e.scalar_tensor_tensor(out=tmp[j][:], in0=xt[:, c, :], scalar=modv[:, s, 4, c:c + 1], in1=rstd[:], op0=ALU.mult, op1=ALU.mult),
                     reads=[xB, rsB, modB], writes=tB[j])
                P.op("act", lambda e, c=c, j=j: e.activation(out=hb[:, c, :], in_=tmp[j][:], func=AF.Identity, bias=modv[:, s, 3, c:c + 1], scale=1.0),
                     reads=[tB[j], modB], writes=hbB)
            for half in range(2):
                for g8 in range(8):
                    gi = half * 8 + g8
                    wk = wc % 2
                    wc += 1
                    wv = wsl[wk][:].rearrange("p (k n) -> p k n", n=512)
                    P.dma("sp", wv, G["wf1"][l, gi], writes=wB[wk])
                    for m in range(4):
                        ps, pb = P.psum()

                        def f(e, ps=ps, m=m, wv=wv):
                            for kc in range(KC):
                                ins = e.matmul(ps[:], lhsT=wv[:, kc, m * 128:(m + 1) * 128], rhs=hb[:, kc, :], start=(kc == 0), stop=(kc == KC - 1))
                            return ins
                        P.op("pe", f, reads=[wB[wk], hbB], writes=pb)
                        j = m % 2
                        c = g8 * 4 + m
                        P.op("act", lambda e, ps=ps, j=j: e.activation(out=tmp[j][:], in_=ps[:], func=AF.Relu), reads=pb, writes=tB[j])
                        P.op("dve", lambda e, c=c, j=j: e.tensor_tensor(out=hid[:, c, :], in0=tmp[j][:], in1=tmp[j][:], op=ALU.mult), reads=tB[j], writes=hidB)
                for mg2 in range(8):
                    wk = wc % 2
                    wc += 1
                    wv = wsl[wk][:].rearrange("p (g k n) -> p g k n", g=2, n=128)
                    P.dma("sp", wv, G["wf2"][l, half, 2 * mg2:2 * mg2 + 2].rearrange("g p k n -> p g k n"), writes=wB[wk])
                    for gg in range(2):
                        mg = 2 * mg2 + gg
                        ps, pb = P.psum()

                        def f(e, ps=ps, gg=gg, wv=wv):
                            for kc in range(32):
                                ins = e.matmul(ps[:], lhsT=wv[:, gg, kc, :], rhs=hid[:, kc, :], start=(kc == 0), stop=(kc == 31))
                            return ins
                        P.op("pe", f, reads=[wB[wk], hidB], writes=pb)
                        if half == 0:
                            P.op("act", lambda e, ps=ps, mg=mg: e.copy(out=mf[:, mg, :], in_=ps[:]), reads=pb, writes=mfB)
                        else:
                            P.op("dve", lambda e, ps=ps, mg=mg: e.tensor_tensor(out=mf[:, mg, :], in0=ps[:], in1=mf[:, mg, :], op=ALU.add), reads=pb, writes=mfB)
            rms_rstd(P, G, mf[:], mfB, KC, sq, sqB, sd, rstd, rsB, D, EPS)
            for c in range(KC):
                j = c % 2
                P.op("dve", lambda e, c=c, j=j: e.scalar_tensor_tensor(out=tmp[j][:], in0=mf[:, c, :], scalar=modv[:, s, 5, c:c + 1], in1=rstd[:], op0=ALU.mult, op1=ALU.mult),
                     reads=[mfB, rsB, modB], writes=tB[j])
                P.op("dve", lambda e, c=c, j=j: e.tensor_tensor(out=xt[:, c, :], in0=xt[:, c, :], in1=tmp[j][:], op=ALU.add), reads=[tB[j]], writes=xB)
            P.dma("pool", xr[:, :, tsl], xt[:], reads=xB)
        P.barrier()


def attention(P, cfg, l, s, G):
    ES = contextlib.ExitStack
    nc = P.nc
    T = cfg["seqs"][s]
    rows = T // GW
    nj = T // 128
    cB = G["cB"]
    identb = G["identb"]
    SCALE = 128 ** -0.5
    with ES() as es:
        master = sb(P, es, "master", [128, 8, 15 * 64], F32)
        mB = Buf()
        P.op("pool", lambda e: e.memset(master[:], NEG), writes=mB)
        m4 = master[:].rearrange("p h (j c) -> p h j c", c=64)
        with nc.allow_non_contiguous_dma(reason="rpb window"):
            for q in range(64):
                c0 = min(max(q - 8, 0), 48)
                a0 = c0 - q + 15
                for half in range(2):
                    p = half * 64 + q
                    P.dma("sp", m4[p:p + 1, :, :, c0:c0 + 16], G["rpb"][l:l + 1, :, :, a0:a0 + 16], writes=mB)
        Kh = sb(P, es, "Kh", [128, T], BF16)
        Qh = sb(P, es, "Qh", [128, T], BF16)
        V0 = sb(P, es, "V0", [128, nj, 128], BF16)
        V1 = sb(P, es, "V1", [128, nj, 128], BF16)
        atth = sb(P, es, "atth", [128, T], BF16)
        hdB = Buf()
        atB = Buf()
        S = [sb(P, es, "S", [128, 512], F32) for _ in range(4)]
        SB_ = [Buf() for _ in range(4)]
        Pm = [sb(P, es, "Pm", [128, 512], BF16) for _ in range(4)]
        PB_ = [Buf() for _ in range(4)]
        PT = [sb(P, es, "PT", [128, 4, 128], BF16) for _ in range(4)]
        PTB = [Buf() for _ in range(4)]
        st = [sb(P, es, "st", [128, 4], F32) for _ in range(4)]
        stB = [Buf() for _ in range(4)]
        for hd in range(8):
            P.dma("sp", Kh[:], G["kfm"][s][hd * 128:(hd + 1) * 128, :], writes=hdB)
            P.dma("sp", Qh[:], G["qfm"][s][hd * 128:(hd + 1) * 128, :], writes=hdB)
            P.dma("sp", V0[:], G["vtm"][s][0:T, hd * 128:(hd + 1) * 128].rearrange("(j p) d -> p j d", p=128), writes=hdB)
            P.dma("sp", V1[:, 0:nj - 1, :], G["vtm"][s][64:T - 64, hd * 128:(hd + 1) * 128].rearrange("(j p) d -> p j d", p=128), writes=hdB)
            for pi in range(rows // 2):
                k = pi % 4
                rr_ = [2 * pi, 2 * pi + 1]
                rs = [min(max(r - 4, 0), rows - 8) for r in rr_]
                oo = [rs[i] - rr_[i] for i in range(2)]
                ps, pb = P.psum()

                def f(e, ps=ps, rr_=rr_, rs=rs):
                    for i in range(2):
                        ins = e.matmul(ps[i * 64:(i + 1) * 64, :], lhsT=Qh[:, rr_[i] * 64:(rr_[i] + 1) * 64], rhs=Kh[:, rs[i] * 64:rs[i] * 64 + 512], start=True, stop=True)
                    return ins
                P.op("pe", f, reads=hdB, writes=pb)
                if oo[0] == oo[1]:
                    P.op("dve", lambda e, ps=ps, k=k, o=oo[0]: e.scalar_tensor_tensor(out=S[k][:], in0=ps[:], scalar=SCALE, in1=master[:, hd, (o + 7) * 64:(o + 15) * 64], op0=ALU.mult, op1=ALU.add),
                         reads=[pb, mB], writes=SB_[k])
                else:
                    for i in range(2):
                        P.op("dve", lambda e, ps=ps, k=k, o=oo[i], i=i: e.scalar_tensor_tensor(out=S[k][i * 64:(i + 1) * 64, :], in0=ps[i * 64:(i + 1) * 64, :], scalar=SCALE,
                                                                                         in1=master[i * 64:(i + 1) * 64, hd, (o + 7) * 64:(o + 15) * 64], op0=ALU.mult, op1=ALU.add),
                             reads=[pb, mB], writes=SB_[k])
                P.op("dve", lambda e, k=k: e.reduce_max(out=st[k][:, 0:1], in_=S[k][:], axis=AX.X), reads=SB_[k], writes=stB[k])
                P.op("dve", lambda e, k=k: e.tensor_scalar(out=st[k][:, 1:2], in0=st[k][:, 0:1], scalar1=-1.0, scalar2=None, op0=ALU.mult), reads=stB[k], writes=stB[k])
                P.op("act", lambda e, k=k: e.activation(out=Pm[k][:], in_=S[k][:], func=AF.Exp, bias=st[k][:, 1:2], scale=1.0, accum_out=st[k][:, 2:3]), reads=[SB_[k], stB[k]], writes=[PB_[k], stB[k]])
                P.op("dve", lambda e, k=k: e.reciprocal(out=st[k][:, 3:4], in_=st[k][:, 2:3]), reads=stB[k], writes=stB[k])
                P.op("dve", lambda e, k=k: e.tensor_scalar(out=Pm[k][:], in0=Pm[k][:], scalar1=st[k][:, 3:4], scalar2=None, op0=ALU.mult), reads=stB[k], writes=PB_[k])
                ps2, pb2 = P.psum()
                pst = ps2[:].bitcast(BF16)

                def f2(e, pst=pst, k=k):
                    for kc in range(4):
                        ins = e.transpose(out=pst[:, kc * 128:(kc + 1) * 128], in_=Pm[k][:, kc * 128:(kc + 1) * 128], identity=identb[:])
                    return ins
                P.op("pe", f2, reads=[PB_[k], cB], writes=pb2)
                P.op("act", lambda e, pst=pst, k=k: e.copy(out=PT[k][:].rearrange("p a b -> p (a b)"), in_=pst[:, 0:512]), reads=pb2, writes=PTB[k])
                ps3, pb3 = P.psum()

                def f3(e, ps3=ps3, k=k, rs=rs):
                    for i in range(2):
                        Vx = V0 if rs[i] % 2 == 0 else V1
                        j0 = rs[i] // 2 if rs[i] % 2 == 0 else (rs[i] - 1) // 2
                        for kc in range(4):
                            ins = e.matmul(ps3[:, i * 64:(i + 1) * 64], lhsT=Vx[:, j0 + kc, :], rhs=PT[k][:, kc, i * 64:(i + 1) * 64], start=(kc == 0), stop=(kc == 3))
                    return ins
                P.op("pe", f3, reads=[PTB[k], hdB], writes=pb3)
                P.op("dve", lambda e, ps3=ps3, pi=pi: e.tensor_copy(out=atth[:, pi * 128:(pi + 1) * 128], in_=ps3[:, 0:128]), reads=pb3, writes=atB)
            P.dma("pool", G["mixin"][s][hd * 128:(hd + 1) * 128, :], atth[:], reads=atB)
        P.barrier()


def scan_pass(P, cfg, l, s, d, G):
    ES = contextlib.ExitStack
    nc = P.nc
    T = cfg["seqs"][s]
    ntile = T // NT
    pv, pvB, cB = G["pv"], G["pvB"], G["cB"]
    identb, blkb, maskb, masknb, keep = G["identb"], G["blkb"], G["maskb"], G["masknb"], G["keep"]
    rk3 = G["rkv"][s]
    with ES() as es:
        st32 = sb(P, es, "st32", [65, 1024], F32)
        w2b = sb(P, es, "w2b", [65, 1024], BF16)
        a2b = sb(P, es, "a2b", [65, 1024], BF16)
        v2b = sb(P, es, "v2b", [33, 1024], BF16)
        uB = Buf()
        sB_ = Buf()
        for src, dst, n in ((G["w2aug"][l, d], w2b, 65), (G["a2aug"][l, d], a2b, 65), (G["v2aug"][l], v2b, 33)):
            P.dma("sp", st32[0:n, :], src, writes=sB_)
            P.op("dve", lambda e, dst=dst, n=n: e.tensor_copy(out=dst[0:n, :], in_=st32[0:n, :]), reads=sB_, writes=[uB, sB_])
        coef0 = sb(P, es, "coef0", [128, 3, 8], F32)
        for i in range(3):
            a = 168 + (i * 2) * 8
            P.op("dve", lambda e, i=i, a=a: e.tensor_tensor(out=coef0[:, i, :], in0=pv[:, a:a + 8], in1=pv[:, a + 8:a + 16], op=ALU.add), reads=pvB, writes=uB)
            P.op("dve", lambda e, i=i: e.tensor_scalar(out=coef0[:, i, :], in0=coef0[:, i, :], scalar1=-1.0, scalar2=1.0, op0=ALU.mult, op1=ALU.add), reads=uB, writes=uB)
        lwa = [sb(P, es, "lwa", [65, NT], BF16) for _ in range(2)]
        laa = [sb(P, es, "laa", [65, NT], BF16) for _ in range(2)]
        lva = [sb(P, es, "lva", [33, NT], BF16) for _ in range(2)]
        ldB = [Buf(), Buf()]
        for k in range(2):
            for t_ in (lwa[k], laa[k], lva[k]):
                P.op("dve", lambda e, t_=t_: e.memset(t_[:], 1.0), writes=ldB[k])
        raw = sb(P, es, "raw", [128, 3, NT + 2], F32)
        rawB = Buf()
        f32n = ["rr", "kv", "vv", "sg", "vf", "lw", "Lf", "Linc", "Lexc", "Wt", "Wex", "Winv", "asg", "kx", "t0", "kkn", "kd", "bb"]
        F = {n: sb(P, es, n, [128, NT], F32) for n in f32n}
        FB = {n: Buf() for n in f32n}
        sqk = sb(P, es, "sqk", [128, NT], BF16)
        sqkB = Buf()
        AR = [sb(P, es, "AR", [128, 8, 256], BF16) for _ in range(4)]
        Bbd = [sb(P, es, "Bbd", [128, 8, 128], BF16) for _ in range(4)]
        Kbd = [sb(P, es, "Kbd", [128, 8, 128], BF16) for _ in range(4)]
        Vbd = [sb(P, es, "Vbd", [128, 8, 128], BF16) for _ in range(4)]
        bon = [sb(P, es, "bon", [128, NT], F32) for _ in range(4)]
        WC = [sb(P, es, "WC", [128, 8], F32) for _ in range(4)]
        yt = [sb(P, es, "yt", [128, NT], F32) for _ in range(4)]
        opB = [Buf() for _ in range(4)]
        ytB = [Buf() for _ in range(4)]
        TT = [sb(P, es, "TT", [128, 512], BF16) for _ in range(4)]
        AM = [sb(P, es, "AM", [128, 512], BF16) for _ in range(4)]
        Nn = [sb(P, es, "Nn", [128, 128], BF16) for _ in range(4)]
        X = [[sb(P, es, "X", [128, 256], BF16) for _ in range(2)] for _ in range(4)]
        QQ = [[sb(P, es, "QQ", [128, 256], BF16) for _ in range(2)] for _ in range(4)]
        GY = [sb(P, es, "GY", [128, 256], BF16) for _ in range(4)]
        Xf = [[sb(P, es, "Xf", [128, 256], F32) for _ in range(2)] for _ in range(4)]
        QQf = [[sb(P, es, "QQf", [128, 256], F32) for _ in range(2)] for _ in range(4)]
        Pf = [sb(P, es, "Pf", [128, 256], F32) for _ in range(4)]
        maskPf, maskNf = G["maskPf"], G["maskNf"]
        uT = [{n: Buf() for n in ("TT", "AM", "Nn", "X0", "X1", "Q0", "Q1", "GY", "Pf", "G")} for _ in range(4)]
        Mst = [[sb(P, es, "M", [128, 128], BF16) for _ in range(2)] for _ in range(8)]
        MB = [[Buf(), Buf()] for _ in range(8)]
        mcur = [0] * 8
        for hq in range(4):
            for t_ in (AR[hq], Bbd[hq], Kbd[hq], Vbd[hq]):
                P.op("pool", lambda e, t_=t_: e.memset(t_[:], 0.0), writes=opB[hq])
        for hp in range(8):
            P.op("pool", lambda e, hp=hp: e.memset(Mst[hp][0][:], 0.0), writes=MB[hp][0])

        def v3(ap):
            return ap.rearrange("p (c t) -> p c t", t=64)

        tiles = list(range(ntile)) if d == 0 else list(range(ntile - 1, -1, -1))
        chunks = list(range(8)) if d == 0 else list(range(7, -1, -1))
        for tn, ti in enumerate(tiles):
            tsl = slice(ti * NT, (ti + 1) * NT)
            lk = tn % 2
            P.dma("sp", lwa[lk][0:64, :], G["lw_d"][s][d * 64:(d + 1) * 64, tsl], writes=ldB[lk])
            P.dma("sp", laa[lk][0:64, :], G["la_d"][s][d * 64:(d + 1) * 64, tsl], writes=ldB[lk])
            P.dma("sp", lva[lk][0:32, :], G["lv_d"][s][:, tsl], writes=ldB[lk])
            for hpg in range(2):
                for hq in range(4):
                    hp = hpg * 4 + hq
                    hs = slice(hp * 128, (hp + 1) * 128)
                    P.dma("sp", raw[:], rk3.rearrange("(i c p) t -> p i c t", i=3, p=128)[:, :, hp, ti * NT:ti * NT + NT + 2], writes=rawB)
                    for i, nm in enumerate(("rr", "kv", "vv")):
                        o = F[nm]
                        a = 168 + (i * 2) * 8 + hp
                        P.op("dve", lambda e, i=i, o=o: e.tensor_scalar(out=o[:], in0=raw[:, i, 1:NT + 1], scalar1=coef0[:, i, hp:hp + 1], scalar2=None, op0=ALU.mult), reads=[rawB, uB], writes=FB[nm])
                        P.op("dve", lambda e, i=i, o=o, a=a: e.scalar_tensor_tensor(out=o[:], in0=raw[:, i, 0:NT], scalar=pv[:, a:a + 1], in1=o[:], op0=ALU.mult, op1=ALU.add), reads=[rawB, pvB], writes=FB[nm])
                        P.op("dve", lambda e, i=i, o=o, a=a: e.scalar_tensor_tensor(out=o[:], in0=raw[:, i, 2:NT + 2], scalar=pv[:, a + 8:a + 9], in1=o[:], op0=ALU.mult, op1=ALU.add), reads=[rawB, pvB], writes=FB[nm])
                    if l > 0:
                        ps, pb = P.psum()
                        P.op("pe", lambda e, ps=ps: e.matmul(ps[:], lhsT=v2b[:, hs], rhs=lva[lk][:], start=True, stop=True), reads=[uB, ldB[lk]], writes=pb)
                        P.op("act", lambda e, ps=ps: e.activation(out=F["sg"][:], in_=ps[:], func=AF.Sigmoid), reads=pb, writes=FB["sg"])
                        P.dma("sp", F["vf"][:], G["vfirst"][s][hs, tsl], writes=FB["vf"])
                        P.op("dve", lambda e: e.tensor_tensor(out=F["vf"][:], in0=F["vf"][:], in1=F["vv"][:], op=ALU.subtract), reads=FB["vv"], writes=FB["vf"])
                        P.op("dve", lambda e: e.tensor_tensor(out=F["vf"][:], in0=F["vf"][:], in1=F["sg"][:], op=ALU.mult), reads=FB["sg"], writes=FB["vf"])
                        P.op("dve", lambda e: e.tensor_tensor(out=F["vv"][:], in0=F["vv"][:], in1=F["vf"][:], op=ALU.add), reads=FB["vf"], writes=FB["vv"])
                    elif d == 0:
                        P.dma("pool", G["vfirst"][s][hs, tsl], F["vv"][:], reads=FB["vv"])
                    ps, pb = P.psum()
                    P.op("pe", lambda e, ps=ps: e.matmul(ps[:], lhsT=w2b[:, hs], rhs=lwa[lk][:], start=True, stop=True), reads=[uB, ldB[lk]], writes=pb)
                    P.op("act", lambda e, ps=ps: e.activation(out=F["lw"][:], in_=ps[:], func=AF.Sigmoid), reads=pb, writes=FB["lw"])
                    P.op("dve", lambda e: e.tensor_scalar(out=F["lw"][:], in0=F["lw"][:], scalar1=-0.6065306597126334, scalar2=None, op0=ALU.mult), writes=FB["lw"])
                    P.op("dve", lambda e: e.tensor_tensor_scan(out=F["Lf"][:], data0=keep[:], data1=F["lw"][:], initial=0.0, op0=ALU.mult, op1=ALU.add), reads=[FB["lw"], cB], writes=FB["Lf"])
                    tot = v3(F["Lf"][:])[:, :, 63]
                    if d == 0:
                        Linc = F["Lf"]
                        LiB = FB["Lf"]
                        P.op("dve", lambda e: e.tensor_tensor(out=F["Lexc"][:], in0=F["Lf"][:], in1=F["lw"][:], op=ALU.subtract), reads=[FB["Lf"], FB["lw"]], writes=FB["Lexc"])
                    else:
                        Linc = F["Linc"]
                        LiB = FB["Linc"]
                        P.op("dve", lambda e: e.tensor_tensor(out=v3(F["Lexc"][:]), in0=tot.unsqueeze(2).to_broadcast([128, 8, 64]), in1=v3(F["Lf"][:]), op=ALU.subtract), reads=[FB["Lf"]], writes=FB["Lexc"])
                        P.op("dve", lambda e: e.tensor_tensor(out=F["Linc"][:], in0=F["Lexc"][:], in1=F["lw"][:], op=ALU.add), reads=[FB["Lexc"], FB["lw"]], writes=FB["Linc"])
                    P.op("act", lambda e: e.activation(out=F["Wt"][:], in_=Linc[:], func=AF.Exp), reads=LiB, writes=FB["Wt"])
                    P.op("act", lambda e: e.activation(out=F["Wex"][:], in_=F["Lexc"][:], func=AF.Exp), reads=FB["Lexc"], writes=FB["Wex"])
                    P.op("act", lambda e: e.activation(out=F["Winv"][:], in_=Linc[:], func=AF.Exp, scale=-1.0), reads=LiB, writes=FB["Winv"])
                    P.op("act", lambda e: e.activation(out=WC[hq][:], in_=tot, func=AF.Exp), reads=FB["Lf"], writes=opB[hq])
                    ps, pb = P.psum()
                    P.op("pe", lambda e, ps=ps: e.matmul(ps[:], lhsT=a2b[:, hs], rhs=laa[lk][:], start=True, stop=True), reads=[uB, ldB[lk]], writes=pb)
                    P.op("act", lambda e, ps=ps: e.activation(out=F["asg"][:], in_=ps[:], func=AF.Sigmoid), reads=pb, writes=FB["asg"])
                    P.op("dve", lambda e: e.tensor_scalar(out=F["kx"][:], in0=F["kv"][:], scalar1=pv[:, 216 + hp:217 + hp], scalar2=None, op0=ALU.mult), reads=[FB["kv"], pvB], writes=FB["kx"])
                    P.op("dve", lambda e: e.tensor_tensor(out=sqk[:], in0=F["kx"][:], in1=F["kx"][:], op=ALU.mult), reads=FB["kx"], writes=sqkB)
                    ps, pb = P.psum()
                    P.op("pe", lambda e, ps=ps: e.matmul(ps[:], lhsT=blkb[:], rhs=sqk[:], start=True, stop=True), reads=[sqkB, cB], writes=pb)
                    P.op("dve", lambda e, ps=ps: e.tensor_scalar(out=F["t0"][:], in0=ps[:], scalar1=1e-24, scalar2=None, op0=ALU.max), reads=pb, writes=FB["t0"])
                    P.op("act", lambda e: e.activation(out=F["t0"][:], in_=F["t0"][:], func=AF.Sqrt), reads=FB["t0"], writes=FB["t0"])
                    P.op("dve", lambda e: e.reciprocal(out=F["t0"][:], in_=F["t0"][:]), reads=FB["t0"], writes=FB["t0"])
                    P.op("dve", lambda e: e.tensor_tensor(out=F["kkn"][:], in0=F["kx"][:], in1=F["t0"][:], op=ALU.mult), reads=[FB["kx"], FB["t0"]], writes=FB["kkn"])
                    P.op("dve", lambda e: e.tensor_scalar(out=F["kd"][:], in0=F["asg"][:], scalar1=-1.0, scalar2=pv[:, 224 + hp:225 + hp], op0=ALU.add, op1=ALU.mult), reads=[FB["asg"], pvB], writes=FB["kd"])
                    P.op("dve", lambda e: e.scalar_tensor_tensor(out=F["kd"][:], in0=F["kd"][:], scalar=1.0, in1=F["kv"][:], op0=ALU.add, op1=ALU.mult), reads=FB["kv"], writes=FB["kd"])
                    P.op("dve", lambda e: e.tensor_tensor(out=F["bb"][:], in0=F["kkn"][:], in1=F["asg"][:], op=ALU.mult), reads=[FB["kkn"], FB["asg"]], writes=FB["bb"])
                    P.op("dve", lambda e: e.tensor_tensor(out=F["t0"][:], in0=F["rr"][:], in1=F["kd"][:], op=ALU.mult), reads=[FB["rr"], FB["kd"], FB["kkn"]], writes=FB["t0"])
                    P.op("dve", lambda e: e.tensor_scalar(out=sqk[:], in0=F["t0"][:], scalar1=pv[:, 232 + hp:233 + hp], scalar2=None, op0=ALU.mult), reads=[FB["t0"], pvB], writes=sqkB)
                    ps, pb = P.psum()
                    P.op("pe", lambda e, ps=ps: e.matmul(ps[:], lhsT=blkb[:], rhs=sqk[:], start=True, stop=True), reads=[sqkB, cB], writes=pb)
                    P.op("dve", lambda e, ps=ps: e.tensor_tensor(out=bon[hq][:], in0=ps[:], in1=F["vv"][:], op=ALU.mult), reads=[pb, FB["vv"]], writes=opB[hq])
                    for hh in range(2):
                        hsl = slice(hh * 64, (hh + 1) * 64)
                        csl = slice(hh * 64, (hh + 1) * 64)
                        P.op("dve", lambda e, hsl=hsl, csl=csl: e.scalar_tensor_tensor(out=AR[hq][hsl, :, csl], in0=v3(F["kkn"][hsl, :]), scalar=-1.0, in1=v3(F["Wex"][hsl, :]), op0=ALU.mult, op1=ALU.mult),
                             reads=[FB["kkn"], FB["Wex"]], writes=opB[hq])
                        P.op("dve", lambda e, hsl=hsl, hh=hh: e.tensor_tensor(out=AR[hq][hsl, :, 128 + hh * 64:128 + (hh + 1) * 64], in0=v3(F["rr"][hsl, :]), in1=v3(F["Wt"][hsl, :]), op=ALU.mult),
                             reads=[FB["rr"], FB["Wt"]], writes=opB[hq])
                        P.op("dve", lambda e, hsl=hsl, csl=csl: e.tensor_tensor(out=Bbd[hq][hsl, :, csl], in0=v3(F["bb"][hsl, :]), in1=v3(F["Winv"][hsl, :]), op=ALU.mult),
                             reads=[FB["bb"], FB["Winv"]], writes=opB[hq])
                        P.op("dve", lambda e, hsl=hsl, csl=csl: e.tensor_tensor(out=Kbd[hq][hsl, :, csl], in0=v3(F["kd"][hsl, :]), in1=v3(F["Winv"][hsl, :]), op=ALU.mult),
                             reads=[FB["kd"], FB["Winv"]], writes=opB[hq])
                        P.op("act", lambda e, hsl=hsl, csl=csl: e.copy(out=Vbd[hq][hsl, :, csl], in_=v3(F["vv"][hsl, :])), reads=[FB["vv"]], writes=opB[hq])
                for ci in ([] if cfg.get("no_units") else chunks):
                    cs_ = slice(ci * 64, (ci + 1) * 64)
                    for hq in range(4):
                        u = uT[hq]
                        ps, pb = P.psum()
                        pst = ps[:].bitcast(BF16)

                        def f(e, pst=pst, hq=hq):
                            e.transpose(out=pst[:, 0:128], in_=Bbd[hq][:, ci, :], identity=identb[:])
                            e.transpose(out=pst[:, 128:256], in_=Kbd[hq][:, ci, :], identity=identb[:])
                            e.transpose(out=pst[:, 256:384], in_=Vbd[hq][:, ci, :], identity=identb[:])
                            return e.transpose(out=pst[:, 384:512], in_=AR[hq][:, ci, 0:128], identity=identb[:])
                        P.op("pe", f, reads=[opB[hq], cB], writes=pb)
                        P.op("act", lambda e, pst=pst, hq=hq: e.copy(out=TT[hq][:, 0:384], in_=pst[:, 0:384]), reads=pb, writes=u["TT"])
                        P.op("act", lambda e, pst=pst, hq=hq: e.copy(out=Xf[hq][0][:, 0:128], in_=pst[:, 384:512]), reads=pb, writes=u["X0"])
                        if cfg.get("ustage", 99) < 0.5:
                            continue
                        ps, pb = P.psum()

                        def f(e, ps=ps, hq=hq):
                            e.matmul(ps[:, 0:256], lhsT=Bbd[hq][:, ci, :], rhs=AR[hq][:, ci, :], start=True, stop=True)
                            return e.matmul(ps[:, 256:512], lhsT=Kbd[hq][:, ci, :], rhs=AR[hq][:, ci, :], start=True, stop=True)
                        P.op("pe", f, reads=[opB[hq]], writes=pb)
                        P.op("dve", lambda e, ps=ps, hq=hq: e.tensor_tensor(out=AM[hq][:], in0=ps[:], in1=maskb[:, d, :], op=ALU.mult), reads=[pb, cB], writes=u["AM"])
                        P.op("dve", lambda e, ps=ps, hq=hq: e.tensor_tensor(out=Pf[hq][:, 0:128], in0=ps[:, 0:128], in1=maskPf[:, d, :], op=ALU.mult), reads=[pb, cB], writes=u["Pf"])
                        if cfg.get("ustage", 99) < 0.8:
                            continue
                        ps, pb = P.psum()
                        P.op("pe", lambda e, ps=ps, hq=hq: e.matmul(ps[:, 0:128], lhsT=AR[hq][:, ci, 0:128], rhs=Bbd[hq][:, ci, :], start=True, stop=True), reads=[opB[hq]], writes=pb)
                        P.op("dve", lambda e, ps=ps, hq=hq: e.tensor_tensor(out=Pf[hq][:, 128:256], in0=ps[:, 0:128], in1=maskNf[:, d, :], op=ALU.mult), reads=[pb, cB], writes=u["Pf"])
                    for hq in (range(4) if cfg.get("ustage", 99) >= 2 else []):
                        u = uT[hq]
                        ps, pb = P.psum()
                        P.op("pe", lambda e, ps=ps, hq=hq: e.matmul(ps[:, 0:128], lhsT=AM[hq][:, 256:384], rhs=TT[hq][:, 256:384], start=True, stop=True), reads=[u["AM"], u["TT"]], writes=pb)
                        P.op("act", lambda e, ps=ps, hq=hq: e.copy(out=Xf[hq][0][:, 128:256], in_=ps[:, 0:128]), reads=pb, writes=u["X0"])
                    Qs = [(Pf[hq][:, 0:128], Pf[hq][:, 128:256], [uT[hq]["Pf"]]) for hq in range(4)]
                    for k in (range(6) if cfg.get("ustage", 99) >= 3 else []):
                        cur = k % 2
                        for hq in range(4):
                            u = uT[hq]
                            Q, Qn, qB = Qs[hq]
                            ps, pb = P.psum()
                            P.op("pe", lambda e, ps=ps, hq=hq, Q=Q: e.matmul(ps[:, 0:256], lhsT=Q, rhs=Xf[hq][cur][:], start=True, stop=True), reads=[qB, u["X%d" % cur]], writes=pb)
                            P.op("dve", lambda e, ps=ps, hq=hq: e.tensor_tensor(out=Xf[hq][1 - cur][:], in0=ps[:, 0:256], in1=Xf[hq][cur][:].bitcast(F32), op=ALU.add), reads=[pb, u["X%d" % cur]], writes=u["X%d" % (1 - cur)])
                            if k < 5:
                                ps, pb = P.psum()

                                def f(e, ps=ps, Q=Q, Qn=Qn):
                                    e.matmul(ps[:, 0:128], lhsT=Qn, rhs=Q, start=True, stop=True)
                                    return e.matmul(ps[:, 128:256], lhsT=Q, rhs=Qn, start=True, stop=True)
                                P.op("pe", f, reads=[qB], writes=pb)
                                qn = "Q%d" % (k % 2)
                                P.op("act", lambda e, ps=ps, hq=hq: e.copy(out=QQf[hq][k % 2][:], in_=ps[:, 0:256]), reads=pb, writes=u[qn])
                                Qs[hq] = (QQf[hq][k % 2][:, 0:128], QQf[hq][k % 2][:, 128:256], [u[qn]])
                    for hq in range(4):
                        u = uT[hq]
                        P.op("act", lambda e, hq=hq: e.copy(out=X[hq][0][:], in_=Xf[hq][0][:].bitcast(F32)), reads=u["X0"], writes=u["G"])
                    for hq in (range(4) if cfg.get("ustage", 99) >= 4 else []):
                        u = uT[hq]
                        Gx = X[hq][0]
                        ps, pb = P.psum()

                        def f(e, ps=ps, hq=hq, Gx=Gx):
                            e.matmul(ps[:, 0:128], lhsT=Gx[:, 0:128], rhs=TT[hq][:, 0:128], start=True, stop=True)
                            return e.matmul(ps[:, 128:256], lhsT=Gx[:, 0:128], rhs=AM[hq][:, 128:256], start=True, stop=True)
                        P.op("pe", f, reads=[u["G"], u["TT"], u["AM"]], writes=pb)
                        P.op("dve", lambda e, ps=ps, hq=hq: e.tensor_tensor(out=GY[hq][:, 0:128], in0=ps[:, 0:128], in1=identb[:], op=ALU.add), reads=[pb, cB], writes=u["GY"])
                        P.op("dve", lambda e, ps=ps, hq=hq: e.tensor_tensor(out=GY[hq][:, 128:256], in0=ps[:, 128:256], in1=AR[hq][:, ci, 128:256], op=ALU.add), reads=[pb, opB[hq]], writes=u["GY"])
                    for hq in (range(4) if cfg.get("ustage", 99) >= 5 else []):
                        u = uT[hq]
                        hp = hpg * 4 + hq
                        Gx = X[hq][0]
                        mc = mcur[hp]
                        Mo = Mst[hp][mc]
                        ps, pb = P.psum()

                        def f(e, ps=ps, hq=hq, Gx=Gx, Mo=Mo):
                            e.matmul(ps[:, 0:128], lhsT=TT[hq][:, 0:128], rhs=Gx[:, 128:256], start=True, stop=False)
                            e.matmul(ps[:, 0:128], lhsT=TT[hq][:, 128:256], rhs=TT[hq][:, 256:384], start=False, stop=False)
                            e.matmul(ps[:, 0:128], lhsT=GY[hq][:, 0:128], rhs=Mo[:], start=False, stop=True)
                            e.matmul(ps[:, 128:256], lhsT=Gx[:, 128:256], rhs=AM[hq][:, 128:256], start=True, stop=False)
                            e.matmul(ps[:, 128:256], lhsT=TT[hq][:, 256:384], rhs=AM[hq][:, 384:512], start=False, stop=False)
                            return e.matmul(ps[:, 128:256], lhsT=Mo[:], rhs=GY[hq][:, 128:256], start=False, stop=True)
                        P.op("pe", f, reads=[u["G"], u["TT"], u["AM"], u["GY"], MB[hp][mc]], writes=pb)
                        P.op("act", lambda e, ps=ps, hq=hq, hp=hp, mc=mc: e.activation(out=Mst[hp][1 - mc][:], in_=ps[:, 0:128], func=AF.Identity, scale=WC[hq][:, ci:ci + 1]), reads=[pb, opB[hq]], writes=MB[hp][1 - mc])
                        mcur[hp] = 1 - mc
                        for hh in range(2):
                            hsl = slice(hh * 64, (hh + 1) * 64)
                            P.op("dve", lambda e, ps=ps, hq=hq, hsl=hsl, hh=hh: e.tensor_tensor(out=yt[hq][hsl, cs_], in0=ps[hsl, 128 + hh * 64:128 + (hh + 1) * 64], in1=bon[hq][hsl, cs_], op=ALU.add),
                                 reads=[pb, opB[hq]], writes=ytB[hq])
                for hq in range(4):
                    hp = hpg * 4 + hq
                    P.dma("pool", G["ysc"][s][d, hp * 128:(hp + 1) * 128, tsl], yt[hq][:], reads=ytB[hq])
        P.barrier()


def scan_post(P, cfg, l, s, G):
    ES = contextlib.ExitStack
    T = cfg["seqs"][s]
    ntile = T // NT
    pv, pvB, cB = G["pv"], G["pvB"], G["cB"]
    blkf = G["blkf"]
    with ES() as es:
        st32 = sb(P, es, "st32", [64, 1024], F32)
        g2b = sb(P, es, "g2b", [64, 1024], BF16)
        uB = Buf()
        P.dma("sp", st32[:], G["g2in"][l], writes=uB)
        P.op("dve", lambda e: e.tensor_copy(out=g2b[:], in_=st32[:]), reads=uB, writes=uB)
        lgt = [sb(P, es, "lgt", [64, NT], BF16) for _ in range(2)]
        lgB = [Buf(), Buf()]
        y0 = [sb(P, es, "y0", [128, NT], F32) for _ in range(2)]
        y1 = [sb(P, es, "y1", [128, NT], F32) for _ in range(2)]
        yB = [Buf(), Buf()]
        ymL = [sb(P, es, "ym", [128, NT], F32) for _ in range(3)]
        sqL = [sb(P, es, "sq", [128, NT], F32) for _ in range(3)]
        sdL = [sb(P, es, "sd", [128, NT], F32) for _ in range(3)]
        tBL = [Buf() for _ in range(3)]
        ob = [sb(P, es, "ob", [128, NT], BF16) for _ in range(2)]
        obB = [Buf(), Buf()]
        n = 0
        for ti in range(ntile):
            tsl = slice(ti * NT, (ti + 1) * NT)
            lk = ti % 2
            P.dma("sp", lgt[lk][:], G["lg_d"][s][:, tsl], writes=lgB[lk])
            for hp in range(8):
                k = n % 2
                ym, sq, sd, tB = ymL[n % 3], sqL[n % 3], sdL[n % 3], tBL[n % 3]
                n += 1
                hs = slice(hp * 128, (hp + 1) * 128)
                P.dma("sp", y0[k][:], G["ysc"][s][0, hs, tsl], writes=yB[k])
                P.dma("sp", y1[k][:], G["ysc"][s][1, hs, tsl], writes=yB[k])
                P.op("dve", lambda e, k=k: e.tensor_tensor(out=y0[k][:], in0=y0[k][:], in1=y1[k][:], op=ALU.add), reads=yB[k], writes=yB[k])
                ps, pb = P.psum()
                P.op("pe", lambda e, ps=ps, k=k: e.matmul(ps[:], lhsT=blkf[:], rhs=y0[k][:], start=True, stop=True), reads=[yB[k], cB], writes=pb)
                P.op("dve", lambda e, ps=ps, k=k: e.scalar_tensor_tensor(out=ym[:], in0=ps[:], scalar=-1.0 / 64, in1=y0[k][:], op0=ALU.mult, op1=ALU.add), reads=[pb, yB[k]], writes=tB)
                P.op("dve", lambda e: e.tensor_tensor(out=sq[:], in0=ym[:], in1=ym[:], op=ALU.mult), reads=tB, writes=tB)
                ps, pb = P.psum()
                P.op("pe", lambda e, ps=ps: e.matmul(ps[:], lhsT=blkf[:], rhs=sq[:], start=True, stop=True), reads=[tB, cB], writes=pb)
                P.op("act", lambda e, ps=ps: e.activation(out=sd[:], in_=ps[:], func=AF.Sqrt, scale=1.0 / 64, bias=float(GN_EPS)), reads=pb, writes=tB)
                P.op("dve", lambda e: e.reciprocal(out=sd[:], in_=sd[:]), reads=tB, writes=tB)
                P.op("dve", lambda e: e.tensor_tensor(out=ym[:], in0=ym[:], in1=sd[:], op=ALU.mult), reads=tB, writes=tB)
                P.op("dve", lambda e, hp=hp: e.tensor_scalar(out=ym[:], in0=ym[:], scalar1=pv[:, 240 + hp:241 + hp], scalar2=pv[:, 248 + hp:249 + hp], op0=ALU.mult, op1=ALU.add), reads=[tB, pvB], writes=tB)
                ps, pb = P.psum()
                P.op("pe", lambda e, ps=ps, hs=hs, lk=lk: e.matmul(ps[:], lhsT=g2b[:, hs], rhs=lgt[lk][:], start=True, stop=True), reads=[uB, lgB[lk]], writes=pb)
                P.op("dve", lambda e, ps=ps, k=k: e.tensor_tensor(out=ob[k][:], in0=ps[:], in1=ym[:], op=ALU.mult), reads=[pb, tB], writes=obB[k])
                P.dma("pool", G["mixin"][s][1024 + hp * 128:1024 + (hp + 1) * 128, tsl], ob[k][:], reads=obB[k])
        P.barrier()


def fm(v):
    v = np.asarray(v, np.float32).reshape(-1, 128)
    return np.ascontiguousarray(v.T)


def tile_w(w, mw=512):
    K, N = w.shape
    return np.ascontiguousarray(w.reshape(K // 128, 128, N // mw, mw).transpose(2, 1, 0, 3))


def prep_shared(inp, L):
    out = {}
    out["win"] = np.stack([tile_w(inp["w_in"][l]) for l in range(L)])
    out["wout"] = np.stack([tile_w(inp["w_out"][l]) for l in range(L)])
    out["wf1"] = np.stack([tile_w(inp["w_ffn1"][l]) for l in range(L)])
    w2l = []
    for l in range(L):
        w = inp["w_ffn2"][l]
        t = w.reshape(2, 32, 128, 16, 128).transpose(0, 3, 2, 1, 4)
        w2l.append(np.ascontiguousarray(t))
    out["wf2"] = np.stack(w2l)
    out["wada"] = np.stack([tile_w(inp["w_ada"][l]) for l in range(L)])
    pvs = []
    for l in range(L):
        cols = [fm(inp["g_pre_mix"][l]), fm(inp["g_post_mix"][l]), fm(inp["g_pre_ffn"][l]), fm(inp["g_post_ffn"][l]),
                fm(inp["b_ada"][l]), fm(inp["g_att_out"][l])]
        for i in range(3):
            for j in range(2):
                cols.append(fm(inp["mu_rkv"][l, i, j]))
        cols += [fm(inp["k_k"][l]), fm(inp["k_a"][l]), fm(inp["r_k"][l].reshape(-1)), fm(inp["ln_x_w"][l]), fm(inp["ln_x_b"][l])]
        pvs.append(np.concatenate(cols, axis=1))
    out["pvec"] = np.stack(pvs).astype(np.float32)
    assert out["pvec"].shape[2] == NPV
    ld, mp, mn = [], [], []
    for l in range(L):
        z32 = np.zeros((D, 32), np.float32)
        v1 = inp["v1"][l - 1] if l > 0 else z32
        w = np.concatenate([inp["w1"][l, 0], inp["w1"][l, 1], inp["a1"][l, 0], inp["a1"][l, 1], inp["g1"][l], v1], axis=1)
        muv = inp["mu_v"][l - 1] if l > 0 else np.zeros((2, D), np.float32)

        def mub(j):
            return np.concatenate([np.repeat(inp["mu_x"][l, 0, j][:, None], 128, 1), np.repeat(inp["mu_x"][l, 1, j][:, None], 128, 1),
                                   np.repeat(inp["mu_x"][l, 2, j][:, None], 64, 1), np.repeat(muv[j][:, None], 32, 1)], axis=1)
        for arr, lst in ((w, ld), (mub(0), mp), (mub(1), mn)):
            lst.append(np.ascontiguousarray(arr.reshape(KC, 128, NLORA).transpose(1, 0, 2)))
    out["lora_dn"] = np.stack(ld).astype(np.float32)
    out["lora_mp"] = np.stack(mp).astype(np.float32)
    out["lora_mn"] = np.stack(mn).astype(np.float32)
    out["w2aug"] = np.stack([np.stack([np.concatenate([inp["w2"][l, d], inp["w0"][l, d][None]], 0) for d in range(2)]) for l in range(L)]).astype(np.float32)
    out["a2aug"] = np.stack([np.stack([np.concatenate([inp["a2"][l, d], inp["a0"][l, d][None]], 0) for d in range(2)]) for l in range(L)]).astype(np.float32)
    out["g2"] = np.ascontiguousarray(inp["g2"][:L]).astype(np.float32)
    v2l = []
    for l in range(L):
        if l == 0:
            v2l.append(np.zeros((33, 1024), np.float32))
        else:
            v2l.append(np.concatenate([inp["v2"][l - 1], inp["v0"][l - 1][None]], 0))
    out["v2aug"] = np.stack(v2l).astype(np.float32)
    out["rpb"] = np.ascontiguousarray(inp["rpb"][:L]).astype(np.float32)
    out["cident"] = np.eye(128, dtype=np.float32)
    blk = np.zeros((128, 128), np.float32)
    blk[:64, :64] = 1
    blk[64:, 64:] = 1
    out["cblk"] = blk
    t = np.arange(64)
    cm = np.zeros((2, 128, 512), np.float32)
    cn = np.zeros((2, 128, 128), np.float32)
    for d in range(2):
        st = (t[:, None] < t[None, :]) if d == 0 else (t[:, None] > t[None, :])
        inc = st | (t[:, None] == t[None, :])
        bst = np.zeros((128, 128), np.float32)
        binc = np.zeros((128, 128), np.float32)
        for h in range(2):
            bst[h * 64:(h + 1) * 64, h * 64:(h + 1) * 64] = st
            binc[h * 64:(h + 1) * 64, h * 64:(h + 1) * 64] = inc
        cm[d] = np.concatenate([bst, binc, bst, binc], axis=1)
        cn[d] = bst.T
    out["cmask"] = cm
    out["cmaskn"] = cn
    kp = np.ones((128, 512), np.float32)
    kp[:, ::64] = 0
    out["ckeep"] = kp
    return out


FULL_CFG = {"depth": 4, "seqs": [4096, 2048]}


def kernel(**inputs):
    inp = {k: np.asarray(v) for k, v in inputs.items()}
    cfg = FULL_CFG
    L = cfg["depth"]
    shared = prep_shared(inp, L)
    nc = build(cfg)
    in_maps = []
    for c in range(8):
        m = dict(shared)
        m["x0"] = np.ascontiguousarray(inp["x_prompt"][c])
        m["x1"] = np.ascontiguousarray(inp["x_sample"][c % 4])
        cc = np.stack([inp["c_prompt"][c], inp["c_sample"][c % 4]], axis=1)
        m["cvec"] = np.ascontiguousarray(cc.reshape(KC, 128, 2).transpose(1, 0, 2)).astype(np.float32)
        in_maps.append(m)
    res = run_bass_kernel_spmd(nc, in_maps, core_ids=list(range(8)))
    yp = np.stack([res.results[c]["y0"] for c in range(8)])
    ys = np.stack([res.results[c]["y1"] for c in range(4)])
    return (yp.astype(np.float32), ys.astype(np.float32))
```

```python
import contextlib
import numpy as np
import ml_dtypes
import concourse.bass as bass
import concourse.mybir as mybir
from concourse.bass_utils import run_bass_kernel_spmd

F32 = mybir.dt.float32
BF16 = mybir.dt.bfloat16
F32R = mybir.dt.float32r
ALU = mybir.AluOpType
AF = mybir.ActivationFunctionType
AX = mybir.AxisListType

D = 2048
KC = 16
NT = 512
GW = 64
NEG = -30000.0
EPS = 1e-6
GN_EPS = 64e-5
NPV = 256
NLORA = 352


class Sem:
    def __init__(self, nc, name):
        self.h = nc.alloc_semaphore(name=name)
        self.count = 0


class Buf:
    __slots__ = ("w", "r")

    def __init__(self):
        self.w = {}
        self.r = {}


def flat(x):
    if isinstance(x, Buf):
        yield x
    elif x is None:
        return
    else:
        for y in x:
            yield from flat(y)


class Prog:
    def __init__(self, nc):
        self.nc = nc
        self.eng = {"pe": nc.tensor, "dve": nc.vector, "act": nc.scalar, "pool": nc.gpsimd, "sp": nc.sync}
        self.esem = {k: Sem(nc, "e_" + k) for k in self.eng}
        self.obs = {k: {} for k in self.eng}
        self.dsems = {"sp": [Sem(nc, f"dsp{i}") for i in range(16)],
                      "pool": [Sem(nc, f"dpl{i}") for i in range(16)],
                      "act": [Sem(nc, f"dac{i}") for i in range(8)]}
        self.dnext = {"sp": 0, "pool": 0, "act": 0}
        self.uid = 0
        self.ps = []
        self.psb = []
        self.psn = 0
        self.rr = 0
        self.rec = None
        self.psp = 0
        self.nmain = 8

    def name(self, s):
        self.uid += 1
        return f"{s}_{self.uid}"

    def _wait(self, e, sem, val):
        if val <= 0:
            return
        o = self.obs[e]
        if o.get(sem, 0) >= val:
            return
        self.eng[e].wait_ge(sem.h, val)
        o[sem] = val

    def _deps(self, e, reads, writes):
        need = {}
        for b in flat(reads):
            for s, v in b.w.items():
                if need.get(s, 0) < v:
                    need[s] = v
        for b in flat(writes):
            for s, v in b.w.items():
                if need.get(s, 0) < v:
                    need[s] = v
            for s, v in b.r.items():
                if need.get(s, 0) < v:
                    need[s] = v
        for s, v in need.items():
            self._wait(e, s, v)

    def _mark(self, s, v, reads, writes):
        for b in flat(reads):
            if b.r.get(s, 0) < v:
                b.r[s] = v
        for b in flat(writes):
            b.w = {s: v}
            b.r = {}

    def replay(self, it):
        if it[0] == "op":
            self.op(it[1], it[2], it[3], it[4])
        else:
            self.dma(it[1], it[2], it[3], it[4], it[5], **it[6])

    def op(self, e, fn, reads=(), writes=()):
        if self.rec is not None:
            self.rec.append(("op", e, fn, reads, writes))
            return
        self._deps(e, reads, writes)
        ins = fn(self.eng[e])
        sem = self.esem[e]
        sem.count += 1
        ins.then_inc(sem.h, 1)
        self._mark(sem, sem.count, reads, writes)

    def dma(self, e, out, in_, reads=(), writes=(), **kw):
        if self.rec is not None:
            self.rec.append(("dma", e, out, in_, reads, writes, kw))
            return
        self._deps(e, reads, writes)
        lst = self.dsems[e]
        ds = lst[self.dnext[e]]
        self.dnext[e] = (self.dnext[e] + 1) % len(lst)
        self._wait(e, ds, 16 * ds.count)
        ds.count += 1
        self.eng[e].dma_start(out=out, in_=in_, **kw).then_inc(ds.h, 16)
        self._mark(ds, 16 * ds.count, reads, writes)

    def barrier(self):
        sems = list(self.esem.values())
        for lst in self.dsems.values():
            sems += lst
        for e in self.eng:
            for s in sems:
                if s in self.esem.values():
                    self._wait(e, s, s.count)
                else:
                    self._wait(e, s, 16 * s.count)

    def psum(self, pool=None):
        if pool == "prep":
            self.psp = (self.psp + 1) % 2
            i = 6 + self.psp
            return self.ps[i], self.psb[i]
        i = self.psn % self.nmain
        self.psn = (self.psn + 1) % self.nmain
        return self.ps[i], self.psb[i]

    def ev_eng(self):
        self.rr += 1
        return "act" if self.rr % 2 else "dve"


def sb(P, es, nm, shape, dt):
    return es.enter_context(P.nc.sbuf_tensor(P.name(nm), list(shape), dt))


def build(cfg):
    L = cfg["depth"]
    TS = cfg["seqs"]
    NS = len(TS)
    dbg = cfg.get("debug", False)
    nc = bass.Bass("TRN2", target_bir_lowering=False)
    P = Prog(nc)
    SK = "ExternalOutput" if dbg else "Internal"

    def din(name, shape, dt=F32):
        return nc.dram_tensor(name, list(shape), dt, kind="ExternalInput").ap()

    def dscr(name, shape, dt, kind=None):
        return nc.dram_tensor(name, list(shape), dt, kind=kind or SK).ap()

    x_in = [din(f"x{s}", [TS[s], D]) for s in range(NS)]
    y_out = [nc.dram_tensor(f"y{s}", [TS[s], D], F32, kind="ExternalOutput").ap() for s in range(NS)]
    cvec = din("cvec", [128, KC, NS])
    win32 = din("win", [L, 12, 128, KC, 512])
    wout32 = din("wout", [L, 4, 128, KC, 512])
    wf132 = din("wf1", [L, 16, 128, KC, 512])
    wf232 = din("wf2", [L, 2, 16, 128, 32, 128])
    wada32 = din("wada", [L, 24, 128, KC, 512])
    pvec = din("pvec", [L, 128, NPV])
    lora_dn = din("lora_dn", [L, 128, KC, NLORA])
    lora_mp = din("lora_mp", [L, 128, KC, NLORA])
    lora_mn = din("lora_mn", [L, 128, KC, NLORA])
    w2aug = din("w2aug", [L, 2, 65, 1024])
    a2aug = din("a2aug", [L, 2, 65, 1024])
    g2in = din("g2", [L, 64, 1024])
    v2aug = din("v2aug", [L, 33, 1024])
    rpb = din("rpb", [L, 8, 15, 31])
    cident = din("cident", [128, 128])
    cmask = din("cmask", [2, 128, 512])
    cmaskn = din("cmaskn", [2, 128, 128])
    cblk = din("cblk", [128, 128])
    ckeep = din("ckeep", [128, 512])

    win = dscr("win_b", [L, 12, 128, KC, 512], BF16, "Internal")
    wout = dscr("wout_b", [L, 4, 128, KC, 512], BF16, "Internal")
    wf1 = dscr("wf1_b", [L, 16, 128, KC, 512], BF16, "Internal")
    wf2 = dscr("wf2_b", [L, 2, 16, 128, 32, 128], BF16, "Internal")
    wada = dscr("wada_b", [L, 24, 128, KC, 512], BF16, "Internal")

    xres = [dscr(f"xres{s}", [D, TS[s]], F32) for s in range(NS)]
    hfm = [dscr(f"hfm{s}", [D, TS[s] + 2], BF16) for s in range(NS)]
    qfm = [dscr(f"qfm{s}", [1024, TS[s]], BF16) for s in range(NS)]
    kfm = [dscr(f"kfm{s}", [1024, TS[s]], BF16) for s in range(NS)]
    vtm = [dscr(f"vtm{s}", [TS[s] + 64, 1024], BF16) for s in range(NS)]
    rkv = [dscr(f"rkv{s}", [3072, TS[s] + 2], F32) for s in range(NS)]
    lw_d = [dscr(f"lw{s}", [128, TS[s]], BF16) for s in range(NS)]
    la_d = [dscr(f"la{s}", [128, TS[s]], BF16) for s in range(NS)]
    lg_d = [dscr(f"lg{s}", [64, TS[s]], BF16) for s in range(NS)]
    lv_d = [dscr(f"lv{s}", [32, TS[s]], BF16) for s in range(NS)]
    mixin = [dscr(f"mixin{s}", [D, TS[s]], BF16) for s in range(NS)]
    ysc = [dscr(f"ysc{s}", [2, 1024, TS[s]], F32) for s in range(NS)]
    vfirst = [dscr(f"vfirst{s}", [1024, TS[s]], F32) for s in range(NS)]

    with contextlib.ExitStack() as g:
        for i in range(8):
            P.ps.append(g.enter_context(nc.psum_tensor(f"ps{i}", [128, 512], F32)))
            P.psb.append(Buf())
        ident = sb(P, g, "ident", [128, 128], F32)
        identb = sb(P, g, "identb", [128, 128], BF16)
        onesb = sb(P, g, "onesb", [128, 128], BF16)
        blkb = sb(P, g, "blkb", [128, 128], BF16)
        blkf = sb(P, g, "blkf", [128, 128], F32)
        maskb = sb(P, g, "maskb", [128, 2, 512], BF16)
        masknb = sb(P, g, "masknb", [128, 2, 128], BF16)
        keep = sb(P, g, "keep", [128, 512], F32)
        maskPf = sb(P, g, "maskPf", [128, 2, 128], F32)
        maskNf = sb(P, g, "maskNf", [128, 2, 128], F32)
        onesf = sb(P, g, "onesf", [128, 512], F32)
        cs = sb(P, g, "cs", [128, KC, NS], BF16)
        pv = sb(P, g, "pv", [128, NPV], F32)
        modv = sb(P, g, "modv", [128, NS, 6, KC], F32)
        cB = Buf()
        pvB = Buf()
        modB = Buf()

        with contextlib.ExitStack() as es:
            t1 = sb(P, es, "t1", [128, 2, 512], F32)
            t2 = sb(P, es, "t2", [128, 2, 128], F32)
            t3 = sb(P, es, "t3", [128, 128], F32)
            cv = sb(P, es, "cv", [128, KC, NS], F32)
            tb = Buf()
            P.dma("sp", ident[:], cident[:, :], writes=cB)
            P.dma("sp", t1[:], cmask.rearrange("d p n -> p d n"), writes=tb)
            P.dma("sp", t2[:], cmaskn.rearrange("d p n -> p d n"), writes=tb)
            P.dma("sp", t3[:], cblk[:, :], writes=tb)
            P.dma("sp", blkf[:], cblk[:, :], writes=cB)
            P.dma("sp", keep[:], ckeep[:, :], writes=cB)
            P.dma("sp", cv[:], cvec[:, :, :], writes=tb)
            P.op("dve", lambda e: e.tensor_copy(out=identb[:], in_=ident[:]), reads=cB, writes=cB)
            P.op("dve", lambda e: e.tensor_copy(out=maskb[:], in_=t1[:]), reads=tb, writes=cB)
            P.op("dve", lambda e: e.tensor_copy(out=masknb[:], in_=t2[:]), reads=tb, writes=cB)
            P.op("dve", lambda e: e.tensor_copy(out=maskPf[:], in_=t1[:, :, 0:128]), reads=tb, writes=cB)
            P.op("dve", lambda e: e.tensor_copy(out=maskNf[:], in_=t2[:]), reads=tb, writes=cB)
            P.op("dve", lambda e: e.tensor_copy(out=blkb[:], in_=t3[:]), reads=tb, writes=cB)
            P.op("dve", lambda e: e.memset(onesb[:], 1.0), writes=cB)
            P.op("dve", lambda e: e.memset(onesf[:], 1.0), writes=cB)
            P.op("act", lambda e: e.activation(out=cs[:], in_=cv[:], func=AF.Silu), reads=tb, writes=cB)
            P.barrier()

        def conv(dst, src):
            n = 1
            for d_ in src.shape:
                n *= d_
            rows = n // 2048
            s2 = src.tensor.reshape([rows, 2048]) if False else None
            return n

        def conv2d(dst2, src2):
            rows = src2.shape[0]
            step = 2048
            for r0 in range(0, rows, step):
                r1 = min(rows, r0 + step)
                P.dma("pool", dst2[r0:r1, :], src2[r0:r1, :])

        for l in range(L):
            conv2d(win[l].rearrange("g p k (a n) -> (g p k a) n", n=512), win32[l].rearrange("g p k (a n) -> (g p k a) n", n=512))
            conv2d(wout[l].rearrange("g p k (a n) -> (g p k a) n", n=512), wout32[l].rearrange("g p k (a n) -> (g p k a) n", n=512))
            conv2d(wf1[l].rearrange("g p k (a n) -> (g p k a) n", n=512), wf132[l].rearrange("g p k (a n) -> (g p k a) n", n=512))
            conv2d(wf2[l].rearrange("h g p (k a) n -> (h g p k) (a n)", a=4), wf232[l].rearrange("h g p (k a) n -> (h g p k) (a n)", a=4))
            conv2d(wada[l].rearrange("g p k (a n) -> (g p k a) n", n=512), wada32[l].rearrange("g p k (a n) -> (g p k a) n", n=512))

        for s in range(NS):
            T = TS[s]
            xr = xres[s].rearrange("(c p) t -> p c t", p=128)
            with contextlib.ExitStack() as es:
                xt = [sb(P, es, "xt", [128, 4, D], F32) for _ in range(2)]
                xtB = [Buf(), Buf()]
                xf = [sb(P, es, "xf", [128, KC, NT], F32) for _ in range(2)]
                xfB = [Buf(), Buf()]
                for ti in range(T // NT):
                    k = ti % 2
                    P.dma("sp", xt[k][:], x_in[s][ti * NT:(ti + 1) * NT, :].rearrange("(j p) d -> p j d", p=128), writes=xtB[k])
                    for c in range(KC):
                        ps, pb = P.psum()

                        def f(e, ps=ps, c=c, k=k):
                            for j in range(4):
                                ins = e.transpose(out=ps[:, j * 128:(j + 1) * 128], in_=xt[k][:, j, c * 128:(c + 1) * 128], identity=ident[:])
                            return ins
                        P.op("pe", f, reads=[xtB[k], cB], writes=pb)
                        ee = P.ev_eng()
                        if ee == "act":
                            P.op("act", lambda e, ps=ps, c=c, k=k: e.copy(out=xf[k][:, c, :], in_=ps[:]), reads=pb, writes=xfB[k])
                        else:
                            P.op("dve", lambda e, ps=ps, c=c, k=k: e.tensor_copy(out=xf[k][:, c, :], in_=ps[:]), reads=pb, writes=xfB[k])
                    P.dma("pool", xr[:, :, ti * NT:(ti + 1) * NT], xf[k][:], reads=xfB[k])
                P.barrier()

        with contextlib.ExitStack() as es:
            zf = sb(P, es, "zf", [128, 24], F32)
            zb = sb(P, es, "zb", [128, 16], BF16)
            zB = Buf()
            P.op("dve", lambda e: e.memset(zf[:], 0.0), writes=zB)
            P.op("dve", lambda e: e.memset(zb[:], 0.0), writes=zB)
            for s in range(NS):
                T = TS[s]
                for col in (0, T + 1):
                    with nc.allow_non_contiguous_dma(reason="halo zero"):
                        P.dma("sp", hfm[s].rearrange("(c p) t -> p c t", p=128)[:, :, col:col + 1], zb[:].unsqueeze(2), reads=zB)
                        P.dma("sp", rkv[s].rearrange("(c p) t -> p c t", p=128)[:, :, col:col + 1], zf[:].unsqueeze(2), reads=zB)
            P.barrier()

        for l in range(L):
            layer(P, g, cfg, l, locals())

        for s in range(NS):
            T = TS[s]
            xr = xres[s].rearrange("(c p) t -> p c t", p=128)
            with contextlib.ExitStack() as es:
                xf = [sb(P, es, "xf", [128, KC, NT], F32) for _ in range(2)]
                xfB = [Buf(), Buf()]
                xt = [sb(P, es, "xt", [128, 4, D], F32) for _ in range(2)]
                xtB = [Buf(), Buf()]
                for ti in range(T // NT):
                    k = ti % 2
                    P.dma("sp", xf[k][:], xr[:, :, ti * NT:(ti + 1) * NT], writes=xfB[k])
                    for j in range(4):
                        for c4 in range(4):
                            ps, pb = P.psum()

                            def f(e, ps=ps, c4=c4, j=j, k=k):
                                for cc in range(4):
                                    c = c4 * 4 + cc
                                    ins = e.transpose(out=ps[:, cc * 128:(cc + 1) * 128], in_=xf[k][:, c, j * 128:(j + 1) * 128], identity=ident[:])
                                return ins
                            P.op("pe", f, reads=[xfB[k], cB], writes=pb)
                            ee = P.ev_eng()
                            if ee == "act":
                                P.op("act", lambda e, ps=ps, c4=c4, j=j, k=k: e.copy(out=xt[k][:, j, c4 * 512:(c4 + 1) * 512], in_=ps[:]), reads=pb, writes=xtB[k])
                            else:
                                P.op("dve", lambda e, ps=ps, c4=c4, j=j, k=k: e.tensor_copy(out=xt[k][:, j, c4 * 512:(c4 + 1) * 512], in_=ps[:]), reads=pb, writes=xtB[k])
                    P.dma("pool", y_out[s][ti * NT:(ti + 1) * NT, :].rearrange("(j p) d -> p j d", p=128), xt[k][:], reads=xtB[k])
                P.barrier()
        P.barrier()
    return nc


def rms_rstd(P, G, src, srcB, kcn, sq, sqB, sd, rstd, rsB, dim, eps):
    onesb = G["onesb"]
    cB = G["cB"]
    P.op("act", lambda e: e.activation(out=sq[:, 0:kcn, :], in_=src, func=AF.Square), reads=srcB, writes=sqB)
    ps, pb = P.psum()

    def f(e):
        for c in range(kcn):
            ins = e.matmul(ps[:], lhsT=onesb[:], rhs=sq[:, c, :], start=(c == 0), stop=(c == kcn - 1))
        return ins
    P.op("pe", f, reads=[sqB, cB], writes=pb)
    P.op("act", lambda e: e.activation(out=sd[:], in_=ps[:], func=AF.Sqrt, scale=1.0 / dim, bias=float(eps)), reads=[pb], writes=rsB)
    P.op("dve", lambda e: e.reciprocal(out=rstd[:], in_=sd[:]), reads=rsB, writes=rsB)


def rms_rstd(P, G, src, srcB, kcn, sq, sqB, sd, rstd, rsB, dim, eps):
    onesb = G["onesb"]
    cB = G["cB"]
    P.op("act", lambda e: e.activation(out=sq[:, 0:kcn, :], in_=src, func=AF.Square), reads=srcB, writes=sqB)
    ps, pb = P.psum()

    def f(e):
        for c in range(kcn):
            ins = e.matmul(ps[:], lhsT=onesb[:], rhs=sq[:, c, :], start=(c == 0), stop=(c == kcn - 1))
        return ins
    P.op("pe", f, reads=[sqB, cB], writes=pb)
    P.op("act", lambda e: e.activation(out=sd[:], in_=ps[:], func=AF.Sqrt, scale=1.0 / dim, bias=float(eps)), reads=[pb], writes=rsB)
    P.op("dve", lambda e: e.reciprocal(out=rstd[:], in_=sd[:]), reads=rsB, writes=rsB)


def rms_rstd(P, G, src, srcB, kcn, sq, sqB, sd, rstd, rsB, dim, eps):
    onesb = G["onesb"]
    cB = G["cB"]
    P.op("act", lambda e: e.activation(out=sq[:, 0:kcn, :], in_=src, func=AF.Square), reads=srcB, writes=sqB)
    ps, pb = P.psum()

    def f(e):
        for c in range(kcn):
            ins = e.matmul(ps[:], lhsT=onesb[:], rhs=sq[:, c, :], start=(c == 0), stop=(c == kcn - 1))
        return ins
    P.op("pe", f, reads=[sqB, cB], writes=pb)
    P.op("act", lambda e: e.activation(out=sd[:], in_=ps[:], func=AF.Sqrt, scale=1.0 / dim, bias=float(eps)), reads=[pb], writes=rsB)
    P.op("dve", lambda e: e.reciprocal(out=rstd[:], in_=sd[:]), reads=rsB, writes=rsB)


def rms_rstd(P, G, src, srcB, kcn, sq, sqB, sd, rstd, rsB, dim, eps):
    onesb = G["onesb"]
    cB = G["cB"]
    P.op("act", lambda e: e.activation(out=sq[:, 0:kcn, :], in_=src, func=AF.Square), reads=srcB, writes=sqB)
    ps, pb = P.psum()

    def f(e):
        for c in range(kcn):
            ins = e.matmul(ps[:], lhsT=onesb[:], rhs=sq[:, c, :], start=(c == 0), stop=(c == kcn - 1))
        return ins
    P.op("pe", f, reads=[sqB, cB], writes=pb)
    P.op("act", lambda e: e.activation(out=sd[:], in_=ps[:], func=AF.Sqrt, scale=1.0 / dim, bias=float(eps)), reads=[pb], writes=rsB)
    P.op("dve", lambda e: e.reciprocal(out=rstd[:], in_=sd[:]), reads=rsB, writes=rsB)


def rms_rstd(P, G, src, srcB, kcn, sq, sqB, sd, rstd, rsB, dim, eps):
    onesb = G["onesb"]
    cB = G["cB"]
    P.op("act", lambda e: e.activation(out=sq[:, 0:kcn, :], in_=src, func=AF.Square), reads=srcB, writes=sqB)
    ps, pb = P.psum()

    def f(e):
        for c in range(kcn):
            ins = e.matmul(ps[:], lhsT=onesb[:], rhs=sq[:, c, :], start=(c == 0), stop=(c == kcn - 1))
        return ins
    P.op("pe", f, reads=[sqB, cB], writes=pb)
    P.op("act", lambda e: e.activation(out=sd[:], in_=ps[:], func=AF.Sqrt, scale=1.0 / dim, bias=float(eps)), reads=[pb], writes=rsB)
    P.op("dve", lambda e: e.reciprocal(out=rstd[:], in_=sd[:]), reads=rsB, writes=rsB)


def rms_rstd(P, G, src, srcB, kcn, sq, sqB, sd, rstd, rsB, dim, eps):
    onesb = G["onesb"]
    cB = G["cB"]
    P.op("act", lambda e: e.activation(out=sq[:, 0:kcn, :], in_=src, func=AF.Square), reads=srcB, writes=sqB)
    ps, pb = P.psum()

    def f(e):
        for c in range(kcn):
            ins = e.matmul(ps[:], lhsT=onesb[:], rhs=sq[:, c, :], start=(c == 0), stop=(c == kcn - 1))
        return ins
    P.op("pe", f, reads=[sqB, cB], writes=pb)
    P.op("act", lambda e: e.activation(out=sd[:], in_=ps[:], func=AF.Sqrt, scale=1.0 / dim, bias=float(eps)), reads=[pb], writes=rsB)
    P.op("dve", lambda e: e.reciprocal(out=rstd[:], in_=sd[:]), reads=rsB, writes=rsB)


def rms_rstd(P, G, src, srcB, kcn, sq, sqB, sd, rstd, rsB, dim, eps):
    onesb = G["onesb"]
    cB = G["cB"]
    P.op("act", lambda e: e.activation(out=sq[:, 0:kcn, :], in_=src, func=AF.Square), reads=srcB, writes=sqB)
    ps, pb = P.psum()

    def f(e):
        for c in range(kcn):
            ins = e.matmul(ps[:], lhsT=onesb[:], rhs=sq[:, c, :], start=(c == 0), stop=(c == kcn - 1))
        return ins
    P.op("pe", f, reads=[sqB, cB], writes=pb)
    P.op("act", lambda e: e.activation(out=sd[:], in_=ps[:], func=AF.Sqrt, scale=1.0 / dim, bias=float(eps)), reads=[pb], writes=rsB)
    P.op("dve", lambda e: e.reciprocal(out=rstd[:], in_=sd[:]), reads=rsB, writes=rsB)


def rms_rstd(P, G, src, srcB, kcn, sq, sqB, sd, rstd, rsB, dim, eps):
    onesb = G["onesb"]
    cB = G["cB"]
    P.op("act", lambda e: e.activation(out=sq[:, 0:kcn, :], in_=src, func=AF.Square), reads=srcB, writes=sqB)
    ps, pb = P.psum()

    def f(e):
        for c in range(kcn):
            ins = e.matmul(ps[:], lhsT=onesb[:], rhs=sq[:, c, :], start=(c == 0), stop=(c == kcn - 1))
        return ins
    P.op("pe", f, reads=[sqB, cB], writes=pb)
    P.op("act", lambda e: e.activation(out=sd[:], in_=ps[:], func=AF.Sqrt, scale=1.0 / dim, bias=float(eps)), reads=[pb], writes=rsB)
    P.op("dve", lambda e: e.reciprocal(out=rstd[:], in_=sd[:]), reads=rsB, writes=rsB)


def rms_rstd(P, G, src, srcB, kcn, sq, sqB, sd, rstd, rsB, dim, eps):
    onesb = G["onesb"]
    cB = G["cB"]
    P.op("act", lambda e: e.activation(out=sq[:, 0:kcn, :], in_=src, func=AF.Square), reads=srcB, writes=sqB)
    ps, pb = P.psum()

    def f(e):
        for c in range(kcn):
            ins = e.matmul(ps[:], lhsT=onesb[:], rhs=sq[:, c, :], start=(c == 0), stop=(c == kcn - 1))
        return ins
    P.op("pe", f, reads=[sqB, cB], writes=pb)
    P.op("act", lambda e: e.activation(out=sd[:], in_=ps[:], func=AF.Sqrt, scale=1.0 / dim, bias=float(eps)), reads=[pb], writes=rsB)
    P.op("dve", lambda e: e.reciprocal(out=rstd[:], in_=sd[:]), reads=rsB, writes=rsB)


def rms_rstd(P, G, src, srcB, kcn, sq, sqB, sd, rstd, rsB, dim, eps):
    onesb = G["onesb"]
    cB = G["cB"]
    P.op("act", lambda e: e.activation(out=sq[:, 0:kcn, :], in_=src, func=AF.Square), reads=srcB, writes=sqB)
    ps, pb = P.psum()

    def f(e):
        for c in range(kcn):
            ins = e.matmul(ps[:], lhsT=onesb[:], rhs=sq[:, c, :], start=(c == 0), stop=(c == kcn - 1))
        return ins
    P.op("pe", f, reads=[sqB, cB], writes=pb)
    P.op("act", lambda e: e.activation(out=sd[:], in_=ps[:], func=AF.Sqrt, scale=1.0 / dim, bias=float(eps)), reads=[pb], writes=rsB)
    P.op("dve", lambda e: e.reciprocal(out=rstd[:], in_=sd[:]), reads=rsB, writes=rsB)


def rms_rstd(P, G, src, srcB, kcn, sq, sqB, sd, rstd, rsB, dim, eps):
    onesb = G["onesb"]
    cB = G["cB"]
    P.op("act", lambda e: e.activation(out=sq[:, 0:kcn, :], in_=src, func=AF.Square), reads=srcB, writes=sqB)
    ps, pb = P.psum()

    def f(e):
        for c in range(kcn):
            ins = e.matmul(ps[:], lhsT=onesb[:], rhs=sq[:, c, :], start=(c == 0), stop=(c == kcn - 1))
        return ins
    P.op("pe", f, reads=[sqB, cB], writes=pb)
    P.op("act", lambda e: e.activation(out=sd[:], in_=ps[:], func=AF.Sqrt, scale=1.0 / dim, bias=float(eps)), reads=[pb], writes=rsB)
    P.op("dve", lambda e: e.reciprocal(out=rstd[:], in_=sd[:]), reads=rsB, writes=rsB)


def rms_rstd(P, G, src, srcB, kcn, sq, sqB, sd, rstd, rsB, dim, eps):
    onesb = G["onesb"]
    cB = G["cB"]
    P.op("act", lambda e: e.activation(out=sq[:, 0:kcn, :], in_=src, func=AF.Square), reads=srcB, writes=sqB)
    ps, pb = P.psum()

    def f(e):
        for c in range(kcn):
            ins = e.matmul(ps[:], lhsT=onesb[:], rhs=sq[:, c, :], start=(c == 0), stop=(c == kcn - 1))
        return ins
    P.op("pe", f, reads=[sqB, cB], writes=pb)
    P.op("act", lambda e: e.activation(out=sd[:], in_=ps[:], func=AF.Sqrt, scale=1.0 / dim, bias=float(eps)), reads=[pb], writes=rsB)
    P.op("dve", lambda e: e.reciprocal(out=rstd[:], in_=sd[:]), reads=rsB, writes=rsB)


def rms_rstd(P, G, src, srcB, kcn, sq, sqB, sd, rstd, rsB, dim, eps):
    onesb = G["onesb"]
    cB = G["cB"]
    P.op("act", lambda e: e.activation(out=sq[:, 0:kcn, :], in_=src, func=AF.Square), reads=srcB, writes=sqB)
    ps, pb = P.psum()

    def f(e):
        for c in range(kcn):
            ins = e.matmul(ps[:], lhsT=onesb[:], rhs=sq[:, c, :], start=(c == 0), stop=(c == kcn - 1))
        return ins
    P.op("pe", f, reads=[sqB, cB], writes=pb)
    P.op("act", lambda e: e.activation(out=sd[:], in_=ps[:], func=AF.Sqrt, scale=1.0 / dim, bias=float(eps)), reads=[pb], writes=rsB)
    P.op("dve", lambda e: e.reciprocal(out=rstd[:], in_=sd[:]), reads=rsB, writes=rsB)


def rms_rstd(P, G, src, srcB, kcn, sq, sqB, sd, rstd, rsB, dim, eps):
    onesb = G["onesb"]
    cB = G["cB"]
    P.op("act", lambda e: e.activation(out=sq[:, 0:kcn, :], in_=src, func=AF.Square), reads=srcB, writes=sqB)
    ps, pb = P.psum()

    def f(e):
        for c in range(kcn):
            ins = e.matmul(ps[:], lhsT=onesb[:], rhs=sq[:, c, :], start=(c == 0), stop=(c == kcn - 1))
        return ins
    P.op("pe", f, reads=[sqB, cB], writes=pb)
    P.op("act", lambda e: e.activation(out=sd[:], in_=ps[:], func=AF.Sqrt, scale=1.0 / dim, bias=float(eps)), reads=[pb], writes=rsB)
    P.op("dve", lambda e: e.reciprocal(out=rstd[:], in_=sd[:]), reads=rsB, writes=rsB)


def rms_rstd(P, G, src, srcB, kcn, sq, sqB, sd, rstd, rsB, dim, eps):
    onesb = G["onesb"]
    cB = G["cB"]
    P.op("act", lambda e: e.activation(out=sq[:, 0:kcn, :], in_=src, func=AF.Square), reads=srcB, writes=sqB)
    ps, pb = P.psum()

    def f(e):
        for c in range(kcn):
            ins = e.matmul(ps[:], lhsT=onesb[:], rhs=sq[:, c, :], start=(c == 0), stop=(c == kcn - 1))
        return ins
    P.op("pe", f, reads=[sqB, cB], writes=pb)
    P.op("act", lambda e: e.activation(out=sd[:], in_=ps[:], func=AF.Sqrt, scale=1.0 / dim, bias=float(eps)), reads=[pb], writes=rsB)
    P.op("dve", lambda e: e.reciprocal(out=rstd[:], in_=sd[:]), reads=rsB, writes=rsB)


def rms_rstd(P, G, src, srcB, kcn, sq, sqB, sd, rstd, rsB, dim, eps):
    onesb = G["onesb"]
    cB = G["cB"]
    P.op("act", lambda e: e.activation(out=sq[:, 0:kcn, :], in_=src, func=AF.Square), reads=srcB, writes=sqB)
    ps, pb = P.psum()

    def f(e):
        for c in range(kcn):
            ins = e.matmul(ps[:], lhsT=onesb[:], rhs=sq[:, c, :], start=(c == 0), stop=(c == kcn - 1))
        return ins
    P.op("pe", f, reads=[sqB, cB], writes=pb)
    P.op("act", lambda e: e.activation(out=sd[:], in_=ps[:], func=AF.Sqrt, scale=1.0 / dim, bias=float(eps)), reads=[pb], writes=rsB)
    P.op("dve", lambda e: e.reciprocal(out=rstd[:], in_=sd[:]), reads=rsB, writes=rsB)


def rms_rstd(P, G, src, srcB, kcn, sq, sqB, sd, rstd, rsB, dim, eps):
    onesb = G["onesb"]
    cB = G["cB"]
    P.op("act", lambda e: e.activation(out=sq[:, 0:kcn, :], in_=src, func=AF.Square), reads=srcB, writes=sqB)
    ps, pb = P.psum()

    def f(e):
        for c in range(kcn):
            ins = e.matmul(ps[:], lhsT=onesb[:], rhs=sq[:, c, :], start=(c == 0), stop=(c == kcn - 1))
        return ins
    P.op("pe", f, reads=[sqB, cB], writes=pb)
    P.op("act", lambda e: e.activation(out=sd[:], in_=ps[:], func=AF.Sqrt, scale=1.0 / dim, bias=float(eps)), reads=[pb], writes=rsB)
    P.op("dve", lambda e: e.reciprocal(out=rstd[:], in_=sd[:]), reads=rsB, writes=rsB)


def rms_rstd(P, G, src, srcB, kcn, sq, sqB, sd, rstd, rsB, dim, eps):
    onesb = G["onesb"]
    cB = G["cB"]
    P.op("act", lambda e: e.activation(out=sq[:, 0:kcn, :], in_=src, func=AF.Square), reads=srcB, writes=sqB)
    ps, pb = P.psum()

    def f(e):
        for c in range(kcn):
            ins = e.matmul(ps[:], lhsT=onesb[:], rhs=sq[:, c, :], start=(c == 0), stop=(c == kcn - 1))
        return ins
    P.op("pe", f, reads=[sqB, cB], writes=pb)
    P.op("act", lambda e: e.activation(out=sd[:], in_=ps[:], func=AF.Sqrt, scale=1.0 / dim, bias=float(eps)), reads=[pb], writes=rsB)
    P.op("dve", lambda e: e.reciprocal(out=rstd[:], in_=sd[:]), reads=rsB, writes=rsB)


def rms_rstd(P, G, src, srcB, kcn, sq, sqB, sd, rstd, rsB, dim, eps):
    onesb = G["onesb"]
    cB = G["cB"]
    P.op("act", lambda e: e.activation(out=sq[:, 0:kcn, :], in_=src, func=AF.Square), reads=srcB, writes=sqB)
    ps, pb = P.psum()

    def f(e):
        for c in range(kcn):
            ins = e.matmul(ps[:], lhsT=onesb[:], rhs=sq[:, c, :], start=(c == 0), stop=(c == kcn - 1))
        return ins
    P.op("pe", f, reads=[sqB, cB], writes=pb)
    P.op("act", lambda e: e.activation(out=sd[:], in_=ps[:], func=AF.Sqrt, scale=1.0 / dim, bias=float(eps)), reads=[pb], writes=rsB)
    P.op("dve", lambda e: e.reciprocal(out=rstd[:], in_=sd[:]), reads=rsB, writes=rsB)


def rms_rstd(P, G, src, srcB, kcn, sq, sqB, sd, rstd, rsB, dim, eps):
    onesb = G["onesb"]
    cB = G["cB"]
    P.op("act", lambda e: e.activation(out=sq[:, 0:kcn, :], in_=src, func=AF.Square), reads=srcB, writes=sqB)
    ps, pb = P.psum()

    def f(e):
        for c in range(kcn):
            ins = e.matmul(ps[:], lhsT=onesb[:], rhs=sq[:, c, :], start=(c == 0), stop=(c == kcn - 1))
        return ins
    P.op("pe", f, reads=[sqB, cB], writes=pb)
    P.op("act", lambda e: e.activation(out=sd[:], in_=ps[:], func=AF.Sqrt, scale=1.0 / dim, bias=float(eps)), reads=[pb], writes=rsB)
    P.op("dve", lambda e: e.reciprocal(out=rstd[:], in_=sd[:]), reads=rsB, writes=rsB)


def rms_rstd(P, G, src, srcB, kcn, sq, sqB, sd, rstd, rsB, dim, eps):
    onesb = G["onesb"]
    cB = G["cB"]
    P.op("act", lambda e: e.activation(out=sq[:, 0:kcn, :], in_=src, func=AF.Square), reads=srcB, writes=sqB)
    ps, pb = P.psum()

    def f(e):
        for c in range(kcn):
            ins = e.matmul(ps[:], lhsT=onesb[:], rhs=sq[:, c, :], start=(c == 0), stop=(c == kcn - 1))
        return ins
    P.op("pe", f, reads=[sqB, cB], writes=pb)
    P.op("act", lambda e: e.activation(out=sd[:], in_=ps[:], func=AF.Sqrt, scale=1.0 / dim, bias=float(eps)), reads=[pb], writes=rsB)
    P.op("dve", lambda e: e.reciprocal(out=rstd[:], in_=sd[:]), reads=rsB, writes=rsB)


def rms_rstd(P, G, src, srcB, kcn, sq, sqB, sd, rstd, rsB, dim, eps):
    onesb = G["onesb"]
    cB = G["cB"]
    P.op("act", lambda e: e.activation(out=sq[:, 0:kcn, :], in_=src, func=AF.Square), reads=srcB, writes=sqB)
    ps, pb = P.psum()

    def f(e):
        for c in range(kcn):
            ins = e.matmul(ps[:], lhsT=onesb[:], rhs=sq[:, c, :], start=(c == 0), stop=(c == kcn - 1))
        return ins
    P.op("pe", f, reads=[sqB, cB], writes=pb)
    P.op("act", lambda e: e.activation(out=sd[:], in_=ps[:], func=AF.Sqrt, scale=1.0 / dim, bias=float(eps)), reads=[pb], writes=rsB)
    P.op("dve", lambda e: e.reciprocal(out=rstd[:], in_=sd[:]), reads=rsB, writes=rsB)


def rms_rstd(P, G, src, srcB, kcn, sq, sqB, sd, rstd, rsB, dim, eps):
    onesb = G["onesb"]
    cB = G["cB"]
    P.op("act", lambda e: e.activation(out=sq[:, 0:kcn, :], in_=src, func=AF.Square), reads=srcB, writes=sqB)
    ps, pb = P.psum()

    def f(e):
        for c in range(kcn):
            ins = e.matmul(ps[:], lhsT=onesb[:], rhs=sq[:, c, :], start=(c == 0), stop=(c == kcn - 1))
        return ins
    P.op("pe", f, reads=[sqB, cB], writes=pb)
    P.op("act", lambda e: e.activation(out=sd[:], in_=ps[:], func=AF.Sqrt, scale=1.0 / dim, bias=float(eps)), reads=[pb], writes=rsB)
    P.op("dve", lambda e: e.reciprocal(out=rstd[:], in_=sd[:]), reads=rsB, writes=rsB)


def rms_rstd(P, G, src, srcB, kcn, sq, sqB, sd, rstd, rsB, dim, eps):
    onesb = G["onesb"]
    cB = G["cB"]
    P.op("act", lambda e: e.activation(out=sq[:, 0:kcn, :], in_=src, func=AF.Square), reads=srcB, writes=sqB)
    ps, pb = P.psum()

    def f(e):
        for c in range(kcn):
            ins = e.matmul(ps[:], lhsT=onesb[:], rhs=sq[:, c, :], start=(c == 0), stop=(c == kcn - 1))
        return ins
    P.op("pe", f, reads=[sqB, cB], writes=pb)
    P.op("act", lambda e: e.activation(out=sd[:], in_=ps[:], func=AF.Sqrt, scale=1.0 / dim, bias=float(eps)), reads=[pb], writes=rsB)
    P.op("dve", lambda e: e.reciprocal(out=rstd[:], in_=sd[:]), reads=rsB, writes=rsB)


def layer(P, g, cfg, l, G):
    nc = P.nc
    TS = cfg["seqs"]
    NS = len(TS)
    pv, modv, cs, cB = G["pv"], G["modv"], G["cs"], G["cB"]
    pvB, modB = G["pvB"], G["modB"]
    ES = contextlib.ExitStack
    stop_after = cfg.get("stop_after", "")

    P.dma("sp", pv[:], G["pvec"][l], writes=pvB)
    with ES() as es:
        wsl = [sb(P, es, "wsl", [128, KC, 512], BF16) for _ in range(2)]
        wB = [Buf(), Buf()]
        modraw = sb(P, es, "modraw", [128, 96, NS], F32)
        mrB = Buf()
        ps, pb = P.psum()
        for gi in range(24):
            k = gi % 2
            P.dma("sp", wsl[k][:], G["wada"][l, gi], writes=wB[k])

            def f(e, gi=gi, k=k):
                for m in range(4):
                    j = gi * 4 + m
                    for kc in range(KC):
                        ins = e.matmul(ps[:, j * NS:(j + 1) * NS], lhsT=wsl[k][:, kc, m * 128:(m + 1) * 128], rhs=cs[:, kc, :], start=(kc == 0), stop=(kc == KC - 1))
                return ins
            P.op("pe", f, reads=[wB[k], cB], writes=pb)
        P.op("dve", lambda e: e.tensor_tensor(out=modraw[:], in0=ps[:, 0:96 * NS].rearrange("p (j s) -> p j s", s=NS),
                                              in1=pv[:, 64:160].unsqueeze(2).to_broadcast([128, 96, NS]), op=ALU.add), reads=[pb, pvB], writes=mrB)
        for s in range(NS):
            P.op("dve", lambda e, s=s: e.tensor_copy(out=modv[:, s, 0, :], in_=modraw[:, 0:16, s]), reads=mrB, writes=modB)
            P.op("dve", lambda e, s=s: e.scalar_tensor_tensor(out=modv[:, s, 1, :], in0=modraw[:, 16:32, s], scalar=1.0, in1=pv[:, 0:16], op0=ALU.add, op1=ALU.mult), reads=[mrB, pvB], writes=modB)
            P.op("dve", lambda e, s=s: e.tensor_tensor(out=modv[:, s, 2, :], in0=modraw[:, 32:48, s], in1=pv[:, 16:32], op=ALU.mult), reads=[mrB, pvB], writes=modB)
            P.op("dve", lambda e, s=s: e.tensor_copy(out=modv[:, s, 3, :], in_=modraw[:, 48:64, s]), reads=mrB, writes=modB)
            P.op("dve", lambda e, s=s: e.scalar_tensor_tensor(out=modv[:, s, 4, :], in0=modraw[:, 64:80, s], scalar=1.0, in1=pv[:, 32:48], op0=ALU.add, op1=ALU.mult), reads=[mrB, pvB], writes=modB)
            P.op("dve", lambda e, s=s: e.tensor_tensor(out=modv[:, s, 5, :], in0=modraw[:, 80:96, s], in1=pv[:, 48:64], op=ALU.mult), reads=[mrB, pvB], writes=modB)
        P.barrier()

    for s in range(NS):
        T = TS[s]
        ntile = T // NT
        xr = G["xres"][s].rearrange("(c p) t -> p c t", p=128)
        hf = G["hfm"][s].rearrange("(c p) t -> p c t", p=128)
        with ES() as es:
            xt = [sb(P, es, "xt", [128, KC, NT], F32) for _ in range(2)]
            xB = [Buf(), Buf()]
            sq = sb(P, es, "sq", [128, KC, NT], BF16)
            sqB = Buf()
            sd = sb(P, es, "sd", [128, NT], F32)
            rstd = sb(P, es, "rstd", [128, NT], F32)
            rsB = Buf()
            tmp = [sb(P, es, "tmp", [128, NT], F32) for _ in range(2)]
            tB = [Buf(), Buf()]
            h = [sb(P, es, "h", [128, KC, NT], BF16) for _ in range(2)]
            hB = [Buf(), Buf()]
            for ti in range(ntile):
                k = ti % 2
                P.dma("sp", xt[k][:], xr[:, :, ti * NT:(ti + 1) * NT], writes=xB[k])
                rms_rstd(P, G, xt[k][:], xB[k], KC, sq, sqB, sd, rstd, rsB, D, EPS)
                for c in range(KC):
                    j = c % 2
                    P.op("dve", lambda e, c=c, j=j, k=k: e.scalar_tensor_tensor(out=tmp[j][:], in0=xt[k][:, c, :], scalar=modv[:, s, 1, c:c + 1], in1=rstd[:], op0=ALU.mult, op1=ALU.mult),
                         reads=[xB[k], rsB, modB], writes=tB[j])
                    P.op("act", lambda e, c=c, j=j, k=k: e.activation(out=h[k][:, c, :], in_=tmp[j][:], func=AF.Identity, bias=modv[:, s, 0, c:c + 1], scale=1.0),
                         reads=[tB[j], modB], writes=hB[k])
                P.dma("pool", hf[:, :, 1 + ti * NT:1 + (ti + 1) * NT], h[k][:], reads=hB[k])
            P.barrier()
        if stop_after == "A1":
            continue

        with ES() as es:
            w32 = sb(P, es, "w32", [128, KC, NLORA], F32)
            p32 = sb(P, es, "p32", [128, KC, NLORA], F32)
            n32 = sb(P, es, "n32", [128, KC, NLORA], F32)
            Wc = sb(P, es, "Wc", [128, KC, NLORA], BF16)
            Wp = sb(P, es, "Wp", [128, KC, NLORA], BF16)
            Wn = sb(P, es, "Wn", [128, KC, NLORA], BF16)
            lB = Buf()
            P.dma("sp", w32[:], G["lora_dn"][l], writes=lB)
            P.dma("sp", p32[:], G["lora_mp"][l], writes=lB)
            P.dma("sp", n32[:], G["lora_mn"][l], writes=lB)
            P.op("dve", lambda e: e.tensor_tensor(out=p32[:], in0=p32[:], in1=w32[:], op=ALU.mult), reads=lB, writes=lB)
            P.op("dve", lambda e: e.tensor_tensor(out=n32[:], in0=n32[:], in1=w32[:], op=ALU.mult), reads=lB, writes=lB)
            P.op("dve", lambda e: e.tensor_tensor(out=w32[:], in0=w32[:], in1=p32[:], op=ALU.subtract), reads=lB, writes=lB)
            P.op("dve", lambda e: e.tensor_tensor(out=w32[:], in0=w32[:], in1=n32[:], op=ALU.subtract), reads=lB, writes=lB)
            P.op("dve", lambda e: e.tensor_copy(out=Wc[:], in_=w32[:]), reads=lB, writes=lB)
            P.op("dve", lambda e: e.tensor_copy(out=Wp[:], in_=p32[:]), reads=lB, writes=lB)
            P.op("dve", lambda e: e.tensor_copy(out=Wn[:], in_=n32[:]), reads=lB, writes=lB)
            hext = [sb(P, es, "hext", [128, KC, NT + 2], BF16) for _ in range(2)]
            hxB = [Buf(), Buf()]
            wsl = [sb(P, es, "wsl", [128, KC, 512], BF16) for _ in range(2)]
            wB = [Buf(), Buf()]
            ot = [sb(P, es, "ot", [128, 4, NT], BF16) for _ in range(2)]
            otB = [Buf(), Buf()]
            of = [sb(P, es, "of", [128, 4, NT], F32) for _ in range(2)]
            ofB = [Buf(), Buf()]
            vt = [sb(P, es, "vt", [128, 512], BF16) for _ in range(2)]
            vtB = [Buf(), Buf()]
            lo = [sb(P, es, "lo", [128, NT], BF16) for _ in range(3)]
            loB = [Buf(), Buf(), Buf()]
            wcount = 0
            for ti in range(ntile):
                k = ti % 2
                tsl = slice(ti * NT, (ti + 1) * NT)
                P.dma("sp", hext[k][:], hf[:, :, ti * NT:ti * NT + NT + 2], writes=hxB[k])
                for og, (c0, c1) in enumerate(((0, 128), (128, 256), (256, 352))):
                    M = c1 - c0
                    ps, pb = P.psum()

                    def f(e, ps=ps, c0=c0, c1=c1, M=M, k=k):
                        n = 0
                        for W, off in ((Wc, 1), (Wp, 0), (Wn, 2)):
                            for kc in range(KC):
                                ins = e.matmul(ps[0:M, :], lhsT=W[:, kc, c0:c1], rhs=hext[k][:, kc, off:off + NT], start=(n == 0), stop=(n == 3 * KC - 1))
                                n += 1
                        return ins
                    P.op("pe", f, reads=[lB, hxB[k]], writes=pb)
                    if og == 0:
                        P.op("act", lambda e, ps=ps: e.activation(out=lo[0][:], in_=ps[:], func=AF.Tanh), reads=pb, writes=loB[0])
                        P.dma("pool", G["lw_d"][s][:, tsl], lo[0][:], reads=loB[0])
                    elif og == 1:
                        P.op("dve", lambda e, ps=ps: e.tensor_copy(out=lo[1][:], in_=ps[:]), reads=pb, writes=loB[1])
                        P.dma("pool", G["la_d"][s][:, tsl], lo[1][:], reads=loB[1])
                    else:
                        P.op("act", lambda e, ps=ps: e.activation(out=lo[2][0:64, :], in_=ps[0:64, :], func=AF.Sigmoid), reads=pb, writes=loB[2])
                        P.op("dve", lambda e, ps=ps: e.tensor_copy(out=lo[2][64:96, :], in_=ps[64:96, :]), reads=pb, writes=loB[2])
                        P.dma("pool", G["lg_d"][s][:, tsl], lo[2][0:64, :], reads=loB[2])
                        P.dma("pool", G["lv_d"][s][:, tsl], lo[2][64:96, :], reads=loB[2])
                for gi in range(12):
                    wk = wcount % 2
                    wcount += 1
                    P.dma("sp", wsl[wk][:], G["win"][l, gi], writes=wB[wk])
                    if gi in (4, 5):
                        for tsub in range(4):
                            ps, pb = P.psum()

                            def f(e, ps=ps, tsub=tsub, wk=wk, k=k):
                                for kc in range(KC):
                                    ins = e.matmul(ps[:], lhsT=hext[k][:, kc, 1 + tsub * 128:1 + (tsub + 1) * 128], rhs=wsl[wk][:, kc, :], start=(kc == 0), stop=(kc == KC - 1))
                                return ins
                            P.op("pe", f, reads=[wB[wk], hxB[k]], writes=pb)
                            j = tsub % 2
                            P.op("act", lambda e, ps=ps, j=j: e.copy(out=vt[j][:], in_=ps[:]), reads=pb, writes=vtB[j])
                            r0 = ti * NT + tsub * 128
                            P.dma("pool", G["vtm"][s][r0:r0 + 128, (gi - 4) * 512:(gi - 3) * 512], vt[j][:], reads=vtB[j])
                    else:
                        j = gi % 2
                        for m in range(4):
                            ps, pb = P.psum()

                            def f(e, ps=ps, m=m, wk=wk, k=k):
                                for kc in range(KC):
                                    ins = e.matmul(ps[:], lhsT=wsl[wk][:, kc, m * 128:(m + 1) * 128], rhs=hext[k][:, kc, 1:NT + 1], start=(kc == 0), stop=(kc == KC - 1))
                                return ins
                            P.op("pe", f, reads=[wB[wk], hxB[k]], writes=pb)
                            dst, dB = (ot[j], otB[j]) if gi < 4 else (of[j], ofB[j])
                            if P.ev_eng() == "act":
                                P.op("act", lambda e, ps=ps, m=m, dst=dst: e.copy(out=dst[:, m, :], in_=ps[:]), reads=pb, writes=dB)
                            else:
                                P.op("dve", lambda e, ps=ps, m=m, dst=dst: e.tensor_copy(out=dst[:, m, :], in_=ps[:]), reads=pb, writes=dB)
                        if gi < 4:
                            dd = (G["qfm"] if gi < 2 else G["kfm"])[s].rearrange("(c p) t -> p c t", p=128)
                            P.dma("pool", dd[:, (gi % 2) * 4:(gi % 2) * 4 + 4, tsl], ot[j][:], reads=otB[j])
                        else:
                            dd = G["rkv"][s].rearrange("(c p) t -> p c t", p=128)
                            P.dma("pool", dd[:, (gi - 6) * 4:(gi - 6) * 4 + 4, 1 + ti * NT:1 + (ti + 1) * NT], of[j][:], reads=ofB[j])
            P.barrier()
        if stop_after == "A2":
            continue
        attention(P, cfg, l, s, G)
        if stop_after == "B1":
            continue
        for d in range(2):
            scan_pass(P, cfg, l, s, d, G)
        if stop_after == "B2":
            continue
        scan_post(P, cfg, l, s, G)
        if stop_after == "B3":
            continue
        ffn_phase(P, cfg, l, s, G)


def ffn_phase(P, cfg, l, s, G):
    ES = contextlib.ExitStack
    TS = cfg["seqs"]
    T = TS[s]
    ntile = T // NT
    pv, modv, cB, pvB, modB = G["pv"], G["modv"], G["cB"], G["pvB"], G["modB"]
    xr = G["xres"][s].rearrange("(c p) t -> p c t", p=128)
    mx = G["mixin"][s].rearrange("(c p) t -> p c t", p=128)
    with ES() as es:
        xt = sb(P, es, "xt", [128, KC, NT], F32)
        xB = Buf()
        mf = sb(P, es, "mf", [128, KC, NT], F32)
        mfB = Buf()
        hb = sb(P, es, "hb", [128, KC, NT], BF16)
        hbB = Buf()
        hid = sb(P, es, "hid", [128, 32, NT], BF16)
        hidB = Buf()
        sq = sb(P, es, "sq", [128, KC, NT], BF16)
        sqB = Buf()
        sd = sb(P, es, "sd", [128, NT], F32)
        rstd = sb(P, es, "rstd", [128, NT], F32)
        rsB = Buf()
        tmp = [sb(P, es, "tmp", [128, NT], F32) for _ in range(2)]
        tB = [Buf(), Buf()]
        wsl = [sb(P, es, "wsl", [128, KC * 512], BF16) for _ in range(2)]
        wB = [Buf(), Buf()]
        gs = sb(P, es, "gs", [128, 8], F32)
        wc = 0
        for ti in range(ntile):
            tsl = slice(ti * NT, (ti + 1) * NT)
            P.dma("sp", xt[:], xr[:, :, tsl], writes=xB)
            P.dma("sp", hb[:], mx[:, :, tsl], writes=hbB)
            rms_rstd(P, G, hb[:, 0:8, :], hbB, 8, sq, sqB, sd, rstd, rsB, 1024, EPS)
            for c in range(8):
                P.op("dve", lambda e, c=c: e.scalar_tensor_tensor(out=hb[:, c, :], in0=hb[:, c, :], scalar=pv[:, 160 + c:161 + c], in1=rstd[:], op0=ALU.mult, op1=ALU.mult),
                     reads=[rsB, pvB], writes=hbB)
            for gi in range(4):
                wk = wc % 2
                wc += 1
                wv = wsl[wk][:].rearrange("p (k n) -> p k n", n=512)
                P.dma("sp", wv, G["wout"][l, gi], writes=wB[wk])
                for m in range(4):
                    ps, pb = P.psum()

                    def f(e, ps=ps, m=m, wv=wv):
                        for kc in range(KC):
                            ins = e.matmul(ps[:], lhsT=wv[:, kc, m * 128:(m + 1) * 128], rhs=hb[:, kc, :], start=(kc == 0), stop=(kc == KC - 1))
                        return ins
                    P.op("pe", f, reads=[wB[wk], hbB], writes=pb)
                    c = gi * 4 + m
                    if P.ev_eng() == "act":
                        P.op("act", lambda e, ps=ps, c=c: e.copy(out=mf[:, c, :], in_=ps[:]), reads=pb, writes=mfB)
                    else:
                        P.op("dve", lambda e, ps=ps, c=c: e.tensor_copy(out=mf[:, c, :], in_=ps[:]), reads=pb, writes=mfB)
            rms_rstd(P, G, mf[:], mfB, KC, sq, sqB, sd, rstd, rsB, D, EPS)
            for c in range(KC):
                j = c % 2
                P.op("dve", lambda e, c=c, j=j: e.scalar_tensor_tensor(out=tmp[j][:], in0=mf[:, c, :], scalar=modv[:, s, 2, c:c + 1], in1=rstd[:], op0=ALU.mult, op1=ALU.mult),
                     reads=[mfB, rsB, modB], writes=tB[j])
                P.op("dve", lambda e, c=c, j=j: e.tensor_tensor(out=xt[:, c, :], in0=xt[:, c, :], in1=tmp[j][:], op=ALU.add), reads=[tB[j]], writes=xB)
            rms_rstd(P, G, xt[:], xB, KC, sq, sqB, sd, rstd, rsB, D, EPS)
            for c in range(KC):
                j = c % 2
                P.op("dve", lambda e, c=c, j=j: e.scalar_tensor_tensor(out=tmp[j][:], in0=xt[:, c, :], scalar=modv[:, s, 4, c:c + 1], in1=rstd[:], op0=ALU.mult, op1=ALU.mult),
                     reads=[xB, rsB, modB], writes=tB[j])
                P.op("act", lambda e, c=c, j=j: e.activation(out=hb[:, c, :], in_=tmp[j][:], func=AF.Identity, bias=modv[:, s, 3, c:c + 1], scale=1.0),
                     reads=[tB[j], modB], writes=hbB)
            for half in range(2):
                for g8 in range(8):
                    gi = half * 8 + g8
                    wk = wc % 2
                    wc += 1
                    wv = wsl[wk][:].rearrange("p (k n) -> p k n", n=512)
                    P.dma("sp", wv, G["wf1"][l, gi], writes=wB[wk])
                    for m in range(4):
                        ps, pb = P.psum()

                        def f(e, ps=ps, m=m, wv=wv):
                            for kc in range(KC):
                                ins = e.matmul(ps[:], lhsT=wv[:, kc, m * 128:(m + 1) * 128], rhs=hb[:, kc, :], start=(kc == 0), stop=(kc == KC - 1))
                            return ins
                        P.op("pe", f, reads=[wB[wk], hbB], writes=pb)
                        j = m % 2
                        c = g8 * 4 + m
                        P.op("act", lambda e, ps=ps, j=j: e.activation(out=tmp[j][:], in_=ps[:], func=AF.Relu), reads=pb, writes=tB[j])
                        P.op("dve", lambda e, c=c, j=j: e.tensor_tensor(out=hid[:, c, :], in0=tmp[j][:], in1=tmp[j][:], op=ALU.mult), reads=tB[j], writes=hidB)
                for mg2 in range(8):
                    wk = wc % 2
                    wc += 1
                    wv = wsl[wk][:].rearrange("p (g k n) -> p g k n", g=2, n=128)
                    P.dma("sp", wv, G["wf2"][l, half, 2 * mg2:2 * mg2 + 2].rearrange("g p k n -> p g k n"), writes=wB[wk])
                    for gg in range(2):
                        mg = 2 * mg2 + gg
                        ps, pb = P.psum()

                        def f(e, ps=ps, gg=gg, wv=wv):
                            for kc in range(32):
                                ins = e.matmul(ps[:], lhsT=wv[:, gg, kc, :], rhs=hid[:, kc, :], start=(kc == 0), stop=(kc == 31))
                            return ins
                        P.op("pe", f, reads=[wB[wk], hidB], writes=pb)
                        if half == 0:
                            P.op("act", lambda e, ps=ps, mg=mg: e.copy(out=mf[:, mg, :], in_=ps[:]), reads=pb, writes=mfB)
                        else:
                            P.op("dve", lambda e, ps=ps, mg=mg: e.tensor_tensor(out=mf[:, mg, :], in0=ps[:], in1=mf[:, mg, :], op=ALU.add), reads=pb, writes=mfB)
            rms_rstd(P, G, mf[:], mfB, KC, sq, sqB, sd, rstd, rsB, D, EPS)
            for c in range(KC):
                j = c % 2
                P.op("dve", lambda e, c=c, j=j: e.scalar_tensor_tensor(out=tmp[j][:], in0=mf[:, c, :], scalar=modv[:, s, 5, c:c + 1], in1=rstd[:], op0=ALU.mult, op1=ALU.mult),
                     reads=[mfB, rsB, modB], writes=tB[j])
                P.op("dve", lambda e, c=c, j=j: e.tensor_tensor(out=xt[:, c, :], in0=xt[:, c, :], in1=tmp[j][:], op=ALU.add), reads=[tB[j]], writes=xB)
            P.dma("pool", xr[:, :, tsl], xt[:], reads=xB)
        P.barrier()


def attention(P, cfg, l, s, G):
    ES = contextlib.ExitStack
    nc = P.nc
    T = cfg["seqs"][s]
    rows = T // GW
    nj = T // 128
    cB = G["cB"]
    identb = G["identb"]
    SCALE = 128 ** -0.5
    with ES() as es:
        master = sb(P, es, "master", [128, 8, 15 * 64], F32)
        mB = Buf()
        P.op("pool", lambda e: e.memset(master[:], NEG), writes=mB)
        m4 = master[:].rearrange("p h (j c) -> p h j c", c=64)
        with nc.allow_non_contiguous_dma(reason="rpb window"):
            for q in range(64):
                c0 = min(max(q - 8, 0), 48)
                a0 = c0 - q + 15
                for half in range(2):
                    p = half * 64 + q
                    P.dma("sp", m4[p:p + 1, :, :, c0:c0 + 16], G["rpb"][l:l + 1, :, :, a0:a0 + 16], writes=mB)
        Kh = sb(P, es, "Kh", [128, T], BF16)
        Qh = sb(P, es, "Qh", [128, T], BF16)
        V0 = sb(P, es, "V0", [128, nj, 128], BF16)
        V1 = sb(P, es, "V1", [128, nj, 128], BF16)
        atth = sb(P, es, "atth", [128, T], BF16)
        hdB = Buf()
        atB = Buf()
        S = [sb(P, es, "S", [128, 512], F32) for _ in range(4)]
        SB_ = [Buf() for _ in range(4)]
        Pm = [sb(P, es, "Pm", [128, 512], BF16) for _ in range(4)]
        PB_ = [Buf() for _ in range(4)]
        PT = [sb(P, es, "PT", [128, 4, 128], BF16) for _ in range(4)]
        PTB = [Buf() for _ in range(4)]
        st = [sb(P, es, "st", [128, 4], F32) for _ in range(4)]
        stB = [Buf() for _ in range(4)]
        for hd in range(8):
            P.dma("sp", Kh[:], G["kfm"][s][hd * 128:(hd + 1) * 128, :], writes=hdB)
            P.dma("sp", Qh[:], G["qfm"][s][hd * 128:(hd + 1) * 128, :], writes=hdB)
            P.dma("sp", V0[:], G["vtm"][s][0:T, hd * 128:(hd + 1) * 128].rearrange("(j p) d -> p j d", p=128), writes=hdB)
            P.dma("sp", V1[:, 0:nj - 1, :], G["vtm"][s][64:T - 64, hd * 128:(hd + 1) * 128].rearrange("(j p) d -> p j d", p=128), writes=hdB)
            for pb4 in range(0, rows // 2, 4):
                B4 = []
                for pi in range(pb4, min(pb4 + 4, rows // 2)):
                    k = pi % 4
                    rr_ = [2 * pi, 2 * pi + 1]
                    rs = [min(max(r - 4, 0), rows - 8) for r in rr_]
                    oo = [rs[i] - rr_[i] for i in range(2)]
                    B4.append((pi, k, rr_, rs, oo))
                PS1 = {}
                for (pi, k, rr_, rs, oo) in B4:
                    ps, pb = P.psum()
                    PS1[pi] = (ps, pb)

                    def f(e, ps=ps, rr_=rr_, rs=rs):
                        for i in range(2):
                            ins = e.matmul(ps[i * 64:(i + 1) * 64, :], lhsT=Qh[:, rr_[i] * 64:(rr_[i] + 1) * 64], rhs=Kh[:, rs[i] * 64:rs[i] * 64 + 512], start=True, stop=True)
                        return ins
                    P.op("pe", f, reads=hdB, writes=pb)
                for (pi, k, rr_, rs, oo) in B4:
                    ps, pb = PS1[pi]
                    if oo[0] == oo[1]:
                        P.op("dve", lambda e, o=oo[0]: e.scalar_tensor_tensor(out=S[k][:], in0=ps[:], scalar=SCALE, in1=master[:, hd, (o + 7) * 64:(o + 15) * 64], op0=ALU.mult, op1=ALU.add),
                             reads=[pb, mB], writes=SB_[k])
                    else:
                        for i in range(2):
                            P.op("dve", lambda e, o=oo[i], i=i: e.scalar_tensor_tensor(out=S[k][i * 64:(i + 1) * 64, :], in0=ps[i * 64:(i + 1) * 64, :], scalar=SCALE,
                                                                               in1=master[i * 64:(i + 1) * 64, hd, (o + 7) * 64:(o + 15) * 64], op0=ALU.mult, op1=ALU.add),
                                 reads=[pb, mB], writes=SB_[k])
                for (pi, k, rr_, rs, oo) in B4:
                    P.op("dve", lambda e: e.reduce_max(out=st[k][:, 0:1], in_=S[k][:], axis=AX.X), reads=SB_[k], writes=stB[k])
                    P.op("dve", lambda e: e.tensor_scalar(out=st[k][:, 1:2], in0=st[k][:, 0:1], scalar1=-1.0, scalar2=None, op0=ALU.mult), reads=stB[k], writes=stB[k])
                for (pi, k, rr_, rs, oo) in B4:
                    P.op("act", lambda e: e.activation(out=Pm[k][:], in_=S[k][:], func=AF.Exp, bias=st[k][:, 1:2], scale=1.0, accum_out=st[k][:, 2:3]), reads=[SB_[k], stB[k]], writes=[PB_[k], stB[k]])
                for (pi, k, rr_, rs, oo) in B4:
                    P.op("dve", lambda e: e.reciprocal(out=st[k][:, 3:4], in_=st[k][:, 2:3]), reads=stB[k], writes=stB[k])
                    P.op("dve", lambda e: e.tensor_scalar(out=Pm[k][:], in0=Pm[k][:], scalar1=st[k][:, 3:4], scalar2=None, op0=ALU.mult), reads=stB[k], writes=PB_[k])
                PS2 = {}
                for (pi, k, rr_, rs, oo) in B4:
                    ps2, pb2 = P.psum()
                    pst = ps2[:].bitcast(BF16)
                    PS2[pi] = (pst, pb2)

                    def f2(e, pst=pst, k=k):
                        for kc in range(4):
                            ins = e.transpose(out=pst[:, kc * 128:(kc + 1) * 128], in_=Pm[k][:, kc * 128:(kc + 1) * 128], identity=identb[:])
                        return ins
                    P.op("pe", f2, reads=[PB_[k], cB], writes=pb2)
                for (pi, k, rr_, rs, oo) in B4:
                    pst, pb2 = PS2[pi]
                    P.op("act", lambda e: e.copy(out=PT[k][:].rearrange("p a b -> p (a b)"), in_=pst[:, 0:512]), reads=pb2, writes=PTB[k])
                PS3 = {}
                for (pi, k, rr_, rs, oo) in B4:
                    ps3, pb3 = P.psum()
                    PS3[pi] = (ps3, pb3)

                    def f3(e, ps3=ps3, k=k, rs=rs):
                        for i in range(2):
                            Vx = V0 if rs[i] % 2 == 0 else V1
                            j0 = rs[i] // 2 if rs[i] % 2 == 0 else (rs[i] - 1) // 2
                            for kc in range(4):
                                ins = e.matmul(ps3[:, i * 64:(i + 1) * 64], lhsT=Vx[:, j0 + kc, :], rhs=PT[k][:, kc, i * 64:(i + 1) * 64], start=(kc == 0), stop=(kc == 3))
                        return ins
                    P.op("pe", f3, reads=[PTB[k], hdB], writes=pb3)
                for (pi, k, rr_, rs, oo) in B4:
                    ps3, pb3 = PS3[pi]
                    P.op("dve", lambda e: e.tensor_copy(out=atth[:, pi * 128:(pi + 1) * 128], in_=ps3[:, 0:128]), reads=pb3, writes=atB)
            P.dma("pool", G["mixin"][s][hd * 128:(hd + 1) * 128, :], atth[:], reads=atB)
        P.barrier()


def scan_pass(P, cfg, l, s, d, G):
    ES = contextlib.ExitStack
    nc = P.nc
    T = cfg["seqs"][s]
    ntile = T // NT
    pv, pvB, cB = G["pv"], G["pvB"], G["cB"]
    identb, blkb, maskb, masknb, keep = G["identb"], G["blkb"], G["maskb"], G["masknb"], G["keep"]
    rk3 = G["rkv"][s]
    with ES() as es:
        st32 = sb(P, es, "st32", [65, 1024], F32)
        w2b = sb(P, es, "w2b", [65, 1024], BF16)
        a2b = sb(P, es, "a2b", [65, 1024], BF16)
        v2b = sb(P, es, "v2b", [33, 1024], BF16)
        uB = Buf()
        sB_ = Buf()
        for src, dst, n in ((G["w2aug"][l, d], w2b, 65), (G["a2aug"][l, d], a2b, 65), (G["v2aug"][l], v2b, 33)):
            P.dma("sp", st32[0:n, :], src, writes=sB_)
            P.op("dve", lambda e, dst=dst, n=n: e.tensor_copy(out=dst[0:n, :], in_=st32[0:n, :]), reads=sB_, writes=[uB, sB_])
        coef0 = sb(P, es, "coef0", [128, 3, 8], F32)
        for i in range(3):
            a = 168 + (i * 2) * 8
            P.op("dve", lambda e, i=i, a=a: e.tensor_tensor(out=coef0[:, i, :], in0=pv[:, a:a + 8], in1=pv[:, a + 8:a + 16], op=ALU.add), reads=pvB, writes=uB)
            P.op("dve", lambda e, i=i: e.tensor_scalar(out=coef0[:, i, :], in0=coef0[:, i, :], scalar1=-1.0, scalar2=1.0, op0=ALU.mult, op1=ALU.add), reads=uB, writes=uB)
        lwa = [sb(P, es, "lwa", [65, NT], BF16) for _ in range(2)]
        laa = [sb(P, es, "laa", [65, NT], BF16) for _ in range(2)]
        lva = [sb(P, es, "lva", [33, NT], BF16) for _ in range(2)]
        ldB = [Buf(), Buf()]
        for k in range(2):
            for t_ in (lwa[k], laa[k], lva[k]):
                P.op("dve", lambda e, t_=t_: e.memset(t_[:], 1.0), writes=ldB[k])
        raw = sb(P, es, "raw", [128, 3, NT + 2], F32)
        rawB = Buf()
        f32n = ["rr", "kv", "vv", "sg", "vf", "lw", "Lf", "Linc", "Lexc", "Wt", "Wex", "Winv", "asg", "kx", "t0", "kkn", "kd", "bb"]
        F = {n: sb(P, es, n, [128, NT], F32) for n in f32n}
        FB = {n: Buf() for n in f32n}
        sqk = sb(P, es, "sqk", [128, NT], BF16)
        sqkB = Buf()
        AR = [sb(P, es, "AR", [128, 8, 256], BF16) for _ in range(4)]
        Bbd = [sb(P, es, "Bbd", [128, 8, 128], BF16) for _ in range(4)]
        Kbd = [sb(P, es, "Kbd", [128, 8, 128], BF16) for _ in range(4)]
        Vbd = [sb(P, es, "Vbd", [128, 8, 128], BF16) for _ in range(4)]
        bon = [sb(P, es, "bon", [128, NT], F32) for _ in range(4)]
        WC = [sb(P, es, "WC", [128, 8], F32) for _ in range(4)]
        yt = [sb(P, es, "yt", [128, NT], F32) for _ in range(4)]
        opB = [Buf() for _ in range(4)]
        ytB = [Buf() for _ in range(4)]
        TT = [sb(P, es, "TT", [128, 512], BF16) for _ in range(4)]
        AM = [sb(P, es, "AM", [128, 512], BF16) for _ in range(4)]
        Nn = [sb(P, es, "Nn", [128, 128], BF16) for _ in range(4)]
        X = [[sb(P, es, "X", [128, 256], BF16) for _ in range(2)] for _ in range(4)]
        QQ = [[sb(P, es, "QQ", [128, 256], BF16) for _ in range(2)] for _ in range(4)]
        GY = [sb(P, es, "GY", [128, 256], BF16) for _ in range(4)]
        Xf = [[sb(P, es, "Xf", [128, 256], F32) for _ in range(2)] for _ in range(4)]
        QQf = [[sb(P, es, "QQf", [128, 256], F32) for _ in range(2)] for _ in range(4)]
        Pf = [sb(P, es, "Pf", [128, 256], F32) for _ in range(4)]
        maskPf, maskNf = G["maskPf"], G["maskNf"]
        uT = [{n: Buf() for n in ("TT", "AM", "Nn", "X0", "X1", "Q0", "Q1", "GY", "Pf", "G")} for _ in range(4)]
        Mst = [[sb(P, es, "M", [128, 128], BF16) for _ in range(2)] for _ in range(8)]
        MB = [[Buf(), Buf()] for _ in range(8)]
        mcur = [0] * 8
        for hq in range(4):
            for t_ in (AR[hq], Bbd[hq], Kbd[hq], Vbd[hq]):
                P.op("pool", lambda e, t_=t_: e.memset(t_[:], 0.0), writes=opB[hq])
        for hp in range(8):
            P.op("pool", lambda e, hp=hp: e.memset(Mst[hp][0][:], 0.0), writes=MB[hp][0])

        def v3(ap):
            return ap.rearrange("p (c t) -> p c t", t=64)

        tiles = list(range(ntile)) if d == 0 else list(range(ntile - 1, -1, -1))
        chunks = list(range(8)) if d == 0 else list(range(7, -1, -1))
        for tn, ti in enumerate(tiles):
            tsl = slice(ti * NT, (ti + 1) * NT)
            lk = tn % 2
            P.dma("sp", lwa[lk][0:64, :], G["lw_d"][s][d * 64:(d + 1) * 64, tsl], writes=ldB[lk])
            P.dma("sp", laa[lk][0:64, :], G["la_d"][s][d * 64:(d + 1) * 64, tsl], writes=ldB[lk])
            P.dma("sp", lva[lk][0:32, :], G["lv_d"][s][:, tsl], writes=ldB[lk])
            for hpg in range(2):
                for hq in range(4):
                    hp = hpg * 4 + hq
                    hs = slice(hp * 128, (hp + 1) * 128)
                    P.dma("sp", raw[:], rk3.rearrange("(i c p) t -> p i c t", i=3, p=128)[:, :, hp, ti * NT:ti * NT + NT + 2], writes=rawB)
                    for i, nm in enumerate(("rr", "kv", "vv")):
                        o = F[nm]
                        a = 168 + (i * 2) * 8 + hp
                        P.op("dve", lambda e, i=i, o=o: e.tensor_scalar(out=o[:], in0=raw[:, i, 1:NT + 1], scalar1=coef0[:, i, hp:hp + 1], scalar2=None, op0=ALU.mult), reads=[rawB, uB], writes=FB[nm])
                        P.op("dve", lambda e, i=i, o=o, a=a: e.scalar_tensor_tensor(out=o[:], in0=raw[:, i, 0:NT], scalar=pv[:, a:a + 1], in1=o[:], op0=ALU.mult, op1=ALU.add), reads=[rawB, pvB], writes=FB[nm])
                        P.op("dve", lambda e, i=i, o=o, a=a: e.scalar_tensor_tensor(out=o[:], in0=raw[:, i, 2:NT + 2], scalar=pv[:, a + 8:a + 9], in1=o[:], op0=ALU.mult, op1=ALU.add), reads=[rawB, pvB], writes=FB[nm])
                    if l > 0:
                        ps, pb = P.psum()
                        P.op("pe", lambda e, ps=ps: e.matmul(ps[:], lhsT=v2b[:, hs], rhs=lva[lk][:], start=True, stop=True), reads=[uB, ldB[lk]], writes=pb)
                        P.op("act", lambda e, ps=ps: e.activation(out=F["sg"][:], in_=ps[:], func=AF.Sigmoid), reads=pb, writes=FB["sg"])
                        P.dma("sp", F["vf"][:], G["vfirst"][s][hs, tsl], writes=FB["vf"])
                        P.op("dve", lambda e: e.tensor_tensor(out=F["vf"][:], in0=F["vf"][:], in1=F["vv"][:], op=ALU.subtract), reads=FB["vv"], writes=FB["vf"])
                        P.op("dve", lambda e: e.tensor_tensor(out=F["vf"][:], in0=F["vf"][:], in1=F["sg"][:], op=ALU.mult), reads=FB["sg"], writes=FB["vf"])
                        P.op("dve", lambda e: e.tensor_tensor(out=F["vv"][:], in0=F["vv"][:], in1=F["vf"][:], op=ALU.add), reads=FB["vf"], writes=FB["vv"])
                    elif d == 0:
                        P.dma("pool", G["vfirst"][s][hs, tsl], F["vv"][:], reads=FB["vv"])
                    ps, pb = P.psum()
                    P.op("pe", lambda e, ps=ps: e.matmul(ps[:], lhsT=w2b[:, hs], rhs=lwa[lk][:], start=True, stop=True), reads=[uB, ldB[lk]], writes=pb)
                    P.op("act", lambda e, ps=ps: e.activation(out=F["lw"][:], in_=ps[:], func=AF.Sigmoid), reads=pb, writes=FB["lw"])
                    P.op("dve", lambda e: e.tensor_scalar(out=F["lw"][:], in0=F["lw"][:], scalar1=-0.6065306597126334, scalar2=None, op0=ALU.mult), writes=FB["lw"])
                    P.op("dve", lambda e: e.tensor_tensor_scan(out=F["Lf"][:], data0=keep[:], data1=F["lw"][:], initial=0.0, op0=ALU.mult, op1=ALU.add), reads=[FB["lw"], cB], writes=FB["Lf"])
                    tot = v3(F["Lf"][:])[:, :, 63]
                    if d == 0:
                        Linc = F["Lf"]
                        LiB = FB["Lf"]
                        P.op("dve", lambda e: e.tensor_tensor(out=F["Lexc"][:], in0=F["Lf"][:], in1=F["lw"][:], op=ALU.subtract), reads=[FB["Lf"], FB["lw"]], writes=FB["Lexc"])
                    else:
                        Linc = F["Linc"]
                        LiB = FB["Linc"]
                        P.op("dve", lambda e: e.tensor_tensor(out=v3(F["Lexc"][:]), in0=tot.unsqueeze(2).to_broadcast([128, 8, 64]), in1=v3(F["Lf"][:]), op=ALU.subtract), reads=[FB["Lf"]], writes=FB["Lexc"])
                        P.op("dve", lambda e: e.tensor_tensor(out=F["Linc"][:], in0=F["Lexc"][:], in1=F["lw"][:], op=ALU.add), reads=[FB["Lexc"], FB["lw"]], writes=FB["Linc"])
                    P.op("act", lambda e: e.activation(out=F["Wt"][:], in_=Linc[:], func=AF.Exp), reads=LiB, writes=FB["Wt"])
                    P.op("act", lambda e: e.activation(out=F["Wex"][:], in_=F["Lexc"][:], func=AF.Exp), reads=FB["Lexc"], writes=FB["Wex"])
                    P.op("act", lambda e: e.activation(out=F["Winv"][:], in_=Linc[:], func=AF.Exp, scale=-1.0), reads=LiB, writes=FB["Winv"])
                    P.op("act", lambda e: e.activation(out=WC[hq][:], in_=tot, func=AF.Exp), reads=FB["Lf"], writes=opB[hq])
                    ps, pb = P.psum()
                    P.op("pe", lambda e, ps=ps: e.matmul(ps[:], lhsT=a2b[:, hs], rhs=laa[lk][:], start=True, stop=True), reads=[uB, ldB[lk]], writes=pb)
                    P.op("act", lambda e, ps=ps: e.activation(out=F["asg"][:], in_=ps[:], func=AF.Sigmoid), reads=pb, writes=FB["asg"])
                    P.op("dve", lambda e: e.tensor_scalar(out=F["kx"][:], in0=F["kv"][:], scalar1=pv[:, 216 + hp:217 + hp], scalar2=None, op0=ALU.mult), reads=[FB["kv"], pvB], writes=FB["kx"])
                    P.op("dve", lambda e: e.tensor_tensor(out=sqk[:], in0=F["kx"][:], in1=F["kx"][:], op=ALU.mult), reads=FB["kx"], writes=sqkB)
                    ps, pb = P.psum()
                    P.op("pe", lambda e, ps=ps: e.matmul(ps[:], lhsT=blkb[:], rhs=sqk[:], start=True, stop=True), reads=[sqkB, cB], writes=pb)
                    P.op("dve", lambda e, ps=ps: e.tensor_scalar(out=F["t0"][:], in0=ps[:], scalar1=1e-24, scalar2=None, op0=ALU.max), reads=pb, writes=FB["t0"])
                    P.op("act", lambda e: e.activation(out=F["t0"][:], in_=F["t0"][:], func=AF.Sqrt), reads=FB["t0"], writes=FB["t0"])
                    P.op("dve", lambda e: e.reciprocal(out=F["t0"][:], in_=F["t0"][:]), reads=FB["t0"], writes=FB["t0"])
                    P.op("dve", lambda e: e.tensor_tensor(out=F["kkn"][:], in0=F["kx"][:], in1=F["t0"][:], op=ALU.mult), reads=[FB["kx"], FB["t0"]], writes=FB["kkn"])
                    P.op("dve", lambda e: e.tensor_scalar(out=F["kd"][:], in0=F["asg"][:], scalar1=-1.0, scalar2=pv[:, 224 + hp:225 + hp], op0=ALU.add, op1=ALU.mult), reads=[FB["asg"], pvB], writes=FB["kd"])
                    P.op("dve", lambda e: e.scalar_tensor_tensor(out=F["kd"][:], in0=F["kd"][:], scalar=1.0, in1=F["kv"][:], op0=ALU.add, op1=ALU.mult), reads=FB["kv"], writes=FB["kd"])
                    P.op("dve", lambda e: e.tensor_tensor(out=F["bb"][:], in0=F["kkn"][:], in1=F["asg"][:], op=ALU.mult), reads=[FB["kkn"], FB["asg"]], writes=FB["bb"])
                    P.op("dve", lambda e: e.tensor_tensor(out=F["t0"][:], in0=F["rr"][:], in1=F["kd"][:], op=ALU.mult), reads=[FB["rr"], FB["kd"], FB["kkn"]], writes=FB["t0"])
                    P.op("dve", lambda e: e.tensor_scalar(out=sqk[:], in0=F["t0"][:], scalar1=pv[:, 232 + hp:233 + hp], scalar2=None, op0=ALU.mult), reads=[FB["t0"], pvB], writes=sqkB)
                    ps, pb = P.psum()
                    P.op("pe", lambda e, ps=ps: e.matmul(ps[:], lhsT=blkb[:], rhs=sqk[:], start=True, stop=True), reads=[sqkB, cB], writes=pb)
                    P.op("dve", lambda e, ps=ps: e.tensor_tensor(out=bon[hq][:], in0=ps[:], in1=F["vv"][:], op=ALU.mult), reads=[pb, FB["vv"]], writes=opB[hq])
                    for hh in range(2):
                        hsl = slice(hh * 64, (hh + 1) * 64)
                        csl = slice(hh * 64, (hh + 1) * 64)
                        P.op("dve", lambda e, hsl=hsl, csl=csl: e.scalar_tensor_tensor(out=AR[hq][hsl, :, csl], in0=v3(F["kkn"][hsl, :]), scalar=-1.0, in1=v3(F["Wex"][hsl, :]), op0=ALU.mult, op1=ALU.mult),
                             reads=[FB["kkn"], FB["Wex"]], writes=opB[hq])
                        P.op("dve", lambda e, hsl=hsl, hh=hh: e.tensor_tensor(out=AR[hq][hsl, :, 128 + hh * 64:128 + (hh + 1) * 64], in0=v3(F["rr"][hsl, :]), in1=v3(F["Wt"][hsl, :]), op=ALU.mult),
                             reads=[FB["rr"], FB["Wt"]], writes=opB[hq])
                        P.op("dve", lambda e, hsl=hsl, csl=csl: e.tensor_tensor(out=Bbd[hq][hsl, :, csl], in0=v3(F["bb"][hsl, :]), in1=v3(F["Winv"][hsl, :]), op=ALU.mult),
                             reads=[FB["bb"], FB["Winv"]], writes=opB[hq])
                        P.op("dve", lambda e, hsl=hsl, csl=csl: e.tensor_tensor(out=Kbd[hq][hsl, :, csl], in0=v3(F["kd"][hsl, :]), in1=v3(F["Winv"][hsl, :]), op=ALU.mult),
                             reads=[FB["kd"], FB["Winv"]], writes=opB[hq])
                        P.op("act", lambda e, hsl=hsl, csl=csl: e.copy(out=Vbd[hq][hsl, :, csl], in_=v3(F["vv"][hsl, :])), reads=[FB["vv"]], writes=opB[hq])
                for ci in ([] if cfg.get("no_units") else chunks):
                    cs_ = slice(ci * 64, (ci + 1) * 64)
                    for hq in range(4):
                        u = uT[hq]
                        ps, pb = P.psum()
                        pst = ps[:].bitcast(BF16)

                        def f(e, pst=pst, hq=hq):
                            e.transpose(out=pst[:, 0:128], in_=Bbd[hq][:, ci, :], identity=identb[:])
                            e.transpose(out=pst[:, 128:256], in_=Kbd[hq][:, ci, :], identity=identb[:])
                            e.transpose(out=pst[:, 256:384], in_=Vbd[hq][:, ci, :], identity=identb[:])
                            return e.transpose(out=pst[:, 384:512], in_=AR[hq][:, ci, 0:128], identity=identb[:])
                        P.op("pe", f, reads=[opB[hq], cB], writes=pb)
                        P.op("act", lambda e, pst=pst, hq=hq: e.copy(out=TT[hq][:, 0:384], in_=pst[:, 0:384]), reads=pb, writes=u["TT"])
                        P.op("act", lambda e, pst=pst, hq=hq: e.copy(out=Xf[hq][0][:, 0:128], in_=pst[:, 384:512]), reads=pb, writes=u["X0"])
                        if cfg.get("ustage", 99) < 0.5:
                            continue
                        ps, pb = P.psum()

                        def f(e, ps=ps, hq=hq):
                            e.matmul(ps[:, 0:256], lhsT=Bbd[hq][:, ci, :], rhs=AR[hq][:, ci, :], start=True, stop=True)
                            return e.matmul(ps[:, 256:512], lhsT=Kbd[hq][:, ci, :], rhs=AR[hq][:, ci, :], start=True, stop=True)
                        P.op("pe", f, reads=[opB[hq]], writes=pb)
                        P.op("dve", lambda e, ps=ps, hq=hq: e.tensor_tensor(out=AM[hq][:], in0=ps[:], in1=maskb[:, d, :], op=ALU.mult), reads=[pb, cB], writes=u["AM"])
                        P.op("dve", lambda e, ps=ps, hq=hq: e.tensor_tensor(out=Pf[hq][:, 0:128], in0=ps[:, 0:128], in1=maskPf[:, d, :], op=ALU.mult), reads=[pb, cB], writes=u["Pf"])
                        if cfg.get("ustage", 99) < 0.8:
                            continue
                        ps, pb = P.psum()
                        P.op("pe", lambda e, ps=ps, hq=hq: e.matmul(ps[:, 0:128], lhsT=AR[hq][:, ci, 0:128], rhs=Bbd[hq][:, ci, :], start=True, stop=True), reads=[opB[hq]], writes=pb)
                        P.op("dve", lambda e, ps=ps, hq=hq: e.tensor_tensor(out=Pf[hq][:, 128:256], in0=ps[:, 0:128], in1=maskNf[:, d, :], op=ALU.mult), reads=[pb, cB], writes=u["Pf"])
                    for hq in (range(4) if cfg.get("ustage", 99) >= 2 else []):
                        u = uT[hq]
                        ps, pb = P.psum()
                        P.op("pe", lambda e, ps=ps, hq=hq: e.matmul(ps[:, 0:128], lhsT=AM[hq][:, 256:384], rhs=TT[hq][:, 256:384], start=True, stop=True), reads=[u["AM"], u["TT"]], writes=pb)
                        P.op("act", lambda e, ps=ps, hq=hq: e.copy(out=Xf[hq][0][:, 128:256], in_=ps[:, 0:128]), reads=pb, writes=u["X0"])
                    Qs = [(Pf[hq][:, 0:128], Pf[hq][:, 128:256], [uT[hq]["Pf"]]) for hq in range(4)]
                    for k in (range(6) if cfg.get("ustage", 99) >= 3 else []):
                        cur = k % 2
                        for hq in range(4):
                            u = uT[hq]
                            Q, Qn, qB = Qs[hq]
                            ps, pb = P.psum()
                            P.op("pe", lambda e, ps=ps, hq=hq, Q=Q: e.matmul(ps[:, 0:256], lhsT=Q, rhs=Xf[hq][cur][:], start=True, stop=True), reads=[qB, u["X%d" % cur]], writes=pb)
                            P.op("dve", lambda e, ps=ps, hq=hq: e.tensor_tensor(out=Xf[hq][1 - cur][:], in0=ps[:, 0:256], in1=Xf[hq][cur][:].bitcast(F32), op=ALU.add), reads=[pb, u["X%d" % cur]], writes=u["X%d" % (1 - cur)])
                            if k < 5:
                                ps, pb = P.psum()

                                def f(e, ps=ps, Q=Q, Qn=Qn):
                                    e.matmul(ps[:, 0:128], lhsT=Qn, rhs=Q, start=True, stop=True)
                                    return e.matmul(ps[:, 128:256], lhsT=Q, rhs=Qn, start=True, stop=True)
                                P.op("pe", f, reads=[qB], writes=pb)
                                qn = "Q%d" % (k % 2)
                                P.op("act", lambda e, ps=ps, hq=hq: e.copy(out=QQf[hq][k % 2][:], in_=ps[:, 0:256]), reads=pb, writes=u[qn])
                                Qs[hq] = (QQf[hq][k % 2][:, 0:128], QQf[hq][k % 2][:, 128:256], [u[qn]])
                    for hq in range(4):
                        u = uT[hq]
                        P.op("act", lambda e, hq=hq: e.copy(out=X[hq][0][:], in_=Xf[hq][0][:].bitcast(F32)), reads=u["X0"], writes=u["G"])
                    for hq in (range(4) if cfg.get("ustage", 99) >= 4 else []):
                        u = uT[hq]
                        Gx = X[hq][0]
                        ps, pb = P.psum()

                        def f(e, ps=ps, hq=hq, Gx=Gx):
                            e.matmul(ps[:, 0:128], lhsT=Gx[:, 0:128], rhs=TT[hq][:, 0:128], start=True, stop=True)
                            return e.matmul(ps[:, 128:256], lhsT=Gx[:, 0:128], rhs=AM[hq][:, 128:256], start=True, stop=True)
                        P.op("pe", f, reads=[u["G"], u["TT"], u["AM"]], writes=pb)
                        P.op("dve", lambda e, ps=ps, hq=hq: e.tensor_tensor(out=GY[hq][:, 0:128], in0=ps[:, 0:128], in1=identb[:], op=ALU.add), reads=[pb, cB], writes=u["GY"])
                        P.op("dve", lambda e, ps=ps, hq=hq: e.tensor_tensor(out=GY[hq][:, 128:256], in0=ps[:, 128:256], in1=AR[hq][:, ci, 128:256], op=ALU.add), reads=[pb, opB[hq]], writes=u["GY"])
                    for hq in (range(4) if cfg.get("ustage", 99) >= 5 else []):
                        u = uT[hq]
                        hp = hpg * 4 + hq
                        Gx = X[hq][0]
                        mc = mcur[hp]
                        Mo = Mst[hp][mc]
                        ps, pb = P.psum()

                        def f(e, ps=ps, hq=hq, Gx=Gx, Mo=Mo):
                            e.matmul(ps[:, 0:128], lhsT=TT[hq][:, 0:128], rhs=Gx[:, 128:256], start=True, stop=False)
                            e.matmul(ps[:, 0:128], lhsT=TT[hq][:, 128:256], rhs=TT[hq][:, 256:384], start=False, stop=False)
                            e.matmul(ps[:, 0:128], lhsT=GY[hq][:, 0:128], rhs=Mo[:], start=False, stop=True)
                            e.matmul(ps[:, 128:256], lhsT=Gx[:, 128:256], rhs=AM[hq][:, 128:256], start=True, stop=False)
                            e.matmul(ps[:, 128:256], lhsT=TT[hq][:, 256:384], rhs=AM[hq][:, 384:512], start=False, stop=False)
                            return e.matmul(ps[:, 128:256], lhsT=Mo[:], rhs=GY[hq][:, 128:256], start=False, stop=True)
                        P.op("pe", f, reads=[u["G"], u["TT"], u["AM"], u["GY"], MB[hp][mc]], writes=pb)
                        P.op("act", lambda e, ps=ps, hq=hq, hp=hp, mc=mc: e.activation(out=Mst[hp][1 - mc][:], in_=ps[:, 0:128], func=AF.Identity, scale=WC[hq][:, ci:ci + 1]), reads=[pb, opB[hq]], writes=MB[hp][1 - mc])
                        mcur[hp] = 1 - mc
                        for hh in range(2):
                            hsl = slice(hh * 64, (hh + 1) * 64)
                            P.op("dve", lambda e, ps=ps, hq=hq, hsl=hsl, hh=hh: e.tensor_tensor(out=yt[hq][hsl, cs_], in0=ps[hsl, 128 + hh * 64:128 + (hh + 1) * 64], in1=bon[hq][hsl, cs_], op=ALU.add),
                                 reads=[pb, opB[hq]], writes=ytB[hq])
                for hq in range(4):
                    hp = hpg * 4 + hq
                    P.dma("pool", G["ysc"][s][d, hp * 128:(hp + 1) * 128, tsl], yt[hq][:], reads=ytB[hq])
        P.barrier()


def scan_post(P, cfg, l, s, G):
    ES = contextlib.ExitStack
    T = cfg["seqs"][s]
    ntile = T // NT
    pv, pvB, cB = G["pv"], G["pvB"], G["cB"]
    blkf = G["blkf"]
    with ES() as es:
        st32 = sb(P, es, "st32", [64, 1024], F32)
        g2b = sb(P, es, "g2b", [64, 1024], BF16)
        uB = Buf()
        P.dma("sp", st32[:], G["g2in"][l], writes=uB)
        P.op("dve", lambda e: e.tensor_copy(out=g2b[:], in_=st32[:]), reads=uB, writes=uB)
        lgt = [sb(P, es, "lgt", [64, NT], BF16) for _ in range(2)]
        lgB = [Buf(), Buf()]
        y0 = [sb(P, es, "y0", [128, NT], F32) for _ in range(4)]
        y1 = [sb(P, es, "y1", [128, NT], F32) for _ in range(4)]
        yB = [Buf() for _ in range(4)]
        ymL = [sb(P, es, "ym", [128, NT], F32) for _ in range(4)]
        sqL = [sb(P, es, "sq", [128, NT], F32) for _ in range(4)]
        sdL = [sb(P, es, "sd", [128, NT], F32) for _ in range(4)]
        tBL = [Buf() for _ in range(4)]
        ob = [sb(P, es, "ob", [128, NT], BF16) for _ in range(4)]
        obB = [Buf() for _ in range(4)]
        for ti in range(ntile):
            tsl = slice(ti * NT, (ti + 1) * NT)
            lk = ti % 2
            P.dma("sp", lgt[lk][:], G["lg_d"][s][:, tsl], writes=lgB[lk])
            for hb in (0, 4):
                HB = [(hp, hp % 4, slice(hp * 128, (hp + 1) * 128)) for hp in range(hb, hb + 4)]
                for (hp, k, hs) in HB:
                    P.dma("sp", y0[k][:], G["ysc"][s][0, hs, tsl], writes=yB[k])
                    P.dma("sp", y1[k][:], G["ysc"][s][1, hs, tsl], writes=yB[k])
                for (hp, k, hs) in HB:
                    P.op("dve", lambda e: e.tensor_tensor(out=y0[k][:], in0=y0[k][:], in1=y1[k][:], op=ALU.add), reads=yB[k], writes=yB[k])
                PS = {}
                for (hp, k, hs) in HB:
                    ps, pb = P.psum()
                    PS[hp] = (ps, pb)
                    P.op("pe", lambda e: e.matmul(ps[:], lhsT=blkf[:], rhs=y0[k][:], start=True, stop=True), reads=[yB[k], cB], writes=pb)
                for (hp, k, hs) in HB:
                    ps, pb = PS[hp]
                    P.op("dve", lambda e: e.scalar_tensor_tensor(out=ymL[k][:], in0=ps[:], scalar=-1.0 / 64, in1=y0[k][:], op0=ALU.mult, op1=ALU.add), reads=[pb, yB[k]], writes=tBL[k])
                    P.op("dve", lambda e: e.tensor_tensor(out=sqL[k][:], in0=ymL[k][:], in1=ymL[k][:], op=ALU.mult), reads=tBL[k], writes=tBL[k])
                for (hp, k, hs) in HB:
                    ps, pb = P.psum()
                    PS[hp] = (ps, pb)
                    P.op("pe", lambda e: e.matmul(ps[:], lhsT=blkf[:], rhs=sqL[k][:], start=True, stop=True), reads=[tBL[k], cB], writes=pb)
                for (hp, k, hs) in HB:
                    ps, pb = PS[hp]
                    P.op("act", lambda e: e.activation(out=sdL[k][:], in_=ps[:], func=AF.Sqrt, scale=1.0 / 64, bias=float(GN_EPS)), reads=pb, writes=tBL[k])
                for (hp, k, hs) in HB:
                    P.op("dve", lambda e: e.reciprocal(out=sdL[k][:], in_=sdL[k][:]), reads=tBL[k], writes=tBL[k])
                    P.op("dve", lambda e: e.tensor_tensor(out=ymL[k][:], in0=ymL[k][:], in1=sdL[k][:], op=ALU.mult), reads=tBL[k], writes=tBL[k])
                    P.op("dve", lambda e: e.tensor_scalar(out=ymL[k][:], in0=ymL[k][:], scalar1=pv[:, 240 + hp:241 + hp], scalar2=pv[:, 248 + hp:249 + hp], op0=ALU.mult, op1=ALU.add), reads=[tBL[k], pvB], writes=tBL[k])
                for (hp, k, hs) in HB:
                    ps, pb = P.psum()
                    PS[hp] = (ps, pb)
                    P.op("pe", lambda e: e.matmul(ps[:], lhsT=g2b[:, hs], rhs=lgt[lk][:], start=True, stop=True), reads=[uB, lgB[lk]], writes=pb)
                for (hp, k, hs) in HB:
                    ps, pb = PS[hp]
                    P.op("dve", lambda e: e.tensor_tensor(out=ob[k][:], in0=ps[:], in1=ymL[k][:], op=ALU.mult), reads=[pb, tBL[k]], writes=obB[k])
                    P.dma("pool", G["mixin"][s][1024 + hp * 128:1024 + (hp + 1) * 128, tsl], ob[k][:], reads=obB[k])
        P.barrier()


def fm(v):
    v = np.asarray(v, np.float32).reshape(-1, 128)
    return np.ascontiguousarray(v.T)


def tile_w(w, mw=512):
    K, N = w.shape
    return np.ascontiguousarray(w.reshape(K // 128, 128, N // mw, mw).transpose(2, 1, 0, 3))


def prep_shared(inp, L):
    out = {}
    out["win"] = np.stack([tile_w(inp["w_in"][l]) for l in range(L)])
    out["wout"] = np.stack([tile_w(inp["w_out"][l]) for l in range(L)])
    out["wf1"] = np.stack([tile_w(inp["w_ffn1"][l]) for l in range(L)])
    w2l = []
    for l in range(L):
        w = inp["w_ffn2"][l]
        t = w.reshape(2, 32, 128, 16, 128).transpose(0, 3, 2, 1, 4)
        w2l.append(np.ascontiguousarray(t))
    out["wf2"] = np.stack(w2l)
    out["wada"] = np.stack([tile_w(inp["w_ada"][l]) for l in range(L)])
    pvs = []
    for l in range(L):
        cols = [fm(inp["g_pre_mix"][l]), fm(inp["g_post_mix"][l]), fm(inp["g_pre_ffn"][l]), fm(inp["g_post_ffn"][l]),
                fm(inp["b_ada"][l]), fm(inp["g_att_out"][l])]
        for i in range(3):
            for j in range(2):
                cols.append(fm(inp["mu_rkv"][l, i, j]))
        cols += [fm(inp["k_k"][l]), fm(inp["k_a"][l]), fm(inp["r_k"][l].reshape(-1)), fm(inp["ln_x_w"][l]), fm(inp["ln_x_b"][l])]
        pvs.append(np.concatenate(cols, axis=1))
    out["pvec"] = np.stack(pvs).astype(np.float32)
    assert out["pvec"].shape[2] == NPV
    ld, mp, mn = [], [], []
    for l in range(L):
        z32 = np.zeros((D, 32), np.float32)
        v1 = inp["v1"][l - 1] if l > 0 else z32
        w = np.concatenate([inp["w1"][l, 0], inp["w1"][l, 1], inp["a1"][l, 0], inp["a1"][l, 1], inp["g1"][l], v1], axis=1)
        muv = inp["mu_v"][l - 1] if l > 0 else np.zeros((2, D), np.float32)

        def mub(j):
            return np.concatenate([np.repeat(inp["mu_x"][l, 0, j][:, None], 128, 1), np.repeat(inp["mu_x"][l, 1, j][:, None], 128, 1),
                                   np.repeat(inp["mu_x"][l, 2, j][:, None], 64, 1), np.repeat(muv[j][:, None], 32, 1)], axis=1)
        for arr, lst in ((w, ld), (mub(0), mp), (mub(1), mn)):
            lst.append(np.ascontiguousarray(arr.reshape(KC, 128, NLORA).transpose(1, 0, 2)))
    out["lora_dn"] = np.stack(ld).astype(np.float32)
    out["lora_mp"] = np.stack(mp).astype(np.float32)
    out["lora_mn"] = np.stack(mn).astype(np.float32)
    out["w2aug"] = np.stack([np.stack([np.concatenate([inp["w2"][l, d], inp["w0"][l, d][None]], 0) for d in range(2)]) for l in range(L)]).astype(np.float32)
    out["a2aug"] = np.stack([np.stack([np.concatenate([inp["a2"][l, d], inp["a0"][l, d][None]], 0) for d in range(2)]) for l in range(L)]).astype(np.float32)
    out["g2"] = np.ascontiguousarray(inp["g2"][:L]).astype(np.float32)
    v2l = []
    for l in range(L):
        if l == 0:
            v2l.append(np.zeros((33, 1024), np.float32))
        else:
            v2l.append(np.concatenate([inp["v2"][l - 1], inp["v0"][l - 1][None]], 0))
    out["v2aug"] = np.stack(v2l).astype(np.float32)
    out["rpb"] = np.ascontiguousarray(inp["rpb"][:L]).astype(np.float32)
    out["cident"] = np.eye(128, dtype=np.float32)
    blk = np.zeros((128, 128), np.float32)
    blk[:64, :64] = 1
    blk[64:, 64:] = 1
    out["cblk"] = blk
    t = np.arange(64)
    cm = np.zeros((2, 128, 512), np.float32)
    cn = np.zeros((2, 128, 128), np.float32)
    for d in range(2):
        st = (t[:, None] < t[None, :]) if d == 0 else (t[:, None] > t[None, :])
        inc = st | (t[:, None] == t[None, :])
        bst = np.zeros((128, 128), np.float32)
        binc = np.zeros((128, 128), np.float32)
        for h in range(2):
            bst[h * 64:(h + 1) * 64, h * 64:(h + 1) * 64] = st
            binc[h * 64:(h + 1) * 64, h * 64:(h + 1) * 64] = inc
        cm[d] = np.concatenate([bst, binc, bst, binc], axis=1)
        cn[d] = bst.T
    out["cmask"] = cm
    out["cmaskn"] = cn
    kp = np.ones((128, 512), np.float32)
    kp[:, ::64] = 0
    out["ckeep"] = kp
    return out


FULL_CFG = {"depth": 4, "seqs": [4096, 2048]}


def kernel(**inputs):
    inp = {k: np.asarray(v) for k, v in inputs.items()}
    cfg = FULL_CFG
    L = cfg["depth"]
    shared = prep_shared(inp, L)
    nc = build(cfg)
    in_maps = []
    for c in range(8):
        m = dict(shared)
        m["x0"] = np.ascontiguousarray(inp["x_prompt"][c])
        m["x1"] = np.ascontiguousarray(inp["x_sample"][c % 4])
        cc = np.stack([inp["c_prompt"][c], inp["c_sample"][c % 4]], axis=1)
        m["cvec"] = np.ascontiguousarray(cc.reshape(KC, 128, 2).transpose(1, 0, 2)).astype(np.float32)
        in_maps.append(m)
    res = run_bass_kernel_spmd(nc, in_maps, core_ids=list(range(8)))
    yp = np.stack([res.results[c]["y0"] for c in range(8)])
    ys = np.stack([res.results[c]["y1"] for c in range(4)])
    return (yp.astype(np.float32), ys.astype(np.float32))
```

```python
import contextlib
import numpy as np
import ml_dtypes
import concourse.bass as bass
import concourse.mybir as mybir
from concourse.bass_utils import run_bass_kernel_spmd

F32 = mybir.dt.float32
BF16 = mybir.dt.bfloat16
F32R = mybir.dt.float32r
ALU = mybir.AluOpType
AF = mybir.ActivationFunctionType
AX = mybir.AxisListType

D = 2048
KC = 16
NT = 512
GW = 64
NEG = -30000.0
EPS = 1e-6
GN_EPS = 64e-5
NPV = 256
NLORA = 352


class Sem:
    def __init__(self, nc, name):
        self.h = nc.alloc_semaphore(name=name)
        self.count = 0


class Buf:
    __slots__ = ("w", "r")

    def __init__(self):
        self.w = {}
        self.r = {}


def flat(x):
    if isinstance(x, Buf):
        yield x
    elif x is None:
        return
    else:
        for y in x:
            yield from flat(y)


class Prog:
    def __init__(self, nc):
        self.nc = nc
        self.eng = {"pe": nc.tensor, "dve": nc.vector, "act": nc.scalar, "pool": nc.gpsimd, "sp": nc.sync}
        self.esem = {k: Sem(nc, "e_" + k) for k in self.eng}
        self.obs = {k: {} for k in self.eng}
        self.dsems = {"sp": [Sem(nc, f"dsp{i}") for i in range(16)],
                      "pool": [Sem(nc, f"dpl{i}") for i in range(16)],
                      "act": [Sem(nc, f"dac{i}") for i in range(8)]}
        self.dnext = {"sp": 0, "pool": 0, "act": 0}
        self.uid = 0
        self.ps = []
        self.psb = []
        self.psn = 0
        self.rr = 0
        self.rec = None
        self.psp = 0
        self.nmain = 8

    def name(self, s):
        self.uid += 1
        return f"{s}_{self.uid}"

    def _wait(self, e, sem, val):
        if val <= 0:
            return
        o = self.obs[e]
        if o.get(sem, 0) >= val:
            return
        self.eng[e].wait_ge(sem.h, val)
        o[sem] = val

    def _deps(self, e, reads, writes):
        need = {}
        for b in flat(reads):
            for s, v in b.w.items():
                if need.get(s, 0) < v:
                    need[s] = v
        for b in flat(writes):
            for s, v in b.w.items():
                if need.get(s, 0) < v:
                    need[s] = v
            for s, v in b.r.items():
                if need.get(s, 0) < v:
                    need[s] = v
        for s, v in need.items():
            self._wait(e, s, v)

    def _mark(self, s, v, reads, writes):
        for b in flat(reads):
            if b.r.get(s, 0) < v:
                b.r[s] = v
        for b in flat(writes):
            b.w = {s: v}
            b.r = {}

    def replay(self, it):
        if it[0] == "op":
            self.op(it[1], it[2], it[3], it[4])
        else:
            self.dma(it[1], it[2], it[3], it[4], it[5], **it[6])

    def op(self, e, fn, reads=(), writes=()):
        if self.rec is not None:
            self.rec.append(("op", e, fn, reads, writes))
            return
        self._deps(e, reads, writes)
        ins = fn(self.eng[e])
        sem = self.esem[e]
        sem.count += 1
        ins.then_inc(sem.h, 1)
        self._mark(sem, sem.count, reads, writes)

    def dma(self, e, out, in_, reads=(), writes=(), **kw):
        if self.rec is not None:
            self.rec.append(("dma", e, out, in_, reads, writes, kw))
            return
        self._deps(e, reads, writes)
        lst = self.dsems[e]
        ds = lst[self.dnext[e]]
        self.dnext[e] = (self.dnext[e] + 1) % len(lst)
        self._wait(e, ds, 16 * ds.count)
        ds.count += 1
        self.eng[e].dma_start(out=out, in_=in_, **kw).then_inc(ds.h, 16)
        self._mark(ds, 16 * ds.count, reads, writes)

    def barrier(self):
        sems = list(self.esem.values())
        for lst in self.dsems.values():
            sems += lst
        for e in self.eng:
            for s in sems:
                if s in self.esem.values():
                    self._wait(e, s, s.count)
                else:
                    self._wait(e, s, 16 * s.count)

    def psum(self, pool=None):
        if pool == "prep":
            self.psp = (self.psp + 1) % 2
            i = 6 + self.psp
            return self.ps[i], self.psb[i]
        i = self.psn % self.nmain
        self.psn = (self.psn + 1) % self.nmain
        return self.ps[i], self.psb[i]

    def ev_eng(self):
        self.rr += 1
        return "act" if self.rr % 2 else "dve"


def sb(P, es, nm, shape, dt):
    return es.enter_context(P.nc.sbuf_tensor(P.name(nm), list(shape), dt))


def build(cfg):
    L = cfg["depth"]
    TS = cfg["seqs"]
    NS = len(TS)
    dbg = cfg.get("debug", False)
    nc = bass.Bass("TRN2", target_bir_lowering=False)
    P = Prog(nc)
    SK = "ExternalOutput" if dbg else "Internal"

    def din(name, shape, dt=F32):
        return nc.dram_tensor(name, list(shape), dt, kind="ExternalInput").ap()

    def dscr(name, shape, dt, kind=None):
        return nc.dram_tensor(name, list(shape), dt, kind=kind or SK).ap()

    x_in = [din(f"x{s}", [TS[s], D]) for s in range(NS)]
    y_out = [nc.dram_tensor(f"y{s}", [TS[s], D], F32, kind="ExternalOutput").ap() for s in range(NS)]
    cvec = din("cvec", [128, KC, NS])
    win32 = din("win", [L, 12, 128, KC, 512])
    wout32 = din("wout", [L, 4, 128, KC, 512])
    wf132 = din("wf1", [L, 16, 128, KC, 512])
    wf232 = din("wf2", [L, 2, 16, 128, 32, 128])
    wada32 = din("wada", [L, 24, 128, KC, 512])
    pvec = din("pvec", [L, 128, NPV])
    lora_dn = din("lora_dn", [L, 128, KC, NLORA])
    lora_mp = din("lora_mp", [L, 128, KC, NLORA])
    lora_mn = din("lora_mn", [L, 128, KC, NLORA])
    w2aug = din("w2aug", [L, 2, 65, 1024])
    a2aug = din("a2aug", [L, 2, 65, 1024])
    g2in = din("g2", [L, 64, 1024])
    v2aug = din("v2aug", [L, 33, 1024])
    rpb = din("rpb", [L, 8, 15, 31])
    cident = din("cident", [128, 128])
    cmask = din("cmask", [2, 128, 512])
    cmaskn = din("cmaskn", [2, 128, 128])
    cblk = din("cblk", [128, 128])
    ckeep = din("ckeep", [128, 512])

    win = dscr("win_b", [L, 12, 128, KC, 512], BF16, "Internal")
    wout = dscr("wout_b", [L, 4, 128, KC, 512], BF16, "Internal")
    wf1 = dscr("wf1_b", [L, 16, 128, KC, 512], BF16, "Internal")
    wf2 = dscr("wf2_b", [L, 2, 16, 128, 32, 128], BF16, "Internal")
    wada = dscr("wada_b", [L, 24, 128, KC, 512], BF16, "Internal")

    xres = [dscr(f"xres{s}", [D, TS[s]], F32) for s in range(NS)]
    hfm = [dscr(f"hfm{s}", [D, TS[s] + 2], BF16) for s in range(NS)]
    qfm = [dscr(f"qfm{s}", [1024, TS[s]], BF16) for s in range(NS)]
    kfm = [dscr(f"kfm{s}", [1024, TS[s]], BF16) for s in range(NS)]
    vtm = [dscr(f"vtm{s}", [TS[s] + 64, 1024], BF16) for s in range(NS)]
    rkv = [dscr(f"rkv{s}", [3072, TS[s] + 2], F32) for s in range(NS)]
    lw_d = [dscr(f"lw{s}", [128, TS[s]], BF16) for s in range(NS)]
    la_d = [dscr(f"la{s}", [128, TS[s]], BF16) for s in range(NS)]
    lg_d = [dscr(f"lg{s}", [64, TS[s]], BF16) for s in range(NS)]
    lv_d = [dscr(f"lv{s}", [32, TS[s]], BF16) for s in range(NS)]
    mixin = [dscr(f"mixin{s}", [D, TS[s]], BF16) for s in range(NS)]
    ysc = [dscr(f"ysc{s}", [2, 1024, TS[s]], F32) for s in range(NS)]
    vfirst = [dscr(f"vfirst{s}", [1024, TS[s]], F32) for s in range(NS)]

    with contextlib.ExitStack() as g:
        for i in range(8):
            P.ps.append(g.enter_context(nc.psum_tensor(f"ps{i}", [128, 512], F32)))
            P.psb.append(Buf())
        ident = sb(P, g, "ident", [128, 128], F32)
        identb = sb(P, g, "identb", [128, 128], BF16)
        onesb = sb(P, g, "onesb", [128, 128], BF16)
        blkb = sb(P, g, "blkb", [128, 128], BF16)
        blkf = sb(P, g, "blkf", [128, 128], F32)
        maskb = sb(P, g, "maskb", [128, 2, 512], BF16)
        masknb = sb(P, g, "masknb", [128, 2, 128], BF16)
        keep = sb(P, g, "keep", [128, 512], F32)
        maskPf = sb(P, g, "maskPf", [128, 2, 128], F32)
        maskNf = sb(P, g, "maskNf", [128, 2, 128], F32)
        onesf = sb(P, g, "onesf", [128, 512], F32)
        cs = sb(P, g, "cs", [128, KC, NS], BF16)
        pv = sb(P, g, "pv", [128, NPV], F32)
        modv = sb(P, g, "modv", [128, NS, 6, KC], F32)
        cB = Buf()
        pvB = Buf()
        modB = Buf()

        with contextlib.ExitStack() as es:
            t1 = sb(P, es, "t1", [128, 2, 512], F32)
            t2 = sb(P, es, "t2", [128, 2, 128], F32)
            t3 = sb(P, es, "t3", [128, 128], F32)
            cv = sb(P, es, "cv", [128, KC, NS], F32)
            tb = Buf()
            P.dma("sp", ident[:], cident[:, :], writes=cB)
            P.dma("sp", t1[:], cmask.rearrange("d p n -> p d n"), writes=tb)
            P.dma("sp", t2[:], cmaskn.rearrange("d p n -> p d n"), writes=tb)
            P.dma("sp", t3[:], cblk[:, :], writes=tb)
            P.dma("sp", blkf[:], cblk[:, :], writes=cB)
            P.dma("sp", keep[:], ckeep[:, :], writes=cB)
            P.dma("sp", cv[:], cvec[:, :, :], writes=tb)
            P.op("dve", lambda e: e.tensor_copy(out=identb[:], in_=ident[:]), reads=cB, writes=cB)
            P.op("dve", lambda e: e.tensor_copy(out=maskb[:], in_=t1[:]), reads=tb, writes=cB)
            P.op("dve", lambda e: e.tensor_copy(out=masknb[:], in_=t2[:]), reads=tb, writes=cB)
            P.op("dve", lambda e: e.tensor_copy(out=maskPf[:], in_=t1[:, :, 0:128]), reads=tb, writes=cB)
            P.op("dve", lambda e: e.tensor_copy(out=maskNf[:], in_=t2[:]), reads=tb, writes=cB)
            P.op("dve", lambda e: e.tensor_copy(out=blkb[:], in_=t3[:]), reads=tb, writes=cB)
            P.op("dve", lambda e: e.memset(onesb[:], 1.0), writes=cB)
            P.op("dve", lambda e: e.memset(onesf[:], 1.0), writes=cB)
            P.op("act", lambda e: e.activation(out=cs[:], in_=cv[:], func=AF.Silu), reads=tb, writes=cB)
            P.barrier()

        def conv(dst, src):
            n = 1
            for d_ in src.shape:
                n *= d_
            rows = n // 2048
            s2 = src.tensor.reshape([rows, 2048]) if False else None
            return n

        def conv2d(dst2, src2):
            rows = src2.shape[0]
            step = 2048
            for r0 in range(0, rows, step):
                r1 = min(rows, r0 + step)
                P.dma("pool", dst2[r0:r1, :], src2[r0:r1, :])

        for l in range(L):
            conv2d(win[l].rearrange("g p k (a n) -> (g p k a) n", n=512), win32[l].rearrange("g p k (a n) -> (g p k a) n", n=512))
            conv2d(wout[l].rearrange("g p k (a n) -> (g p k a) n", n=512), wout32[l].rearrange("g p k (a n) -> (g p k a) n", n=512))
            conv2d(wf1[l].rearrange("g p k (a n) -> (g p k a) n", n=512), wf132[l].rearrange("g p k (a n) -> (g p k a) n", n=512))
            conv2d(wf2[l].rearrange("h g p (k a) n -> (h g p k) (a n)", a=4), wf232[l].rearrange("h g p (k a) n -> (h g p k) (a n)", a=4))
            conv2d(wada[l].rearrange("g p k (a n) -> (g p k a) n", n=512), wada32[l].rearrange("g p k (a n) -> (g p k a) n", n=512))

        for s in range(NS):
            T = TS[s]
            xr = xres[s].rearrange("(c p) t -> p c t", p=128)
            with contextlib.ExitStack() as es:
                xt = [sb(P, es, "xt", [128, 4, D], F32) for _ in range(2)]
                xtB = [Buf(), Buf()]
                xf = [sb(P, es, "xf", [128, KC, NT], F32) for _ in range(2)]
                xfB = [Buf(), Buf()]
                for ti in range(T // NT):
                    k = ti % 2
                    P.dma("sp", xt[k][:], x_in[s][ti * NT:(ti + 1) * NT, :].rearrange("(j p) d -> p j d", p=128), writes=xtB[k])
                    for c in range(KC):
                        ps, pb = P.psum()

                        def f(e, ps=ps, c=c, k=k):
                            for j in range(4):
                                ins = e.transpose(out=ps[:, j * 128:(j + 1) * 128], in_=xt[k][:, j, c * 128:(c + 1) * 128], identity=ident[:])
                            return ins
                        P.op("pe", f, reads=[xtB[k], cB], writes=pb)
                        ee = P.ev_eng()
                        if ee == "act":
                            P.op("act", lambda e, ps=ps, c=c, k=k: e.copy(out=xf[k][:, c, :], in_=ps[:]), reads=pb, writes=xfB[k])
                        else:
                            P.op("dve", lambda e, ps=ps, c=c, k=k: e.tensor_copy(out=xf[k][:, c, :], in_=ps[:]), reads=pb, writes=xfB[k])
                    P.dma("pool", xr[:, :, ti * NT:(ti + 1) * NT], xf[k][:], reads=xfB[k])
                P.barrier()

        with contextlib.ExitStack() as es:
            zf = sb(P, es, "zf", [128, 24], F32)
            zb = sb(P, es, "zb", [128, 16], BF16)
            zB = Buf()
            P.op("dve", lambda e: e.memset(zf[:], 0.0), writes=zB)
            P.op("dve", lambda e: e.memset(zb[:], 0.0), writes=zB)
            for s in range(NS):
                T = TS[s]
                for col in (0, T + 1):
                    with nc.allow_non_contiguous_dma(reason="halo zero"):
                        P.dma("sp", hfm[s].rearrange("(c p) t -> p c t", p=128)[:, :, col:col + 1], zb[:].unsqueeze(2), reads=zB)
                        P.dma("sp", rkv[s].rearrange("(c p) t -> p c t", p=128)[:, :, col:col + 1], zf[:].unsqueeze(2), reads=zB)
            P.barrier()

        for l in range(L):
            layer(P, g, cfg, l, locals())

        for s in range(NS):
            T = TS[s]
            xr = xres[s].rearrange("(c p) t -> p c t", p=128)
            with contextlib.ExitStack() as es:
                xf = [sb(P, es, "xf", [128, KC, NT], F32) for _ in range(2)]
                xfB = [Buf(), Buf()]
                xt = [sb(P, es, "xt", [128, 4, D], F32) for _ in range(2)]
                xtB = [Buf(), Buf()]
                for ti in range(T // NT):
                    k = ti % 2
                    P.dma("sp", xf[k][:], xr[:, :, ti * NT:(ti + 1) * NT], writes=xfB[k])
                    for j in range(4):
                        for c4 in range(4):
                            ps, pb = P.psum()

                            def f(e, ps=ps, c4=c4, j=j, k=k):
                                for cc in range(4):
                                    c = c4 * 4 + cc
                                    ins = e.transpose(out=ps[:, cc * 128:(cc + 1) * 128], in_=xf[k][:, c, j * 128:(j + 1) * 128], identity=ident[:])
                                return ins
                            P.op("pe", f, reads=[xfB[k], cB], writes=pb)
                            ee = P.ev_eng()
                            if ee == "act":
                                P.op("act", lambda e, ps=ps, c4=c4, j=j, k=k: e.copy(out=xt[k][:, j, c4 * 512:(c4 + 1) * 512], in_=ps[:]), reads=pb, writes=xtB[k])
                            else:
                                P.op("dve", lambda e, ps=ps, c4=c4, j=j, k=k: e.tensor_copy(out=xt[k][:, j, c4 * 512:(c4 + 1) * 512], in_=ps[:]), reads=pb, writes=xtB[k])
                    P.dma("pool", y_out[s][ti * NT:(ti + 1) * NT, :].rearrange("(j p) d -> p j d", p=128), xt[k][:], reads=xtB[k])
                P.barrier()
        P.barrier()
    return nc


def rms_rstd(P, G, src, srcB, kcn, sq, sqB, sd, rstd, rsB, dim, eps):
    onesb = G["onesb"]
    cB = G["cB"]
    P.op("act", lambda e: e.activation(out=sq[:, 0:kcn, :], in_=src, func=AF.Square), reads=srcB, writes=sqB)
    ps, pb = P.psum()

    def f(e):
        for c in range(kcn):
            ins = e.matmul(ps[:], lhsT=onesb[:], rhs=sq[:, c, :], start=(c == 0), stop=(c == kcn - 1))
        return ins
    P.op("pe", f, reads=[sqB, cB], writes=pb)
    P.op("act", lambda e: e.activation(out=sd[:], in_=ps[:], func=AF.Sqrt, scale=1.0 / dim, bias=float(eps)), reads=[pb], writes=rsB)
    P.op("dve", lambda e: e.reciprocal(out=rstd[:], in_=sd[:]), reads=rsB, writes=rsB)


def rms_rstd(P, G, src, srcB, kcn, sq, sqB, sd, rstd, rsB, dim, eps):
    onesb = G["onesb"]
    cB = G["cB"]
    P.op("act", lambda e: e.activation(out=sq[:, 0:kcn, :], in_=src, func=AF.Square), reads=srcB, writes=sqB)
    ps, pb = P.psum()

    def f(e):
        for c in range(kcn):
            ins = e.matmul(ps[:], lhsT=onesb[:], rhs=sq[:, c, :], start=(c == 0), stop=(c == kcn - 1))
        return ins
    P.op("pe", f, reads=[sqB, cB], writes=pb)
    P.op("act", lambda e: e.activation(out=sd[:], in_=ps[:], func=AF.Sqrt, scale=1.0 / dim, bias=float(eps)), reads=[pb], writes=rsB)
    P.op("dve", lambda e: e.reciprocal(out=rstd[:], in_=sd[:]), reads=rsB, writes=rsB)


def rms_rstd(P, G, src, srcB, kcn, sq, sqB, sd, rstd, rsB, dim, eps):
    onesb = G["onesb"]
    cB = G["cB"]
    P.op("act", lambda e: e.activation(out=sq[:, 0:kcn, :], in_=src, func=AF.Square), reads=srcB, writes=sqB)
    ps, pb = P.psum()

    def f(e):
        for c in range(kcn):
            ins = e.matmul(ps[:], lhsT=onesb[:], rhs=sq[:, c, :], start=(c == 0), stop=(c == kcn - 1))
        return ins
    P.op("pe", f, reads=[sqB, cB], writes=pb)
    P.op("act", lambda e: e.activation(out=sd[:], in_=ps[:], func=AF.Sqrt, scale=1.0 / dim, bias=float(eps)), reads=[pb], writes=rsB)
    P.op("dve", lambda e: e.reciprocal(out=rstd[:], in_=sd[:]), reads=rsB, writes=rsB)


def rms_rstd(P, G, src, srcB, kcn, sq, sqB, sd, rstd, rsB, dim, eps):
    onesb = G["onesb"]
    cB = G["cB"]
    P.op("act", lambda e: e.activation(out=sq[:, 0:kcn, :], in_=src, func=AF.Square), reads=srcB, writes=sqB)
    ps, pb = P.psum()

    def f(e):
        for c in range(kcn):
            ins = e.matmul(ps[:], lhsT=onesb[:], rhs=sq[:, c, :], start=(c == 0), stop=(c == kcn - 1))
        return ins
    P.op("pe", f, reads=[sqB, cB], writes=pb)
    P.op("act", lambda e: e.activation(out=sd[:], in_=ps[:], func=AF.Sqrt, scale=1.0 / dim, bias=float(eps)), reads=[pb], writes=rsB)
    P.op("dve", lambda e: e.reciprocal(out=rstd[:], in_=sd[:]), reads=rsB, writes=rsB)


def rms_rstd(P, G, src, srcB, kcn, sq, sqB, sd, rstd, rsB, dim, eps):
    onesb = G["onesb"]
    cB = G["cB"]
    P.op("act", lambda e: e.activation(out=sq[:, 0:kcn, :], in_=src, func=AF.Square), reads=srcB, writes=sqB)
    ps, pb = P.psum()

    def f(e):
        for c in range(kcn):
            ins = e.matmul(ps[:], lhsT=onesb[:], rhs=sq[:, c, :], start=(c == 0), stop=(c == kcn - 1))
        return ins
    P.op("pe", f, reads=[sqB, cB], writes=pb)
    P.op("act", lambda e: e.activation(out=sd[:], in_=ps[:], func=AF.Sqrt, scale=1.0 / dim, bias=float(eps)), reads=[pb], writes=rsB)
    P.op("dve", lambda e: e.reciprocal(out=rstd[:], in_=sd[:]), reads=rsB, writes=rsB)


def rms_rstd(P, G, src, srcB, kcn, sq, sqB, sd, rstd, rsB, dim, eps):
    onesb = G["onesb"]
    cB = G["cB"]
    P.op("act", lambda e: e.activation(out=sq[:, 0:kcn, :], in_=src, func=AF.Square), reads=srcB, writes=sqB)
    ps, pb = P.psum()

    def f(e):
        for c in range(kcn):
            ins = e.matmul(ps[:], lhsT=onesb[:], rhs=sq[:, c, :], start=(c == 0), stop=(c == kcn - 1))
        return ins
    P.op("pe", f, reads=[sqB, cB], writes=pb)
    P.op("act", lambda e: e.activation(out=sd[:], in_=ps[:], func=AF.Sqrt, scale=1.0 / dim, bias=float(eps)), reads=[pb], writes=rsB)
    P.op("dve", lambda e: e.reciprocal(out=rstd[:], in_=sd[:]), reads=rsB, writes=rsB)


def rms_rstd(P, G, src, srcB, kcn, sq, sqB, sd, rstd, rsB, dim, eps):
    onesb = G["onesb"]
    cB = G["cB"]
    P.op("act", lambda e: e.activation(out=sq[:, 0:kcn, :], in_=src, func=AF.Square), reads=srcB, writes=sqB)
    ps, pb = P.psum()

    def f(e):
        for c in range(kcn):
            ins = e.matmul(ps[:], lhsT=onesb[:], rhs=sq[:, c, :], start=(c == 0), stop=(c == kcn - 1))
        return ins
    P.op("pe", f, reads=[sqB, cB], writes=pb)
    P.op("act", lambda e: e.activation(out=sd[:], in_=ps[:], func=AF.Sqrt, scale=1.0 / dim, bias=float(eps)), reads=[pb], writes=rsB)
    P.op("dve", lambda e: e.reciprocal(out=rstd[:], in_=sd[:]), reads=rsB, writes=rsB)


def rms_rstd(P, G, src, srcB, kcn, sq, sqB, sd, rstd, rsB, dim, eps):
    onesb = G["onesb"]
    cB = G["cB"]
    P.op("act", lambda e: e.activation(out=sq[:, 0:kcn, :], in_=src, func=AF.Square), reads=srcB, writes=sqB)
    ps, pb = P.psum()

    def f(e):
        for c in range(kcn):
            ins = e.matmul(ps[:], lhsT=onesb[:], rhs=sq[:, c, :], start=(c == 0), stop=(c == kcn - 1))
        return ins
    P.op("pe", f, reads=[sqB, cB], writes=pb)
    P.op("act", lambda e: e.activation(out=sd[:], in_=ps[:], func=AF.Sqrt, scale=1.0 / dim, bias=float(eps)), reads=[pb], writes=rsB)
    P.op("dve", lambda e: e.reciprocal(out=rstd[:], in_=sd[:]), reads=rsB, writes=rsB)


def rms_rstd(P, G, src, srcB, kcn, sq, sqB, sd, rstd, rsB, dim, eps):
    onesb = G["onesb"]
    cB = G["cB"]
    P.op("act", lambda e: e.activation(out=sq[:, 0:kcn, :], in_=src, func=AF.Square), reads=srcB, writes=sqB)
    ps, pb = P.psum()

    def f(e):
        for c in range(kcn):
            ins = e.matmul(ps[:], lhsT=onesb[:], rhs=sq[:, c, :], start=(c == 0), stop=(c == kcn - 1))
        return ins
    P.op("pe", f, reads=[sqB, cB], writes=pb)
    P.op("act", lambda e: e.activation(out=sd[:], in_=ps[:], func=AF.Sqrt, scale=1.0 / dim, bias=float(eps)), reads=[pb], writes=rsB)
    P.op("dve", lambda e: e.reciprocal(out=rstd[:], in_=sd[:]), reads=rsB, writes=rsB)


def rms_rstd(P, G, src, srcB, kcn, sq, sqB, sd, rstd, rsB, dim, eps):
    onesb = G["onesb"]
    cB = G["cB"]
    P.op("act", lambda e: e.activation(out=sq[:, 0:kcn, :], in_=src, func=AF.Square), reads=srcB, writes=sqB)
    ps, pb = P.psum()

    def f(e):
        for c in range(kcn):
            ins = e.matmul(ps[:], lhsT=onesb[:], rhs=sq[:, c, :], start=(c == 0), stop=(c == kcn - 1))
        return ins
    P.op("pe", f, reads=[sqB, cB], writes=pb)
    P.op("act", lambda e: e.activation(out=sd[:], in_=ps[:], func=AF.Sqrt, scale=1.0 / dim, bias=float(eps)), reads=[pb], writes=rsB)
    P.op("dve", lambda e: e.reciprocal(out=rstd[:], in_=sd[:]), reads=rsB, writes=rsB)


def rms_rstd(P, G, src, srcB, kcn, sq, sqB, sd, rstd, rsB, dim, eps):
    onesb = G["onesb"]
    cB = G["cB"]
    P.op("act", lambda e: e.activation(out=sq[:, 0:kcn, :], in_=src, func=AF.Square), reads=srcB, writes=sqB)
    ps, pb = P.psum()

    def f(e):
        for c in range(kcn):
            ins = e.matmul(ps[:], lhsT=onesb[:], rhs=sq[:, c, :], start=(c == 0), stop=(c == kcn - 1))
        return ins
    P.op("pe", f, reads=[sqB, cB], writes=pb)
    P.op("act", lambda e: e.activation(out=sd[:], in_=ps[:], func=AF.Sqrt, scale=1.0 / dim, bias=float(eps)), reads=[pb], writes=rsB)
    P.op("dve", lambda e: e.reciprocal(out=rstd[:], in_=sd[:]), reads=rsB, writes=rsB)


def rms_rstd(P, G, src, srcB, kcn, sq, sqB, sd, rstd, rsB, dim, eps):
    onesb = G["onesb"]
    cB = G["cB"]
    P.op("act", lambda e: e.activation(out=sq[:, 0:kcn, :], in_=src, func=AF.Square), reads=srcB, writes=sqB)
    ps, pb = P.psum()

    def f(e):
        for c in range(kcn):
            ins = e.matmul(ps[:], lhsT=onesb[:], rhs=sq[:, c, :], start=(c == 0), stop=(c == kcn - 1))
        return ins
    P.op("pe", f, reads=[sqB, cB], writes=pb)
    P.op("act", lambda e: e.activation(out=sd[:], in_=ps[:], func=AF.Sqrt, scale=1.0 / dim, bias=float(eps)), reads=[pb], writes=rsB)
    P.op("dve", lambda e: e.reciprocal(out=rstd[:], in_=sd[:]), reads=rsB, writes=rsB)


def rms_rstd(P, G, src, srcB, kcn, sq, sqB, sd, rstd, rsB, dim, eps):
    onesb = G["onesb"]
    cB = G["cB"]
    P.op("act", lambda e: e.activation(out=sq[:, 0:kcn, :], in_=src, func=AF.Square), reads=srcB, writes=sqB)
    ps, pb = P.psum()

    def f(e):
        for c in range(kcn):
            ins = e.matmul(ps[:], lhsT=onesb[:], rhs=sq[:, c, :], start=(c == 0), stop=(c == kcn - 1))
        return ins
    P.op("pe", f, reads=[sqB, cB], writes=pb)
    P.op("act", lambda e: e.activation(out=sd[:], in_=ps[:], func=AF.Sqrt, scale=1.0 / dim, bias=float(eps)), reads=[pb], writes=rsB)
    P.op("dve", lambda e: e.reciprocal(out=rstd[:], in_=sd[:]), reads=rsB, writes=rsB)


def rms_rstd(P, G, src, srcB, kcn, sq, sqB, sd, rstd, rsB, dim, eps):
    onesb = G["onesb"]
    cB = G["cB"]
    P.op("act", lambda e: e.activation(out=sq[:, 0:kcn, :], in_=src, func=AF.Square), reads=srcB, writes=sqB)
    ps, pb = P.psum()

    def f(e):
        for c in range(kcn):
            ins = e.matmul(ps[:], lhsT=onesb[:], rhs=sq[:, c, :], start=(c == 0), stop=(c == kcn - 1))
        return ins
    P.op("pe", f, reads=[sqB, cB], writes=pb)
    P.op("act", lambda e: e.activation(out=sd[:], in_=ps[:], func=AF.Sqrt, scale=1.0 / dim, bias=float(eps)), reads=[pb], writes=rsB)
    P.op("dve", lambda e: e.reciprocal(out=rstd[:], in_=sd[:]), reads=rsB, writes=rsB)


def rms_rstd(P, G, src, srcB, kcn, sq, sqB, sd, rstd, rsB, dim, eps):
    onesb = G["onesb"]
    cB = G["cB"]
    P.op("act", lambda e: e.activation(out=sq[:, 0:kcn, :], in_=src, func=AF.Square), reads=srcB, writes=sqB)
    ps, pb = P.psum()

    def f(e):
        for c in range(kcn):
            ins = e.matmul(ps[:], lhsT=onesb[:], rhs=sq[:, c, :], start=(c == 0), stop=(c == kcn - 1))
        return ins
    P.op("pe", f, reads=[sqB, cB], writes=pb)
    P.op("act", lambda e: e.activation(out=sd[:], in_=ps[:], func=AF.Sqrt, scale=1.0 / dim, bias=float(eps)), reads=[pb], writes=rsB)
    P.op("dve", lambda e: e.reciprocal(out=rstd[:], in_=sd[:]), reads=rsB, writes=rsB)


def rms_rstd(P, G, src, srcB, kcn, sq, sqB, sd, rstd, rsB, dim, eps):
    onesb = G["onesb"]
    cB = G["cB"]
    P.op("act", lambda e: e.activation(out=sq[:, 0:kcn, :], in_=src, func=AF.Square), reads=srcB, writes=sqB)
    ps, pb = P.psum()

    def f(e):
        for c in range(kcn):
            ins = e.matmul(ps[:], lhsT=onesb[:], rhs=sq[:, c, :], start=(c == 0), stop=(c == kcn - 1))
        return ins
    P.op("pe", f, reads=[sqB, cB], writes=pb)
    P.op("act", lambda e: e.activation(out=sd[:], in_=ps[:], func=AF.Sqrt, scale=1.0 / dim, bias=float(eps)), reads=[pb], writes=rsB)
    P.op("dve", lambda e: e.reciprocal(out=rstd[:], in_=sd[:]), reads=rsB, writes=rsB)


def rms_rstd(P, G, src, srcB, kcn, sq, sqB, sd, rstd, rsB, dim, eps):
    onesb = G["onesb"]
    cB = G["cB"]
    P.op("act", lambda e: e.activation(out=sq[:, 0:kcn, :], in_=src, func=AF.Square), reads=srcB, writes=sqB)
    ps, pb = P.psum()

    def f(e):
        for c in range(kcn):
            ins = e.matmul(ps[:], lhsT=onesb[:], rhs=sq[:, c, :], start=(c == 0), stop=(c == kcn - 1))
        return ins
    P.op("pe", f, reads=[sqB, cB], writes=pb)
    P.op("act", lambda e: e.activation(out=sd[:], in_=ps[:], func=AF.Sqrt, scale=1.0 / dim, bias=float(eps)), reads=[pb], writes=rsB)
    P.op("dve", lambda e: e.reciprocal(out=rstd[:], in_=sd[:]), reads=rsB, writes=rsB)


def rms_rstd(P, G, src, srcB, kcn, sq, sqB, sd, rstd, rsB, dim, eps):
    onesb = G["onesb"]
    cB = G["cB"]
    P.op("act", lambda e: e.activation(out=sq[:, 0:kcn, :], in_=src, func=AF.Square), reads=srcB, writes=sqB)
    ps, pb = P.psum()

    def f(e):
        for c in range(kcn):
            ins = e.matmul(ps[:], lhsT=onesb[:], rhs=sq[:, c, :], start=(c == 0), stop=(c == kcn - 1))
        return ins
    P.op("pe", f, reads=[sqB, cB], writes=pb)
    P.op("act", lambda e: e.activation(out=sd[:], in_=ps[:], func=AF.Sqrt, scale=1.0 / dim, bias=float(eps)), reads=[pb], writes=rsB)
    P.op("dve", lambda e: e.reciprocal(out=rstd[:], in_=sd[:]), reads=rsB, writes=rsB)


def rms_rstd(P, G, src, srcB, kcn, sq, sqB, sd, rstd, rsB, dim, eps):
    onesb = G["onesb"]
    cB = G["cB"]
    P.op("act", lambda e: e.activation(out=sq[:, 0:kcn, :], in_=src, func=AF.Square), reads=srcB, writes=sqB)
    ps, pb = P.psum()

    def f(e):
        for c in range(kcn):
            ins = e.matmul(ps[:], lhsT=onesb[:], rhs=sq[:, c, :], start=(c == 0), stop=(c == kcn - 1))
        return ins
    P.op("pe", f, reads=[sqB, cB], writes=pb)
    P.op("act", lambda e: e.activation(out=sd[:], in_=ps[:], func=AF.Sqrt, scale=1.0 / dim, bias=float(eps)), reads=[pb], writes=rsB)
    P.op("dve", lambda e: e.reciprocal(out=rstd[:], in_=sd[:]), reads=rsB, writes=rsB)


def rms_rstd(P, G, src, srcB, kcn, sq, sqB, sd, rstd, rsB, dim, eps):
    onesb = G["onesb"]
    cB = G["cB"]
    P.op("act", lambda e: e.activation(out=sq[:, 0:kcn, :], in_=src, func=AF.Square), reads=srcB, writes=sqB)
    ps, pb = P.psum()

    def f(e):
        for c in range(kcn):
            ins = e.matmul(ps[:], lhsT=onesb[:], rhs=sq[:, c, :], start=(c == 0), stop=(c == kcn - 1))
        return ins
    P.op("pe", f, reads=[sqB, cB], writes=pb)
    P.op("act", lambda e: e.activation(out=sd[:], in_=ps[:], func=AF.Sqrt, scale=1.0 / dim, bias=float(eps)), reads=[pb], writes=rsB)
    P.op("dve", lambda e: e.reciprocal(out=rstd[:], in_=sd[:]), reads=rsB, writes=rsB)


def rms_rstd(P, G, src, srcB, kcn, sq, sqB, sd, rstd, rsB, dim, eps):
    onesb = G["onesb"]
    cB = G["cB"]
    P.op("act", lambda e: e.activation(out=sq[:, 0:kcn, :], in_=src, func=AF.Square), reads=srcB, writes=sqB)
    ps, pb = P.psum()

    def f(e):
        for c in range(kcn):
            ins = e.matmul(ps[:], lhsT=onesb[:], rhs=sq[:, c, :], start=(c == 0), stop=(c == kcn - 1))
        return ins
    P.op("pe", f, reads=[sqB, cB], writes=pb)
    P.op("act", lambda e: e.activation(out=sd[:], in_=ps[:], func=AF.Sqrt, scale=1.0 / dim, bias=float(eps)), reads=[pb], writes=rsB)
    P.op("dve", lambda e: e.reciprocal(out=rstd[:], in_=sd[:]), reads=rsB, writes=rsB)


def rms_rstd(P, G, src, srcB, kcn, sq, sqB, sd, rstd, rsB, dim, eps):
    onesb = G["onesb"]
    cB = G["cB"]
    P.op("act", lambda e: e.activation(out=sq[:, 0:kcn, :], in_=src, func=AF.Square), reads=srcB, writes=sqB)
    ps, pb = P.psum()

    def f(e):
        for c in range(kcn):
            ins = e.matmul(ps[:], lhsT=onesb[:], rhs=sq[:, c, :], start=(c == 0), stop=(c == kcn - 1))
        return ins
    P.op("pe", f, reads=[sqB, cB], writes=pb)
    P.op("act", lambda e: e.activation(out=sd[:], in_=ps[:], func=AF.Sqrt, scale=1.0 / dim, bias=float(eps)), reads=[pb], writes=rsB)
    P.op("dve", lambda e: e.reciprocal(out=rstd[:], in_=sd[:]), reads=rsB, writes=rsB)


def rms_rstd(P, G, src, srcB, kcn, sq, sqB, sd, rstd, rsB, dim, eps):
    onesb = G["onesb"]
    cB = G["cB"]
    P.op("act", lambda e: e.activation(out=sq[:, 0:kcn, :], in_=src, func=AF.Square), reads=srcB, writes=sqB)
    ps, pb = P.psum()

    def f(e):
        for c in range(kcn):
            ins = e.matmul(ps[:], lhsT=onesb[:], rhs=sq[:, c, :], start=(c == 0), stop=(c == kcn - 1))
        return ins
    P.op("pe", f, reads=[sqB, cB], writes=pb)
    P.op("act", lambda e: e.activation(out=sd[:], in_=ps[:], func=AF.Sqrt, scale=1.0 / dim, bias=float(eps)), reads=[pb], writes=rsB)
    P.op("dve", lambda e: e.reciprocal(out=rstd[:], in_=sd[:]), reads=rsB, writes=rsB)


def rms_rstd(P, G, src, srcB, kcn, sq, sqB, sd, rstd, rsB, dim, eps):
    onesb = G["onesb"]
    cB = G["cB"]
    P.op("act", lambda e: e.activation(out=sq[:, 0:kcn, :], in_=src, func=AF.Square), reads=srcB, writes=sqB)
    ps, pb = P.psum()

    def f(e):
        for c in range(kcn):
            ins = e.matmul(ps[:], lhsT=onesb[:], rhs=sq[:, c, :], start=(c == 0), stop=(c == kcn - 1))
        return ins
    P.op("pe", f, reads=[sqB, cB], writes=pb)
    P.op("act", lambda e: e.activation(out=sd[:], in_=ps[:], func=AF.Sqrt, scale=1.0 / dim, bias=float(eps)), reads=[pb], writes=rsB)
    P.op("dve", lambda e: e.reciprocal(out=rstd[:], in_=sd[:]), reads=rsB, writes=rsB)


def rms_rstd(P, G, src, srcB, kcn, sq, sqB, sd, rstd, rsB, dim, eps):
    onesb = G["onesb"]
    cB = G["cB"]
    P.op("act", lambda e: e.activation(out=sq[:, 0:kcn, :], in_=src, func=AF.Square), reads=srcB, writes=sqB)
    ps, pb = P.psum()

    def f(e):
        for c in range(kcn):
            ins = e.matmul(ps[:], lhsT=onesb[:], rhs=sq[:, c, :], start=(c == 0), stop=(c == kcn - 1))
        return ins
    P.op("pe", f, reads=[sqB, cB], writes=pb)
    P.op("act", lambda e: e.activation(out=sd[:], in_=ps[:], func=AF.Sqrt, scale=1.0 / dim, bias=float(eps)), reads=[pb], writes=rsB)
    P.op("dve", lambda e: e.reciprocal(out=rstd[:], in_=sd[:]), reads=rsB, writes=rsB)


def layer(P, g, cfg, l, G):
    nc = P.nc
    TS = cfg["seqs"]
    NS = len(TS)
    pv, modv, cs, cB = G["pv"], G["modv"], G["cs"], G["cB"]
    pvB, modB = G["pvB"], G["modB"]
    ES = contextlib.ExitStack
    stop_after = cfg.get("stop_after", "")

    P.dma("sp", pv[:], G["pvec"][l], writes=pvB)
    with ES() as es:
        wsl = [sb(P, es, "wsl", [128, KC, 512], BF16) for _ in range(2)]
        wB = [Buf(), Buf()]
        modraw = sb(P, es, "modraw", [128, 96, NS], F32)
        mrB = Buf()
        ps, pb = P.psum()
        for gi in range(24):
            k = gi % 2
            P.dma("sp", wsl[k][:], G["wada"][l, gi], writes=wB[k])

            def f(e, gi=gi, k=k):
                for m in range(4):
                    j = gi * 4 + m
                    for kc in range(KC):
                        ins = e.matmul(ps[:, j * NS:(j + 1) * NS], lhsT=wsl[k][:, kc, m * 128:(m + 1) * 128], rhs=cs[:, kc, :], start=(kc == 0), stop=(kc == KC - 1))
                return ins
            P.op("pe", f, reads=[wB[k], cB], writes=pb)
        P.op("dve", lambda e: e.tensor_tensor(out=modraw[:], in0=ps[:, 0:96 * NS].rearrange("p (j s) -> p j s", s=NS),
                                              in1=pv[:, 64:160].unsqueeze(2).to_broadcast([128, 96, NS]), op=ALU.add), reads=[pb, pvB], writes=mrB)
        for s in range(NS):
            P.op("dve", lambda e, s=s: e.tensor_copy(out=modv[:, s, 0, :], in_=modraw[:, 0:16, s]), reads=mrB, writes=modB)
            P.op("dve", lambda e, s=s: e.scalar_tensor_tensor(out=modv[:, s, 1, :], in0=modraw[:, 16:32, s], scalar=1.0, in1=pv[:, 0:16], op0=ALU.add, op1=ALU.mult), reads=[mrB, pvB], writes=modB)
            P.op("dve", lambda e, s=s: e.tensor_tensor(out=modv[:, s, 2, :], in0=modraw[:, 32:48, s], in1=pv[:, 16:32], op=ALU.mult), reads=[mrB, pvB], writes=modB)
            P.op("dve", lambda e, s=s: e.tensor_copy(out=modv[:, s, 3, :], in_=modraw[:, 48:64, s]), reads=mrB, writes=modB)
            P.op("dve", lambda e, s=s: e.scalar_tensor_tensor(out=modv[:, s, 4, :], in0=modraw[:, 64:80, s], scalar=1.0, in1=pv[:, 32:48], op0=ALU.add, op1=ALU.mult), reads=[mrB, pvB], writes=modB)
            P.op("dve", lambda e, s=s: e.tensor_tensor(out=modv[:, s, 5, :], in0=modraw[:, 80:96, s], in1=pv[:, 48:64], op=ALU.mult), reads=[mrB, pvB], writes=modB)
        P.barrier()

    for s in range(NS):
        T = TS[s]
        ntile = T // NT
        xr = G["xres"][s].rearrange("(c p) t -> p c t", p=128)
        hf = G["hfm"][s].rearrange("(c p) t -> p c t", p=128)
        with ES() as es:
            xt = [sb(P, es, "xt", [128, KC, NT], F32) for _ in range(2)]
            xB = [Buf(), Buf()]
            sq = sb(P, es, "sq", [128, KC, NT], BF16)
            sqB = Buf()
            sd = sb(P, es, "sd", [128, NT], F32)
            rstd = sb(P, es, "rstd", [128, NT], F32)
            rsB = Buf()
            tmp = [sb(P, es, "tmp", [128, NT], F32) for _ in range(2)]
            tB = [Buf(), Buf()]
            h = [sb(P, es, "h", [128, KC, NT], BF16) for _ in range(2)]
            hB = [Buf(), Buf()]
            for ti in range(ntile):
                k = ti % 2
                P.dma("sp", xt[k][:], xr[:, :, ti * NT:(ti + 1) * NT], writes=xB[k])
                rms_rstd(P, G, xt[k][:], xB[k], KC, sq, sqB, sd, rstd, rsB, D, EPS)
                for c in range(KC):
                    j = c % 2
                    P.op("dve", lambda e, c=c, j=j, k=k: e.scalar_tensor_tensor(out=tmp[j][:], in0=xt[k][:, c, :], scalar=modv[:, s, 1, c:c + 1], in1=rstd[:], op0=ALU.mult, op1=ALU.mult),
                         reads=[xB[k], rsB, modB], writes=tB[j])
                    P.op("act", lambda e, c=c, j=j, k=k: e.activation(out=h[k][:, c, :], in_=tmp[j][:], func=AF.Identity, bias=modv[:, s, 0, c:c + 1], scale=1.0),
                         reads=[tB[j], modB], writes=hB[k])
                P.dma("pool", hf[:, :, 1 + ti * NT:1 + (ti + 1) * NT], h[k][:], reads=hB[k])
            P.barrier()
        if stop_after == "A1":
            continue

        with ES() as es:
            w32 = sb(P, es, "w32", [128, KC, NLORA], F32)
            p32 = sb(P, es, "p32", [128, KC, NLORA], F32)
            n32 = sb(P, es, "n32", [128, KC, NLORA], F32)
            Wc = sb(P, es, "Wc", [128, KC, NLORA], BF16)
            Wp = sb(P, es, "Wp", [128, KC, NLORA], BF16)
            Wn = sb(P, es, "Wn", [128, KC, NLORA], BF16)
            lB = Buf()
            P.dma("sp", w32[:], G["lora_dn"][l], writes=lB)
            P.dma("sp", p32[:], G["lora_mp"][l], writes=lB)
            P.dma("sp", n32[:], G["lora_mn"][l], writes=lB)
            P.op("dve", lambda e: e.tensor_tensor(out=p32[:], in0=p32[:], in1=w32[:], op=ALU.mult), reads=lB, writes=lB)
            P.op("dve", lambda e: e.tensor_tensor(out=n32[:], in0=n32[:], in1=w32[:], op=ALU.mult), reads=lB, writes=lB)
            P.op("dve", lambda e: e.tensor_tensor(out=w32[:], in0=w32[:], in1=p32[:], op=ALU.subtract), reads=lB, writes=lB)
            P.op("dve", lambda e: e.tensor_tensor(out=w32[:], in0=w32[:], in1=n32[:], op=ALU.subtract), reads=lB, writes=lB)
            P.op("dve", lambda e: e.tensor_copy(out=Wc[:], in_=w32[:]), reads=lB, writes=lB)
            P.op("dve", lambda e: e.tensor_copy(out=Wp[:], in_=p32[:]), reads=lB, writes=lB)
            P.op("dve", lambda e: e.tensor_copy(out=Wn[:], in_=n32[:]), reads=lB, writes=lB)
            hext = [sb(P, es, "hext", [128, KC, NT + 2], BF16) for _ in range(2)]
            hxB = [Buf(), Buf()]
            wsl = [sb(P, es, "wsl", [128, KC, 512], BF16) for _ in range(2)]
            wB = [Buf(), Buf()]
            ot = [sb(P, es, "ot", [128, 4, NT], BF16) for _ in range(2)]
            otB = [Buf(), Buf()]
            of = [sb(P, es, "of", [128, 4, NT], F32) for _ in range(2)]
            ofB = [Buf(), Buf()]
            vt = [sb(P, es, "vt", [128, 512], BF16) for _ in range(2)]
            vtB = [Buf(), Buf()]
            lo = [sb(P, es, "lo", [128, NT], BF16) for _ in range(3)]
            loB = [Buf(), Buf(), Buf()]
            wcount = 0
            for ti in range(ntile):
                k = ti % 2
                tsl = slice(ti * NT, (ti + 1) * NT)
                P.dma("sp", hext[k][:], hf[:, :, ti * NT:ti * NT + NT + 2], writes=hxB[k])
                for og, (c0, c1) in enumerate(((0, 128), (128, 256), (256, 352))):
                    M = c1 - c0
                    ps, pb = P.psum()

                    def f(e, ps=ps, c0=c0, c1=c1, M=M, k=k):
                        n = 0
                        for W, off in ((Wc, 1), (Wp, 0), (Wn, 2)):
                            for kc in range(KC):
                                ins = e.matmul(ps[0:M, :], lhsT=W[:, kc, c0:c1], rhs=hext[k][:, kc, off:off + NT], start=(n == 0), stop=(n == 3 * KC - 1))
                                n += 1
                        return ins
                    P.op("pe", f, reads=[lB, hxB[k]], writes=pb)
                    if og == 0:
                        P.op("act", lambda e, ps=ps: e.activation(out=lo[0][:], in_=ps[:], func=AF.Tanh), reads=pb, writes=loB[0])
                        P.dma("pool", G["lw_d"][s][:, tsl], lo[0][:], reads=loB[0])
                    elif og == 1:
                        P.op("dve", lambda e, ps=ps: e.tensor_copy(out=lo[1][:], in_=ps[:]), reads=pb, writes=loB[1])
                        P.dma("pool", G["la_d"][s][:, tsl], lo[1][:], reads=loB[1])
                    else:
                        P.op("act", lambda e, ps=ps: e.activation(out=lo[2][0:64, :], in_=ps[0:64, :], func=AF.Sigmoid), reads=pb, writes=loB[2])
                        P.op("dve", lambda e, ps=ps: e.tensor_copy(out=lo[2][64:96, :], in_=ps[64:96, :]), reads=pb, writes=loB[2])
                        P.dma("pool", G["lg_d"][s][:, tsl], lo[2][0:64, :], reads=loB[2])
                        P.dma("pool", G["lv_d"][s][:, tsl], lo[2][64:96, :], reads=loB[2])
                for gi in range(12):
                    wk = wcount % 2
                    wcount += 1
                    P.dma("sp", wsl[wk][:], G["win"][l, gi], writes=wB[wk])
                    if gi in (4, 5):
                        for tsub in range(4):
                            ps, pb = P.psum()

                            def f(e, ps=ps, tsub=tsub, wk=wk, k=k):
                                for kc in range(KC):
                                    ins = e.matmul(ps[:], lhsT=hext[k][:, kc, 1 + tsub * 128:1 + (tsub + 1) * 128], rhs=wsl[wk][:, kc, :], start=(kc == 0), stop=(kc == KC - 1))
                                return ins
                            P.op("pe", f, reads=[wB[wk], hxB[k]], writes=pb)
                            j = tsub % 2
                            P.op("act", lambda e, ps=ps, j=j: e.copy(out=vt[j][:], in_=ps[:]), reads=pb, writes=vtB[j])
                            r0 = ti * NT + tsub * 128
                            P.dma("pool", G["vtm"][s][r0:r0 + 128, (gi - 4) * 512:(gi - 3) * 512], vt[j][:], reads=vtB[j])
                    else:
                        j = gi % 2
                        for m in range(4):
                            ps, pb = P.psum()

                            def f(e, ps=ps, m=m, wk=wk, k=k):
                                for kc in range(KC):
                                    ins = e.matmul(ps[:], lhsT=wsl[wk][:, kc, m * 128:(m + 1) * 128], rhs=hext[k][:, kc, 1:NT + 1], start=(kc == 0), stop=(kc == KC - 1))
                                return ins
                            P.op("pe", f, reads=[wB[wk], hxB[k]], writes=pb)
                            dst, dB = (ot[j], otB[j]) if gi < 4 else (of[j], ofB[j])
                            if P.ev_eng() == "act":
                                P.op("act", lambda e, ps=ps, m=m, dst=dst: e.copy(out=dst[:, m, :], in_=ps[:]), reads=pb, writes=dB)
                            else:
                                P.op("dve", lambda e, ps=ps, m=m, dst=dst: e.tensor_copy(out=dst[:, m, :], in_=ps[:]), reads=pb, writes=dB)
                        if gi < 4:
                            dd = (G["qfm"] if gi < 2 else G["kfm"])[s].rearrange("(c p) t -> p c t", p=128)
                            P.dma("pool", dd[:, (gi % 2) * 4:(gi % 2) * 4 + 4, tsl], ot[j][:], reads=otB[j])
                        else:
                            dd = G["rkv"][s].rearrange("(c p) t -> p c t", p=128)
                            P.dma("pool", dd[:, (gi - 6) * 4:(gi - 6) * 4 + 4, 1 + ti * NT:1 + (ti + 1) * NT], of[j][:], reads=ofB[j])
            P.barrier()
        if stop_after == "A2":
            continue
        attention(P, cfg, l, s, G)
        if stop_after == "B1":
            continue
        for d in range(2):
            scan_pass(P, cfg, l, s, d, G)
        if stop_after == "B2":
            continue
        scan_post(P, cfg, l, s, G)
        if stop_after == "B3":
            continue
        ffn_phase(P, cfg, l, s, G)


def ffn_phase(P, cfg, l, s, G):
    ES = contextlib.ExitStack
    TS = cfg["seqs"]
    T = TS[s]
    ntile = T // NT
    pv, modv, cB, pvB, modB = G["pv"], G["modv"], G["cB"], G["pvB"], G["modB"]
    xr = G["xres"][s].rearrange("(c p) t -> p c t", p=128)
    mx = G["mixin"][s].rearrange("(c p) t -> p c t", p=128)
    with ES() as es:
        xt = sb(P, es, "xt", [128, KC, NT], F32)
        xB = Buf()
        mf = sb(P, es, "mf", [128, KC, NT], F32)
        mfB = Buf()
        hb = sb(P, es, "hb", [128, KC, NT], BF16)
        hbB = Buf()
        hid = sb(P, es, "hid", [128, 32, NT], BF16)
        hidB = Buf()
        sq = sb(P, es, "sq", [128, KC, NT], BF16)
        sqB = Buf()
        sd = sb(P, es, "sd", [128, NT], F32)
        rstd = sb(P, es, "rstd", [128, NT], F32)
        rsB = Buf()
        tmp = [sb(P, es, "tmp", [128, NT], F32) for _ in range(2)]
        tB = [Buf(), Buf()]
        wsl = [sb(P, es, "wsl", [128, KC * 512], BF16) for _ in range(2)]
        wB = [Buf(), Buf()]
        gs = sb(P, es, "gs", [128, 8], F32)
        wc = 0
        for ti in range(ntile):
            tsl = slice(ti * NT, (ti + 1) * NT)
            P.dma("sp", xt[:], xr[:, :, tsl], writes=xB)
            P.dma("sp", hb[:], mx[:, :, tsl], writes=hbB)
            rms_rstd(P, G, hb[:, 0:8, :], hbB, 8, sq, sqB, sd, rstd, rsB, 1024, EPS)
            for c in range(8):
                P.op("dve", lambda e, c=c: e.scalar_tensor_tensor(out=hb[:, c, :], in0=hb[:, c, :], scalar=pv[:, 160 + c:161 + c], in1=rstd[:], op0=ALU.mult, op1=ALU.mult),
                     reads=[rsB, pvB], writes=hbB)
            for gi in range(4):
                wk = wc % 2
                wc += 1
                wv = wsl[wk][:].rearrange("p (k n) -> p k n", n=512)
                P.dma("sp", wv, G["wout"][l, gi], writes=wB[wk])
                for m in range(4):
                    ps, pb = P.psum()

                    def f(e, ps=ps, m=m, wv=wv):
                        for kc in range(KC):
                            ins = e.matmul(ps[:], lhsT=wv[:, kc, m * 128:(m + 1) * 128], rhs=hb[:, kc, :], start=(kc == 0), stop=(kc == KC - 1))
                        return ins
                    P.op("pe", f, reads=[wB[wk], hbB], writes=pb)
                    c = gi * 4 + m
                    if P.ev_eng() == "act":
                        P.op("act", lambda e, ps=ps, c=c: e.copy(out=mf[:, c, :], in_=ps[:]), reads=pb, writes=mfB)
                    else:
                        P.op("dve", lambda e, ps=ps, c=c: e.tensor_copy(out=mf[:, c, :], in_=ps[:]), reads=pb, writes=mfB)
            rms_rstd(P, G, mf[:], mfB, KC, sq, sqB, sd, rstd, rsB, D, EPS)
            for c in range(KC):
                j = c % 2
                P.op("dve", lambda e, c=c, j=j: e.scalar_tensor_tensor(out=tmp[j][:], in0=mf[:, c, :], scalar=modv[:, s, 2, c:c + 1], in1=rstd[:], op0=ALU.mult, op1=ALU.mult),
                     reads=[mfB, rsB, modB], writes=tB[j])
                P.op("dve", lambda e, c=c, j=j: e.tensor_tensor(out=xt[:, c, :], in0=xt[:, c, :], in1=tmp[j][:], op=ALU.add), reads=[tB[j]], writes=xB)
            rms_rstd(P, G, xt[:], xB, KC, sq, sqB, sd, rstd, rsB, D, EPS)
            for c in range(KC):
                j = c % 2
                P.op("dve", lambda e, c=c, j=j: e.scalar_tensor_tensor(out=tmp[j][:], in0=xt[:, c, :], scalar=modv[:, s, 4, c:c + 1], in1=rstd[:], op0=ALU.mult, op1=ALU.mult),
                     reads=[xB, rsB, modB], writes=tB[j])
                P.op("act", lambda e, c=c, j=j: e.activation(out=hb[:, c, :], in_=tmp[j][:], func=AF.Identity, bias=modv[:, s, 3, c:c + 1], scale=1.0),
                     reads=[tB[j], modB], writes=hbB)
            for half in range(2):
                for g8 in range(8):
                    gi = half * 8 + g8
                    wk = wc % 2
                    wc += 1
                    wv = wsl[wk][:].rearrange("p (k n) -> p k n", n=512)
                    P.dma("sp", wv, G["wf1"][l, gi], writes=wB[wk])
                    for m in range(4):
                        ps, pb = P.psum()

                        def f(e, ps=ps, m=m, wv=wv):
                            for kc in range(KC):
                                ins = e.matmul(ps[:], lhsT=wv[:, kc, m * 128:(m + 1) * 128], rhs=hb[:, kc, :], start=(kc == 0), stop=(kc == KC - 1))
                            return ins
                        P.op("pe", f, reads=[wB[wk], hbB], writes=pb)
                        j = m % 2
                        c = g8 * 4 + m
                        P.op("act", lambda e, ps=ps, j=j: e.activation(out=tmp[j][:], in_=ps[:], func=AF.Relu), reads=pb, writes=tB[j])
                        P.op("dve", lambda e, c=c, j=j: e.tensor_tensor(out=hid[:, c, :], in0=tmp[j][:], in1=tmp[j][:], op=ALU.mult), reads=tB[j], writes=hidB)
                for mg2 in range(8):
                    wk = wc % 2
                    wc += 1
                    wv = wsl[wk][:].rearrange("p (g k n) -> p g k n", g=2, n=128)
                    P.dma("sp", wv, G["wf2"][l, half, 2 * mg2:2 * mg2 + 2].rearrange("g p k n -> p g k n"), writes=wB[wk])
                    for gg in range(2):
                        mg = 2 * mg2 + gg
                        ps, pb = P.psum()

                        def f(e, ps=ps, gg=gg, wv=wv):
                            for kc in range(32):
                                ins = e.matmul(ps[:], lhsT=wv[:, gg, kc, :], rhs=hid[:, kc, :], start=(kc == 0), stop=(kc == 31))
                            return ins
                        P.op("pe", f, reads=[wB[wk], hidB], writes=pb)
                        if half == 0:
                            P.op("act", lambda e, ps=ps, mg=mg: e.copy(out=mf[:, mg, :], in_=ps[:]), reads=pb, writes=mfB)
                        else:
                            P.op("dve", lambda e, ps=ps, mg=mg: e.tensor_tensor(out=mf[:, mg, :], in0=ps[:], in1=mf[:, mg, :], op=ALU.add), reads=pb, writes=mfB)
            rms_rstd(P, G, mf[:], mfB, KC, sq, sqB, sd, rstd, rsB, D, EPS)
            for c in range(KC):
                j = c % 2
                P.op("dve", lambda e, c=c, j=j: e.scalar_tensor_tensor(out=tmp[j][:], in0=mf[:, c, :], scalar=modv[:, s, 5, c:c + 1], in1=rstd[:], op0=ALU.mult, op1=ALU.mult),
                     reads=[mfB, rsB, modB], writes=tB[j])
                P.op("dve", lambda e, c=c, j=j: e.tensor_tensor(out=xt[:, c, :], in0=xt[:, c, :], in1=tmp[j][:], op=ALU.add), reads=[tB[j]], writes=xB)
            P.dma("pool", xr[:, :, tsl], xt[:], reads=xB)
        P.barrier()


def attention(P, cfg, l, s, G):
    ES = contextlib.ExitStack
    nc = P.nc
    T = cfg["seqs"][s]
    rows = T // GW
    nj = T // 128
    cB = G["cB"]
    identb = G["identb"]
    SCALE = 128 ** -0.5
    with ES() as es:
        master = sb(P, es, "master", [128, 8, 15 * 64], F32)
        mB = Buf()
        P.op("pool", lambda e: e.memset(master[:], NEG), writes=mB)
        m4 = master[:].rearrange("p h (j c) -> p h j c", c=64)
        with nc.allow_non_contiguous_dma(reason="rpb window"):
            for q in range(64):
                c0 = min(max(q - 8, 0), 48)
                a0 = c0 - q + 15
                for half in range(2):
                    p = half * 64 + q
                    P.dma("sp", m4[p:p + 1, :, :, c0:c0 + 16], G["rpb"][l:l + 1, :, :, a0:a0 + 16], writes=mB)
        Kh = sb(P, es, "Kh", [128, T], BF16)
        Qh = sb(P, es, "Qh", [128, T], BF16)
        V0 = sb(P, es, "V0", [128, nj, 128], BF16)
        V1 = sb(P, es, "V1", [128, nj, 128], BF16)
        atth = sb(P, es, "atth", [128, T], BF16)
        hdB = Buf()
        atB = Buf()
        S = [sb(P, es, "S", [128, 512], F32) for _ in range(8)]
        SB_ = [Buf() for _ in range(8)]
        Pm = [sb(P, es, "Pm", [128, 512], BF16) for _ in range(8)]
        PB_ = [Buf() for _ in range(8)]
        PT = [sb(P, es, "PT", [128, 4, 128], BF16) for _ in range(8)]
        PTB = [Buf() for _ in range(8)]
        st = [sb(P, es, "st", [128, 4], F32) for _ in range(8)]
        stB = [Buf() for _ in range(8)]
        for hd in range(8):
            P.dma("sp", Kh[:], G["kfm"][s][hd * 128:(hd + 1) * 128, :], writes=hdB)
            P.dma("sp", Qh[:], G["qfm"][s][hd * 128:(hd + 1) * 128, :], writes=hdB)
            P.dma("sp", V0[:], G["vtm"][s][0:T, hd * 128:(hd + 1) * 128].rearrange("(j p) d -> p j d", p=128), writes=hdB)
            P.dma("sp", V1[:, 0:nj - 1, :], G["vtm"][s][64:T - 64, hd * 128:(hd + 1) * 128].rearrange("(j p) d -> p j d", p=128), writes=hdB)
            for pb4 in range(0, rows // 2, 8):
                B4 = []
                for pi in range(pb4, min(pb4 + 8, rows // 2)):
                    k = pi % 8
                    rr_ = [2 * pi, 2 * pi + 1]
                    rs = [min(max(r - 4, 0), rows - 8) for r in rr_]
                    oo = [rs[i] - rr_[i] for i in range(2)]
                    B4.append((pi, k, rr_, rs, oo))
                PS1 = {}
                for (pi, k, rr_, rs, oo) in B4:
                    ps, pb = P.psum()
                    PS1[pi] = (ps, pb)

                    def f(e, ps=ps, rr_=rr_, rs=rs):
                        for i in range(2):
                            ins = e.matmul(ps[i * 64:(i + 1) * 64, :], lhsT=Qh[:, rr_[i] * 64:(rr_[i] + 1) * 64], rhs=Kh[:, rs[i] * 64:rs[i] * 64 + 512], start=True, stop=True)
                        return ins
                    P.op("pe", f, reads=hdB, writes=pb)
                for (pi, k, rr_, rs, oo) in B4:
                    ps, pb = PS1[pi]
                    if oo[0] == oo[1]:
                        P.op("dve", lambda e, o=oo[0]: e.scalar_tensor_tensor(out=S[k][:], in0=ps[:], scalar=SCALE, in1=master[:, hd, (o + 7) * 64:(o + 15) * 64], op0=ALU.mult, op1=ALU.add),
                             reads=[pb, mB], writes=SB_[k])
                    else:
                        for i in range(2):
                            P.op("dve", lambda e, o=oo[i], i=i: e.scalar_tensor_tensor(out=S[k][i * 64:(i + 1) * 64, :], in0=ps[i * 64:(i + 1) * 64, :], scalar=SCALE,
                                                                               in1=master[i * 64:(i + 1) * 64, hd, (o + 7) * 64:(o + 15) * 64], op0=ALU.mult, op1=ALU.add),
                                 reads=[pb, mB], writes=SB_[k])
                for (pi, k, rr_, rs, oo) in B4:
                    P.op("dve", lambda e: e.reduce_max(out=st[k][:, 0:1], in_=S[k][:], axis=AX.X), reads=SB_[k], writes=stB[k])
                    P.op("dve", lambda e: e.tensor_scalar(out=st[k][:, 1:2], in0=st[k][:, 0:1], scalar1=-1.0, scalar2=None, op0=ALU.mult), reads=stB[k], writes=stB[k])
                for (pi, k, rr_, rs, oo) in B4:
                    P.op("act", lambda e: e.activation(out=Pm[k][:], in_=S[k][:], func=AF.Exp, bias=st[k][:, 1:2], scale=1.0, accum_out=st[k][:, 2:3]), reads=[SB_[k], stB[k]], writes=[PB_[k], stB[k]])
                for (pi, k, rr_, rs, oo) in B4:
                    P.op("dve", lambda e: e.reciprocal(out=st[k][:, 3:4], in_=st[k][:, 2:3]), reads=stB[k], writes=stB[k])
                    P.op("dve", lambda e: e.tensor_scalar(out=Pm[k][:], in0=Pm[k][:], scalar1=st[k][:, 3:4], scalar2=None, op0=ALU.mult), reads=stB[k], writes=PB_[k])
                PS2 = {}
                for (pi, k, rr_, rs, oo) in B4:
                    ps2, pb2 = P.psum()
                    pst = ps2[:].bitcast(BF16)
                    PS2[pi] = (pst, pb2)

                    def f2(e, pst=pst, k=k):
                        for kc in range(4):
                            ins = e.transpose(out=pst[:, kc * 128:(kc + 1) * 128], in_=Pm[k][:, kc * 128:(kc + 1) * 128], identity=identb[:])
                        return ins
                    P.op("pe", f2, reads=[PB_[k], cB], writes=pb2)
                for (pi, k, rr_, rs, oo) in B4:
                    pst, pb2 = PS2[pi]
                    P.op("act", lambda e: e.copy(out=PT[k][:].rearrange("p a b -> p (a b)"), in_=pst[:, 0:512]), reads=pb2, writes=PTB[k])
                PS3 = {}
                for (pi, k, rr_, rs, oo) in B4:
                    ps3, pb3 = P.psum()
                    PS3[pi] = (ps3, pb3)

                    def f3(e, ps3=ps3, k=k, rs=rs):
                        for i in range(2):
                            Vx = V0 if rs[i] % 2 == 0 else V1
                            j0 = rs[i] // 2 if rs[i] % 2 == 0 else (rs[i] - 1) // 2
                            for kc in range(4):
                                ins = e.matmul(ps3[:, i * 64:(i + 1) * 64], lhsT=Vx[:, j0 + kc, :], rhs=PT[k][:, kc, i * 64:(i + 1) * 64], start=(kc == 0), stop=(kc == 3))
                        return ins
                    P.op("pe", f3, reads=[PTB[k], hdB], writes=pb3)
                for (pi, k, rr_, rs, oo) in B4:
                    ps3, pb3 = PS3[pi]
                    P.op("dve", lambda e: e.tensor_copy(out=atth[:, pi * 128:(pi + 1) * 128], in_=ps3[:, 0:128]), reads=pb3, writes=atB)
            P.dma("pool", G["mixin"][s][hd * 128:(hd + 1) * 128, :], atth[:], reads=atB)
        P.barrier()


def scan_pass(P, cfg, l, s, d, G):
    ES = contextlib.ExitStack
    nc = P.nc
    T = cfg["seqs"][s]
    ntile = T // NT
    pv, pvB, cB = G["pv"], G["pvB"], G["cB"]
    identb, blkb, maskb, masknb, keep = G["identb"], G["blkb"], G["maskb"], G["masknb"], G["keep"]
    rk3 = G["rkv"][s]
    with ES() as es:
        st32 = sb(P, es, "st32", [65, 1024], F32)
        w2b = sb(P, es, "w2b", [65, 1024], BF16)
        a2b = sb(P, es, "a2b", [65, 1024], BF16)
        v2b = sb(P, es, "v2b", [33, 1024], BF16)
        uB = Buf()
        sB_ = Buf()
        for src, dst, n in ((G["w2aug"][l, d], w2b, 65), (G["a2aug"][l, d], a2b, 65), (G["v2aug"][l], v2b, 33)):
            P.dma("sp", st32[0:n, :], src, writes=sB_)
            P.op("dve", lambda e, dst=dst, n=n: e.tensor_copy(out=dst[0:n, :], in_=st32[0:n, :]), reads=sB_, writes=[uB, sB_])
        coef0 = sb(P, es, "coef0", [128, 3, 8], F32)
        for i in range(3):
            a = 168 + (i * 2) * 8
            P.op("dve", lambda e, i=i, a=a: e.tensor_tensor(out=coef0[:, i, :], in0=pv[:, a:a + 8], in1=pv[:, a + 8:a + 16], op=ALU.add), reads=pvB, writes=uB)
            P.op("dve", lambda e, i=i: e.tensor_scalar(out=coef0[:, i, :], in0=coef0[:, i, :], scalar1=-1.0, scalar2=1.0, op0=ALU.mult, op1=ALU.add), reads=uB, writes=uB)
        lwa = [sb(P, es, "lwa", [65, NT], BF16) for _ in range(2)]
        laa = [sb(P, es, "laa", [65, NT], BF16) for _ in range(2)]
        lva = [sb(P, es, "lva", [33, NT], BF16) for _ in range(2)]
        ldB = [Buf(), Buf()]
        for k in range(2):
            for t_ in (lwa[k], laa[k], lva[k]):
                P.op("dve", lambda e, t_=t_: e.memset(t_[:], 1.0), writes=ldB[k])
        raw = sb(P, es, "raw", [128, 3, NT + 2], F32)
        rawB = Buf()
        f32n = ["rr", "kv", "vv", "sg", "vf", "lw", "Lf", "Linc", "Lexc", "Wt", "Wex", "Winv", "asg", "kx", "t0", "kkn", "kd", "bb"]
        F = {n: sb(P, es, n, [128, NT], F32) for n in f32n}
        FB = {n: Buf() for n in f32n}
        sqk = sb(P, es, "sqk", [128, NT], BF16)
        sqkB = Buf()
        AR = [sb(P, es, "AR", [128, 8, 256], BF16) for _ in range(4)]
        Bbd = [sb(P, es, "Bbd", [128, 8, 128], BF16) for _ in range(4)]
        Kbd = [sb(P, es, "Kbd", [128, 8, 128], BF16) for _ in range(4)]
        Vbd = [sb(P, es, "Vbd", [128, 8, 128], BF16) for _ in range(4)]
        bon = [sb(P, es, "bon", [128, NT], F32) for _ in range(4)]
        WC = [sb(P, es, "WC", [128, 8], F32) for _ in range(4)]
        yt = [sb(P, es, "yt", [128, NT], F32) for _ in range(4)]
        opB = [Buf() for _ in range(4)]
        ytB = [Buf() for _ in range(4)]
        TT = [sb(P, es, "TT", [128, 512], BF16) for _ in range(4)]
        AM = [sb(P, es, "AM", [128, 512], BF16) for _ in range(4)]
        Nn = [sb(P, es, "Nn", [128, 128], BF16) for _ in range(4)]
        X = [[sb(P, es, "X", [128, 256], BF16) for _ in range(2)] for _ in range(4)]
        QQ = [[sb(P, es, "QQ", [128, 256], BF16) for _ in range(2)] for _ in range(4)]
        GY = [sb(P, es, "GY", [128, 256], BF16) for _ in range(4)]
        Xf = [[sb(P, es, "Xf", [128, 256], F32) for _ in range(2)] for _ in range(4)]
        QQf = [[sb(P, es, "QQf", [128, 256], F32) for _ in range(2)] for _ in range(4)]
        Pf = [sb(P, es, "Pf", [128, 256], F32) for _ in range(4)]
        maskPf, maskNf = G["maskPf"], G["maskNf"]
        uT = [{n: Buf() for n in ("TT", "AM", "Nn", "X0", "X1", "Q0", "Q1", "GY", "Pf", "G")} for _ in range(4)]
        Mst = [[sb(P, es, "M", [128, 128], BF16) for _ in range(2)] for _ in range(8)]
        MB = [[Buf(), Buf()] for _ in range(8)]
        mcur = [0] * 8
        for hq in range(4):
            for t_ in (AR[hq], Bbd[hq], Kbd[hq], Vbd[hq]):
                P.op("pool", lambda e, t_=t_: e.memset(t_[:], 0.0), writes=opB[hq])
        for hp in range(8):
            P.op("pool", lambda e, hp=hp: e.memset(Mst[hp][0][:], 0.0), writes=MB[hp][0])

        def v3(ap):
            return ap.rearrange("p (c t) -> p c t", t=64)

        tiles = list(range(ntile)) if d == 0 else list(range(ntile - 1, -1, -1))
        chunks = list(range(8)) if d == 0 else list(range(7, -1, -1))
        for tn, ti in enumerate(tiles):
            tsl = slice(ti * NT, (ti + 1) * NT)
            lk = tn % 2
            P.dma("sp", lwa[lk][0:64, :], G["lw_d"][s][d * 64:(d + 1) * 64, tsl], writes=ldB[lk])
            P.dma("sp", laa[lk][0:64, :], G["la_d"][s][d * 64:(d + 1) * 64, tsl], writes=ldB[lk])
            P.dma("sp", lva[lk][0:32, :], G["lv_d"][s][:, tsl], writes=ldB[lk])
            for hpg in range(2):
                for hq in range(4):
                    hp = hpg * 4 + hq
                    hs = slice(hp * 128, (hp + 1) * 128)
                    P.dma("sp", raw[:], rk3.rearrange("(i c p) t -> p i c t", i=3, p=128)[:, :, hp, ti * NT:ti * NT + NT + 2], writes=rawB)
                    for i, nm in enumerate(("rr", "kv", "vv")):
                        o = F[nm]
                        a = 168 + (i * 2) * 8 + hp
                        P.op("dve", lambda e, i=i, o=o: e.tensor_scalar(out=o[:], in0=raw[:, i, 1:NT + 1], scalar1=coef0[:, i, hp:hp + 1], scalar2=None, op0=ALU.mult), reads=[rawB, uB], writes=FB[nm])
                        P.op("dve", lambda e, i=i, o=o, a=a: e.scalar_tensor_tensor(out=o[:], in0=raw[:, i, 0:NT], scalar=pv[:, a:a + 1], in1=o[:], op0=ALU.mult, op1=ALU.add), reads=[rawB, pvB], writes=FB[nm])
                        P.op("dve", lambda e, i=i, o=o, a=a: e.scalar_tensor_tensor(out=o[:], in0=raw[:, i, 2:NT + 2], scalar=pv[:, a + 8:a + 9], in1=o[:], op0=ALU.mult, op1=ALU.add), reads=[rawB, pvB], writes=FB[nm])
                    if l > 0:
                        ps, pb = P.psum()
                        P.op("pe", lambda e, ps=ps: e.matmul(ps[:], lhsT=v2b[:, hs], rhs=lva[lk][:], start=True, stop=True), reads=[uB, ldB[lk]], writes=pb)
                        P.op("act", lambda e, ps=ps: e.activation(out=F["sg"][:], in_=ps[:], func=AF.Sigmoid), reads=pb, writes=FB["sg"])
                        P.dma("sp", F["vf"][:], G["vfirst"][s][hs, tsl], writes=FB["vf"])
                        P.op("dve", lambda e: e.tensor_tensor(out=F["vf"][:], in0=F["vf"][:], in1=F["vv"][:], op=ALU.subtract), reads=FB["vv"], writes=FB["vf"])
                        P.op("dve", lambda e: e.tensor_tensor(out=F["vf"][:], in0=F["vf"][:], in1=F["sg"][:], op=ALU.mult), reads=FB["sg"], writes=FB["vf"])
                        P.op("dve", lambda e: e.tensor_tensor(out=F["vv"][:], in0=F["vv"][:], in1=F["vf"][:], op=ALU.add), reads=FB["vf"], writes=FB["vv"])
                    elif d == 0:
                        P.dma("pool", G["vfirst"][s][hs, tsl], F["vv"][:], reads=FB["vv"])
                    ps, pb = P.psum()
                    P.op("pe", lambda e, ps=ps: e.matmul(ps[:], lhsT=w2b[:, hs], rhs=lwa[lk][:], start=True, stop=True), reads=[uB, ldB[lk]], writes=pb)
                    P.op("act", lambda e, ps=ps: e.activation(out=F["lw"][:], in_=ps[:], func=AF.Sigmoid), reads=pb, writes=FB["lw"])
                    P.op("dve", lambda e: e.tensor_scalar(out=F["lw"][:], in0=F["lw"][:], scalar1=-0.6065306597126334, scalar2=None, op0=ALU.mult), writes=FB["lw"])
                    P.op("dve", lambda e: e.tensor_tensor_scan(out=F["Lf"][:], data0=keep[:], data1=F["lw"][:], initial=0.0, op0=ALU.mult, op1=ALU.add), reads=[FB["lw"], cB], writes=FB["Lf"])
                    tot = v3(F["Lf"][:])[:, :, 63]
                    if d == 0:
                        Linc = F["Lf"]
                        LiB = FB["Lf"]
                        P.op("dve", lambda e: e.tensor_tensor(out=F["Lexc"][:], in0=F["Lf"][:], in1=F["lw"][:], op=ALU.subtract), reads=[FB["Lf"], FB["lw"]], writes=FB["Lexc"])
                    else:
                        Linc = F["Linc"]
                        LiB = FB["Linc"]
                        P.op("dve", lambda e: e.tensor_tensor(out=v3(F["Lexc"][:]), in0=tot.unsqueeze(2).to_broadcast([128, 8, 64]), in1=v3(F["Lf"][:]), op=ALU.subtract), reads=[FB["Lf"]], writes=FB["Lexc"])
                        P.op("dve", lambda e: e.tensor_tensor(out=F["Linc"][:], in0=F["Lexc"][:], in1=F["lw"][:], op=ALU.add), reads=[FB["Lexc"], FB["lw"]], writes=FB["Linc"])
                    P.op("act", lambda e: e.activation(out=F["Wt"][:], in_=Linc[:], func=AF.Exp), reads=LiB, writes=FB["Wt"])
                    P.op("act", lambda e: e.activation(out=F["Wex"][:], in_=F["Lexc"][:], func=AF.Exp), reads=FB["Lexc"], writes=FB["Wex"])
                    P.op("act", lambda e: e.activation(out=F["Winv"][:], in_=Linc[:], func=AF.Exp, scale=-1.0), reads=LiB, writes=FB["Winv"])
                    P.op("act", lambda e: e.activation(out=WC[hq][:], in_=tot, func=AF.Exp), reads=FB["Lf"], writes=opB[hq])
                    ps, pb = P.psum()
                    P.op("pe", lambda e, ps=ps: e.matmul(ps[:], lhsT=a2b[:, hs], rhs=laa[lk][:], start=True, stop=True), reads=[uB, ldB[lk]], writes=pb)
                    P.op("act", lambda e, ps=ps: e.activation(out=F["asg"][:], in_=ps[:], func=AF.Sigmoid), reads=pb, writes=FB["asg"])
                    P.op("dve", lambda e: e.tensor_scalar(out=F["kx"][:], in0=F["kv"][:], scalar1=pv[:, 216 + hp:217 + hp], scalar2=None, op0=ALU.mult), reads=[FB["kv"], pvB], writes=FB["kx"])
                    P.op("dve", lambda e: e.tensor_tensor(out=sqk[:], in0=F["kx"][:], in1=F["kx"][:], op=ALU.mult), reads=FB["kx"], writes=sqkB)
                    ps, pb = P.psum()
                    P.op("pe", lambda e, ps=ps: e.matmul(ps[:], lhsT=blkb[:], rhs=sqk[:], start=True, stop=True), reads=[sqkB, cB], writes=pb)
                    P.op("dve", lambda e, ps=ps: e.tensor_scalar(out=F["t0"][:], in0=ps[:], scalar1=1e-24, scalar2=None, op0=ALU.max), reads=pb, writes=FB["t0"])
                    P.op("act", lambda e: e.activation(out=F["t0"][:], in_=F["t0"][:], func=AF.Sqrt), reads=FB["t0"], writes=FB["t0"])
                    P.op("dve", lambda e: e.reciprocal(out=F["t0"][:], in_=F["t0"][:]), reads=FB["t0"], writes=FB["t0"])
                    P.op("dve", lambda e: e.tensor_tensor(out=F["kkn"][:], in0=F["kx"][:], in1=F["t0"][:], op=ALU.mult), reads=[FB["kx"], FB["t0"]], writes=FB["kkn"])
                    P.op("dve", lambda e: e.tensor_scalar(out=F["kd"][:], in0=F["asg"][:], scalar1=-1.0, scalar2=pv[:, 224 + hp:225 + hp], op0=ALU.add, op1=ALU.mult), reads=[FB["asg"], pvB], writes=FB["kd"])
                    P.op("dve", lambda e: e.scalar_tensor_tensor(out=F["kd"][:], in0=F["kd"][:], scalar=1.0, in1=F["kv"][:], op0=ALU.add, op1=ALU.mult), reads=FB["kv"], writes=FB["kd"])
                    P.op("dve", lambda e: e.tensor_tensor(out=F["bb"][:], in0=F["kkn"][:], in1=F["asg"][:], op=ALU.mult), reads=[FB["kkn"], FB["asg"]], writes=FB["bb"])
                    P.op("dve", lambda e: e.tensor_tensor(out=F["t0"][:], in0=F["rr"][:], in1=F["kd"][:], op=ALU.mult), reads=[FB["rr"], FB["kd"], FB["kkn"]], writes=FB["t0"])
                    P.op("dve", lambda e: e.tensor_scalar(out=sqk[:], in0=F["t0"][:], scalar1=pv[:, 232 + hp:233 + hp], scalar2=None, op0=ALU.mult), reads=[FB["t0"], pvB], writes=sqkB)
                    ps, pb = P.psum()
                    P.op("pe", lambda e, ps=ps: e.matmul(ps[:], lhsT=blkb[:], rhs=sqk[:], start=True, stop=True), reads=[sqkB, cB], writes=pb)
                    P.op("dve", lambda e, ps=ps: e.tensor_tensor(out=bon[hq][:], in0=ps[:], in1=F["vv"][:], op=ALU.mult), reads=[pb, FB["vv"]], writes=opB[hq])
                    for hh in range(2):
                        hsl = slice(hh * 64, (hh + 1) * 64)
                        csl = slice(hh * 64, (hh + 1) * 64)
                        P.op("dve", lambda e, hsl=hsl, csl=csl: e.scalar_tensor_tensor(out=AR[hq][hsl, :, csl], in0=v3(F["kkn"][hsl, :]), scalar=-1.0, in1=v3(F["Wex"][hsl, :]), op0=ALU.mult, op1=ALU.mult),
                             reads=[FB["kkn"], FB["Wex"]], writes=opB[hq])
                        P.op("dve", lambda e, hsl=hsl, hh=hh: e.tensor_tensor(out=AR[hq][hsl, :, 128 + hh * 64:128 + (hh + 1) * 64], in0=v3(F["rr"][hsl, :]), in1=v3(F["Wt"][hsl, :]), op=ALU.mult),
                             reads=[FB["rr"], FB["Wt"]], writes=opB[hq])
                        P.op("dve", lambda e, hsl=hsl, csl=csl: e.tensor_tensor(out=Bbd[hq][hsl, :, csl], in0=v3(F["bb"][hsl, :]), in1=v3(F["Winv"][hsl, :]), op=ALU.mult),
                             reads=[FB["bb"], FB["Winv"]], writes=opB[hq])
                        P.op("dve", lambda e, hsl=hsl, csl=csl: e.tensor_tensor(out=Kbd[hq][hsl, :, csl], in0=v3(F["kd"][hsl, :]), in1=v3(F["Winv"][hsl, :]), op=ALU.mult),
                             reads=[FB["kd"], FB["Winv"]], writes=opB[hq])
                        P.op("act", lambda e, hsl=hsl, csl=csl: e.copy(out=Vbd[hq][hsl, :, csl], in_=v3(F["vv"][hsl, :])), reads=[FB["vv"]], writes=opB[hq])
                for ci in ([] if cfg.get("no_units") else chunks):
                    cs_ = slice(ci * 64, (ci + 1) * 64)
                    for hq in range(4):
                        u = uT[hq]
                        ps, pb = P.psum()
                        pst = ps[:].bitcast(BF16)

                        def f(e, pst=pst, hq=hq):
                            e.transpose(out=pst[:, 0:128], in_=Bbd[hq][:, ci, :], identity=identb[:])
                            e.transpose(out=pst[:, 128:256], in_=Kbd[hq][:, ci, :], identity=identb[:])
                            e.transpose(out=pst[:, 256:384], in_=Vbd[hq][:, ci, :], identity=identb[:])
                            return e.transpose(out=pst[:, 384:512], in_=AR[hq][:, ci, 0:128], identity=identb[:])
                        P.op("pe", f, reads=[opB[hq], cB], writes=pb)
                        P.op("act", lambda e, pst=pst, hq=hq: e.copy(out=TT[hq][:, 0:384], in_=pst[:, 0:384]), reads=pb, writes=u["TT"])
                        P.op("act", lambda e, pst=pst, hq=hq: e.copy(out=Xf[hq][0][:, 0:128], in_=pst[:, 384:512]), reads=pb, writes=u["X0"])
                        if cfg.get("ustage", 99) < 0.5:
                            continue
                        ps, pb = P.psum()

                        def f(e, ps=ps, hq=hq):
                            e.matmul(ps[:, 0:256], lhsT=Bbd[hq][:, ci, :], rhs=AR[hq][:, ci, :], start=True, stop=True)
                            return e.matmul(ps[:, 256:512], lhsT=Kbd[hq][:, ci, :], rhs=AR[hq][:, ci, :], start=True, stop=True)
                        P.op("pe", f, reads=[opB[hq]], writes=pb)
                        P.op("dve", lambda e, ps=ps, hq=hq: e.tensor_tensor(out=AM[hq][:], in0=ps[:], in1=maskb[:, d, :], op=ALU.mult), reads=[pb, cB], writes=u["AM"])
                        P.op("dve", lambda e, ps=ps, hq=hq: e.tensor_tensor(out=Pf[hq][:, 0:128], in0=ps[:, 0:128], in1=maskPf[:, d, :], op=ALU.mult), reads=[pb, cB], writes=u["Pf"])
                        if cfg.get("ustage", 99) < 0.8:
                            continue
                        ps, pb = P.psum()
                        P.op("pe", lambda e, ps=ps, hq=hq: e.matmul(ps[:, 0:128], lhsT=AR[hq][:, ci, 0:128], rhs=Bbd[hq][:, ci, :], start=True, stop=True), reads=[opB[hq]], writes=pb)
                        P.op("dve", lambda e, ps=ps, hq=hq: e.tensor_tensor(out=Pf[hq][:, 128:256], in0=ps[:, 0:128], in1=maskNf[:, d, :], op=ALU.mult), reads=[pb, cB], writes=u["Pf"])
                    for hq in (range(4) if cfg.get("ustage", 99) >= 2 else []):
                        u = uT[hq]
                        ps, pb = P.psum()
                        P.op("pe", lambda e, ps=ps, hq=hq: e.matmul(ps[:, 0:128], lhsT=AM[hq][:, 256:384], rhs=TT[hq][:, 256:384], start=True, stop=True), reads=[u["AM"], u["TT"]], writes=pb)
                        P.op("act", lambda e, ps=ps, hq=hq: e.copy(out=Xf[hq][0][:, 128:256], in_=ps[:, 0:128]), reads=pb, writes=u["X0"])
                    Qs = [(Pf[hq][:, 0:128], Pf[hq][:, 128:256], [uT[hq]["Pf"]]) for hq in range(4)]
                    for k in (range(6) if cfg.get("ustage", 99) >= 3 else []):
                        cur = k % 2
                        for hq in range(4):
                            u = uT[hq]
                            Q, Qn, qB = Qs[hq]
                            ps, pb = P.psum()
                            P.op("pe", lambda e, ps=ps, hq=hq, Q=Q: e.matmul(ps[:, 0:256], lhsT=Q, rhs=Xf[hq][cur][:], start=True, stop=True), reads=[qB, u["X%d" % cur]], writes=pb)
                            P.op("dve", lambda e, ps=ps, hq=hq: e.tensor_tensor(out=Xf[hq][1 - cur][:], in0=ps[:, 0:256], in1=Xf[hq][cur][:].bitcast(F32), op=ALU.add), reads=[pb, u["X%d" % cur]], writes=u["X%d" % (1 - cur)])
                            if k < 5:
                                ps, pb = P.psum()

                                def f(e, ps=ps, Q=Q, Qn=Qn):
                                    e.matmul(ps[:, 0:128], lhsT=Qn, rhs=Q, start=True, stop=True)
                                    return e.matmul(ps[:, 128:256], lhsT=Q, rhs=Qn, start=True, stop=True)
                                P.op("pe", f, reads=[qB], writes=pb)
                                qn = "Q%d" % (k % 2)
                                P.op("act", lambda e, ps=ps, hq=hq: e.copy(out=QQf[hq][k % 2][:], in_=ps[:, 0:256]), reads=pb, writes=u[qn])
                                Qs[hq] = (QQf[hq][k % 2][:, 0:128], QQf[hq][k % 2][:, 128:256], [u[qn]])
                    for hq in range(4):
                        u = uT[hq]
                        P.op("act", lambda e, hq=hq: e.copy(out=X[hq][0][:], in_=Xf[hq][0][:].bitcast(F32)), reads=u["X0"], writes=u["G"])
                    for hq in (range(4) if cfg.get("ustage", 99) >= 4 else []):
                        u = uT[hq]
                        Gx = X[hq][0]
                        ps, pb = P.psum()

                        def f(e, ps=ps, hq=hq, Gx=Gx):
                            e.matmul(ps[:, 0:128], lhsT=Gx[:, 0:128], rhs=TT[hq][:, 0:128], start=True, stop=True)
                            return e.matmul(ps[:, 128:256], lhsT=Gx[:, 0:128], rhs=AM[hq][:, 128:256], start=True, stop=True)
                        P.op("pe", f, reads=[u["G"], u["TT"], u["AM"]], writes=pb)
                        P.op("dve", lambda e, ps=ps, hq=hq: e.tensor_tensor(out=GY[hq][:, 0:128], in0=ps[:, 0:128], in1=identb[:], op=ALU.add), reads=[pb, cB], writes=u["GY"])
                        P.op("dve", lambda e, ps=ps, hq=hq: e.tensor_tensor(out=GY[hq][:, 128:256], in0=ps[:, 128:256], in1=AR[hq][:, ci, 128:256], op=ALU.add), reads=[pb, opB[hq]], writes=u["GY"])
                    for hq in (range(4) if cfg.get("ustage", 99) >= 5 else []):
                        u = uT[hq]
                        hp = hpg * 4 + hq
                        Gx = X[hq][0]
                        mc = mcur[hp]
                        Mo = Mst[hp][mc]
                        ps, pb = P.psum()

                        def f(e, ps=ps, hq=hq, Gx=Gx, Mo=Mo):
                            e.matmul(ps[:, 0:128], lhsT=TT[hq][:, 0:128], rhs=Gx[:, 128:256], start=True, stop=False)
                            e.matmul(ps[:, 0:128], lhsT=TT[hq][:, 128:256], rhs=TT[hq][:, 256:384], start=False, stop=False)
                            e.matmul(ps[:, 0:128], lhsT=GY[hq][:, 0:128], rhs=Mo[:], start=False, stop=True)
                            e.matmul(ps[:, 128:256], lhsT=Gx[:, 128:256], rhs=AM[hq][:, 128:256], start=True, stop=False)
                            e.matmul(ps[:, 128:256], lhsT=TT[hq][:, 256:384], rhs=AM[hq][:, 384:512], start=False, stop=False)
                            return e.matmul(ps[:, 128:256], lhsT=Mo[:], rhs=GY[hq][:, 128:256], start=False, stop=True)
                        P.op("pe", f, reads=[u["G"], u["TT"], u["AM"], u["GY"], MB[hp][mc]], writes=pb)
                        P.op("act", lambda e, ps=ps, hq=hq, hp=hp, mc=mc: e.activation(out=Mst[hp][1 - mc][:], in_=ps[:, 0:128], func=AF.Identity, scale=WC[hq][:, ci:ci + 1]), reads=[pb, opB[hq]], writes=MB[hp][1 - mc])
                        mcur[hp] = 1 - mc
                        for hh in range(2):
                            hsl = slice(hh * 64, (hh + 1) * 64)
                            P.op("dve", lambda e, ps=ps, hq=hq, hsl=hsl, hh=hh: e.tensor_tensor(out=yt[hq][hsl, cs_], in0=ps[hsl, 128 + hh * 64:128 + (hh + 1) * 64], in1=bon[hq][hsl, cs_], op=ALU.add),
                                 reads=[pb, opB[hq]], writes=ytB[hq])
                for hq in range(4):
                    hp = hpg * 4 + hq
                    P.dma("pool", G["ysc"][s][d, hp * 128:(hp + 1) * 128, tsl], yt[hq][:], reads=ytB[hq])
        P.barrier()


def scan_post(P, cfg, l, s, G):
    ES = contextlib.ExitStack
    T = cfg["seqs"][s]
    ntile = T // NT
    pv, pvB, cB = G["pv"], G["pvB"], G["cB"]
    blkf = G["blkf"]
    with ES() as es:
        st32 = sb(P, es, "st32", [64, 1024], F32)
        g2b = sb(P, es, "g2b", [64, 1024], BF16)
        uB = Buf()
        P.dma("sp", st32[:], G["g2in"][l], writes=uB)
        P.op("dve", lambda e: e.tensor_copy(out=g2b[:], in_=st32[:]), reads=uB, writes=uB)
        lgt = [sb(P, es, "lgt", [64, NT], BF16) for _ in range(2)]
        lgB = [Buf(), Buf()]
        y0 = [sb(P, es, "y0", [128, NT], F32) for _ in range(8)]
        y1 = [sb(P, es, "y1", [128, NT], F32) for _ in range(8)]
        yB = [Buf() for _ in range(8)]
        ymL = [sb(P, es, "ym", [128, NT], F32) for _ in range(8)]
        sqL = [sb(P, es, "sq", [128, NT], F32) for _ in range(8)]
        sdL = [sb(P, es, "sd", [128, NT], F32) for _ in range(8)]
        tBL = [Buf() for _ in range(8)]
        ob = [sb(P, es, "ob", [128, NT], BF16) for _ in range(8)]
        obB = [Buf() for _ in range(8)]
        for ti in range(ntile):
            tsl = slice(ti * NT, (ti + 1) * NT)
            lk = ti % 2
            P.dma("sp", lgt[lk][:], G["lg_d"][s][:, tsl], writes=lgB[lk])
            for hb in (0,):
                HB = [(hp, hp % 8, slice(hp * 128, (hp + 1) * 128)) for hp in range(hb, hb + 8)]
                for (hp, k, hs) in HB:
                    P.dma("sp", y0[k][:], G["ysc"][s][0, hs, tsl], writes=yB[k])
                    P.dma("sp", y1[k][:], G["ysc"][s][1, hs, tsl], writes=yB[k])
                for (hp, k, hs) in HB:
                    P.op("dve", lambda e: e.tensor_tensor(out=y0[k][:], in0=y0[k][:], in1=y1[k][:], op=ALU.add), reads=yB[k], writes=yB[k])
                PS = {}
                for (hp, k, hs) in HB:
                    ps, pb = P.psum()
                    PS[hp] = (ps, pb)
                    P.op("pe", lambda e: e.matmul(ps[:], lhsT=blkf[:], rhs=y0[k][:], start=True, stop=True), reads=[yB[k], cB], writes=pb)
                for (hp, k, hs) in HB:
                    ps, pb = PS[hp]
                    P.op("dve", lambda e: e.scalar_tensor_tensor(out=ymL[k][:], in0=ps[:], scalar=-1.0 / 64, in1=y0[k][:], op0=ALU.mult, op1=ALU.add), reads=[pb, yB[k]], writes=tBL[k])
                    P.op("dve", lambda e: e.tensor_tensor(out=sqL[k][:], in0=ymL[k][:], in1=ymL[k][:], op=ALU.mult), reads=tBL[k], writes=tBL[k])
                for (hp, k, hs) in HB:
                    ps, pb = P.psum()
                    PS[hp] = (ps, pb)
                    P.op("pe", lambda e: e.matmul(ps[:], lhsT=blkf[:], rhs=sqL[k][:], start=True, stop=True), reads=[tBL[k], cB], writes=pb)
                for (hp, k, hs) in HB:
                    ps, pb = PS[hp]
                    P.op("act", lambda e: e.activation(out=sdL[k][:], in_=ps[:], func=AF.Sqrt, scale=1.0 / 64, bias=float(GN_EPS)), reads=pb, writes=tBL[k])
                for (hp, k, hs) in HB:
                    P.op("dve", lambda e: e.reciprocal(out=sdL[k][:], in_=sdL[k][:]), reads=tBL[k], writes=tBL[k])
                    P.op("dve", lambda e: e.tensor_tensor(out=ymL[k][:], in0=ymL[k][:], in1=sdL[k][:], op=ALU.mult), reads=tBL[k], writes=tBL[k])
                    P.op("dve", lambda e: e.tensor_scalar(out=ymL[k][:], in0=ymL[k][:], scalar1=pv[:, 240 + hp:241 + hp], scalar2=pv[:, 248 + hp:249 + hp], op0=ALU.mult, op1=ALU.add), reads=[tBL[k], pvB], writes=tBL[k])
                for (hp, k, hs) in HB:
                    ps, pb = P.psum()
                    PS[hp] = (ps, pb)
                    P.op("pe", lambda e: e.matmul(ps[:], lhsT=g2b[:, hs], rhs=lgt[lk][:], start=True, stop=True), reads=[uB, lgB[lk]], writes=pb)
                for (hp, k, hs) in HB:
                    ps, pb = PS[hp]
                    P.op("dve", lambda e: e.tensor_tensor(out=ob[k][:], in0=ps[:], in1=ymL[k][:], op=ALU.mult), reads=[pb, tBL[k]], writes=obB[k])
                    P.dma("pool", G["mixin"][s][1024 + hp * 128:1024 + (hp + 1) * 128, tsl], ob[k][:], reads=obB[k])
        P.barrier()


def fm(v):
    v = np.asarray(v, np.float32).reshape(-1, 128)
    return np.ascontiguousarray(v.T)


def tile_w(w, mw=512):
    K, N = w.shape
    return np.ascontiguousarray(w.reshape(K // 128, 128, N // mw, mw).transpose(2, 1, 0, 3))


def prep_shared(inp, L):
    out = {}
    out["win"] = np.stack([tile_w(inp["w_in"][l]) for l in range(L)])
    out["wout"] = np.stack([tile_w(inp["w_out"][l]) for l in range(L)])
    out["wf1"] = np.stack([tile_w(inp["w_ffn1"][l]) for l in range(L)])
    w2l = []
    for l in range(L):
        w = inp["w_ffn2"][l]
        t = w.reshape(2, 32, 128, 16, 128).transpose(0, 3, 2, 1, 4)
        w2l.append(np.ascontiguousarray(t))
    out["wf2"] = np.stack(w2l)
    out["wada"] = np.stack([tile_w(inp["w_ada"][l]) for l in range(L)])
    pvs = []
    for l in range(L):
        cols = [fm(inp["g_pre_mix"][l]), fm(inp["g_post_mix"][l]), fm(inp["g_pre_ffn"][l]), fm(inp["g_post_ffn"][l]),
                fm(inp["b_ada"][l]), fm(inp["g_att_out"][l])]
        for i in range(3):
            for j in range(2):
                cols.append(fm(inp["mu_rkv"][l, i, j]))
        cols += [fm(inp["k_k"][l]), fm(inp["k_a"][l]), fm(inp["r_k"][l].reshape(-1)), fm(inp["ln_x_w"][l]), fm(inp["ln_x_b"][l])]
        pvs.append(np.concatenate(cols, axis=1))
    out["pvec"] = np.stack(pvs).astype(np.float32)
    assert out["pvec"].shape[2] == NPV
    ld, mp, mn = [], [], []
    for l in range(L):
        z32 = np.zeros((D, 32), np.float32)
        v1 = inp["v1"][l - 1] if l > 0 else z32
        w = np.concatenate([inp["w1"][l, 0], inp["w1"][l, 1], inp["a1"][l, 0], inp["a1"][l, 1], inp["g1"][l], v1], axis=1)
        muv = inp["mu_v"][l - 1] if l > 0 else np.zeros((2, D), np.float32)

        def mub(j):
            return np.concatenate([np.repeat(inp["mu_x"][l, 0, j][:, None], 128, 1), np.repeat(inp["mu_x"][l, 1, j][:, None], 128, 1),
                                   np.repeat(inp["mu_x"][l, 2, j][:, None], 64, 1), np.repeat(muv[j][:, None], 32, 1)], axis=1)
        for arr, lst in ((w, ld), (mub(0), mp), (mub(1), mn)):
            lst.append(np.ascontiguousarray(arr.reshape(KC, 128, NLORA).transpose(1, 0, 2)))
    out["lora_dn"] = np.stack(ld).astype(np.float32)
    out["lora_mp"] = np.stack(mp).astype(np.float32)
    out["lora_mn"] = np.stack(mn).astype(np.float32)
    out["w2aug"] = np.stack([np.stack([np.concatenate([inp["w2"][l, d], inp["w0"][l, d][None]], 0) for d in range(2)]) for l in range(L)]).astype(np.float32)
    out["a2aug"] = np.stack([np.stack([np.concatenate([inp["a2"][l, d], inp["a0"][l, d][None]], 0) for d in range(2)]) for l in range(L)]).astype(np.float32)
    out["g2"] = np.ascontiguousarray(inp["g2"][:L]).astype(np.float32)
    v2l = []
    for l in range(L):
        if l == 0:
            v2l.append(np.zeros((33, 1024), np.float32))
        else:
            v2l.append(np.concatenate([inp["v2"][l - 1], inp["v0"][l - 1][None]], 0))
    out["v2aug"] = np.stack(v2l).astype(np.float32)
    out["rpb"] = np.ascontiguousarray(inp["rpb"][:L]).astype(np.float32)
    out["cident"] = np.eye(128, dtype=np.float32)
    blk = np.zeros((128, 128), np.float32)
    blk[:64, :64] = 1
    blk[64:, 64:] = 1
    out["cblk"] = blk
    t = np.arange(64)
    cm = np.zeros((2, 128, 512), np.float32)
    cn = np.zeros((2, 128, 128), np.float32)
    for d in range(2):
        st = (t[:, None] < t[None, :]) if d == 0 else (t[:, None] > t[None, :])
        inc = st | (t[:, None] == t[None, :])
        bst = np.zeros((128, 128), np.float32)
        binc = np.zeros((128, 128), np.float32)
        for h in range(2):
            bst[h * 64:(h + 1) * 64, h * 64:(h + 1) * 64] = st
            binc[h * 64:(h + 1) * 64, h * 64:(h + 1) * 64] = inc
        cm[d] = np.concatenate([bst, binc, bst, binc], axis=1)
        cn[d] = bst.T
    out["cmask"] = cm
    out["cmaskn"] = cn
    kp = np.ones((128, 512), np.float32)
    kp[:, ::64] = 0
    out["ckeep"] = kp
    return out


FULL_CFG = {"depth": 4, "seqs": [4096, 2048]}


def kernel(**inputs):
    inp = {k: np.asarray(v) for k, v in inputs.items()}
    cfg = FULL_CFG
    L = cfg["depth"]
    shared = prep_shared(inp, L)
    nc = build(cfg)
    in_maps = []
    for c in range(8):
        m = dict(shared)
        m["x0"] = np.ascontiguousarray(inp["x_prompt"][c])
        m["x1"] = np.ascontiguousarray(inp["x_sample"][c % 4])
        cc = np.stack([inp["c_prompt"][c], inp["c_sample"][c % 4]], axis=1)
        m["cvec"] = np.ascontiguousarray(cc.reshape(KC, 128, 2).transpose(1, 0, 2)).astype(np.float32)
        in_maps.append(m)
    res = run_bass_kernel_spmd(nc, in_maps, core_ids=list(range(8)))
    yp = np.stack([res.results[c]["y0"] for c in range(8)])
    ys = np.stack([res.results[c]["y1"] for c in range(4)])
    return (yp.astype(np.float32), ys.astype(np.float32))
```
